# Optimizing a Trainium2 kernel written in Bass

```python
import math
import jax, jax.numpy as jnp
from jax import lax
import numpy as np

D_MODEL = 2048
BATCH = 4
SEQ = 4096
DEPTH = 4

GRID_W = 64
CTX_LEN = 256
N_MIXERS = 3
N_A = (DEPTH + 2) // 3
N_B = (DEPTH + 1) // 3
N_C = DEPTH // 3

A_HEADS = 16
A_KV_HEADS = 4
A_GROUP = A_HEADS // A_KV_HEADS
A_HEAD_DIM = D_MODEL // A_HEADS

LRU_WIDTH = D_MODEL
LRU_BLOCKS = 8
LRU_BLOCK = LRU_WIDTH // LRU_BLOCKS
CONV_W = 4
LRU_C = 8.0

MLA_HEADS = 16
MLA_Q_LORA = 512
MLA_KV_LORA = 512
MLA_QK_NOPE = 128
MLA_QK_ROPE = 64
MLA_V_DIM = 128

D_FF = 4 * D_MODEL

Q_BLOCK = 128
ROPE_THETA = 10000.0
EPS = 1e-6
DN_ALPHA = (2 * DEPTH) ** 0.25
DN_BETA = (8 * DEPTH) ** -0.25

kernel_name = 'hybrid_gqa_rglru_mla_diffusion_trunk'


def layer_norm(x, g, b):
    xf = x.astype(jnp.float32)
    mu = jnp.mean(xf, -1, keepdims=True)
    var = jnp.mean(jnp.square(xf - mu), -1, keepdims=True)
    return ((xf - mu) * lax.rsqrt(var + EPS) * g + b).astype(x.dtype)


def rms_norm(x, g):
    xf = x.astype(jnp.float32)
    return (xf * lax.rsqrt(jnp.mean(jnp.square(xf), -1, keepdims=True) + EPS) * g).astype(x.dtype)


def axial_tables(rows, rot_dim):
    row = jnp.repeat(jnp.arange(rows), GRID_W).astype(jnp.float32)
    col = (jnp.arange(rows * GRID_W) % GRID_W).astype(jnp.float32)
    m = rot_dim // 2
    inv = ROPE_THETA ** (-(jnp.arange(m // 2, dtype=jnp.float32) * 2.0) / m)
    ang_r = row[:, None] * inv
    ang_c = col[:, None] * inv
    return (jnp.cos(ang_r), jnp.sin(ang_r), jnp.cos(ang_c), jnp.sin(ang_c))


def _rot_half(x, cos, sin):
    shape = (cos.shape[0],) + (1,) * (x.ndim - 3) + (cos.shape[1],)
    c = cos.reshape(shape)
    s = sin.reshape(shape)
    x1, x2 = jnp.split(x, 2, axis=-1)
    return jnp.concatenate([x1 * c - x2 * s, x2 * c + x1 * s], axis=-1)


def rope_2d(x, tabs):
    cr, sr, cc, sc = tabs
    xr, xc = jnp.split(x.astype(jnp.float32), 2, axis=-1)
    return jnp.concatenate([_rot_half(xr, cr, sr), _rot_half(xc, cc, sc)], axis=-1).astype(x.dtype)


def block_attention(q, k, v, scale):
    bsz, s = q.shape[:2]
    nb = s // Q_BLOCK
    qb = jnp.moveaxis(q.reshape((bsz, nb, Q_BLOCK) + q.shape[2:]), 1, 0)

    def one_block(qi):
        sc = jnp.einsum('bqkgd,btkd->bkgqt', qi, k, preferred_element_type=jnp.float32) * scale
        p = jax.nn.softmax(sc, axis=-1).astype(v.dtype)
        return jnp.einsum('bkgqt,btkd->bqkgd', p, v)

    o = lax.map(one_block, qb)
    return jnp.moveaxis(o, 0, 1).reshape((bsz, s) + o.shape[3:])


def gqa_mixer(h_lat, h_ctx, wq, wk, wv, wo, q_g, k_g, tabs, need_ctx_out):
    def q_proj(h):
        bsz, t, _ = h.shape
        return rms_norm((h @ wq).reshape(bsz, t, A_KV_HEADS, A_GROUP, A_HEAD_DIM), q_g)

    def kv_proj(h):
        bsz, t, _ = h.shape
        k = rms_norm((h @ wk).reshape(bsz, t, A_KV_HEADS, A_HEAD_DIM), k_g)
        v = (h @ wv).reshape(bsz, t, A_KV_HEADS, A_HEAD_DIM)
        return k, v

    scale = A_HEAD_DIM ** -0.5
    k_c, v_c = kv_proj(h_ctx)
    q_l = rope_2d(q_proj(h_lat), tabs)
    k_l, v_l = kv_proj(h_lat)
    k_l = rope_2d(k_l, tabs)
    o_l = block_attention(q_l, jnp.concatenate([k_c, k_l], 1), jnp.concatenate([v_c, v_l], 1), scale)
    out_l = o_l.reshape(h_lat.shape[0], h_lat.shape[1], -1) @ wo
    out_c = None
    if need_ctx_out:
        o_c = block_attention(q_proj(h_ctx), k_c, v_c, scale)
        out_c = o_c.reshape(h_ctx.shape[0], h_ctx.shape[1], -1) @ wo
    return out_l, out_c


def centered_dwconv(u, w, b):
    t = u.shape[1]
    left = CONV_W // 2
    right = CONV_W - 1 - left
    up = jnp.pad(u, ((0, 0), (left, right), (0, 0)))
    return sum(up[:, j:j + t] * w[j] for j in range(CONV_W)) + b


def linear_scan(a, b, h0):
    b = b.at[:, 0].add(a[:, 0] * h0)

    def comb(l, r):
        return (l[0] * r[0], r[0] * l[1] + r[1])

    return lax.associative_scan(comb, (a, b), axis=1)[1]


def rglru_mixer(h_lat, h_ctx, wx, wy, conv_w, conv_b, ra_w, ra_b, ix_w, ix_b, lam, wo, need_ctx_out):
    dt = h_lat.dtype

    def coeffs(u, d):
        shp = u.shape
        ub = u.reshape(shp[0], shp[1], LRU_BLOCKS, LRU_BLOCK)
        r = jax.nn.sigmoid(jnp.einsum('btnk,nkj->btnj', ub, ra_w[d]).reshape(shp) + ra_b[d])
        gi = jax.nn.sigmoid(jnp.einsum('btnk,nkj->btnj', ub, ix_w[d]).reshape(shp) + ix_b[d])
        log_a = LRU_C * r.astype(jnp.float32) * jax.nn.log_sigmoid(lam[d].astype(jnp.float32))
        a = jnp.exp(log_a)
        bx = jnp.sqrt(-jnp.expm1(2.0 * log_a)) * (gi * u).astype(jnp.float32)
        return a, bx

    def scan_fwd(u, h0):
        a, bx = coeffs(u, 0)
        return linear_scan(a, bx, h0)

    def scan_bwd(u, h0):
        a, bx = coeffs(u, 1)
        return jnp.flip(linear_scan(jnp.flip(a, 1), jnp.flip(bx, 1), h0), 1)

    u_c = centered_dwconv(h_ctx @ wx, conv_w, conv_b)
    u_l = centered_dwconv(h_lat @ wx, conv_w, conv_b)
    zeros = jnp.zeros((h_ctx.shape[0], LRU_WIDTH), jnp.float32)
    hc_f = scan_fwd(u_c, zeros)
    hc_b = scan_bwd(u_c, zeros)
    hl_f = scan_fwd(u_l, hc_f[:, -1])
    hl_b = scan_bwd(u_l, hc_b[:, 0])
    out_l = ((hl_f + hl_b).astype(dt) * jax.nn.gelu(h_lat @ wy)) @ wo
    out_c = None
    if need_ctx_out:
        out_c = ((hc_f + hc_b).astype(dt) * jax.nn.gelu(h_ctx @ wy)) @ wo
    return out_l, out_c


def mla_mixer(h_lat, h_ctx, wq_a, q_a_g, wq_b, wkv_a, kv_a_g, wkv_b, wo, tabs, need_ctx_out):
    def q_proj(h, rotary):
        bsz, t, _ = h.shape
        q = (rms_norm(h @ wq_a, q_a_g) @ wq_b).reshape(bsz, t, MLA_HEADS, MLA_QK_NOPE + MLA_QK_ROPE)
        q_nope, q_pe = jnp.split(q, [MLA_QK_NOPE], axis=-1)
        if rotary:
            q_pe = rope_2d(q_pe, tabs)
        return jnp.concatenate([q_nope, q_pe], -1)[:, :, :, None, :]

    def kv_proj(h, rotary):
        bsz, t, _ = h.shape
        ckv, k_pe = jnp.split(h @ wkv_a, [MLA_KV_LORA], axis=-1)
        kv = (rms_norm(ckv, kv_a_g) @ wkv_b).reshape(bsz, t, MLA_HEADS, MLA_QK_NOPE + MLA_V_DIM)
        k_nope, v = jnp.split(kv, [MLA_QK_NOPE], axis=-1)
        k_pe = k_pe[:, :, None, :]
        if rotary:
            k_pe = rope_2d(k_pe, tabs)
        k = jnp.concatenate([k_nope, jnp.broadcast_to(k_pe, (bsz, t, MLA_HEADS, MLA_QK_ROPE))], -1)
        return k, v

    scale = (MLA_QK_NOPE + MLA_QK_ROPE) ** -0.5
    k_c, v_c = kv_proj(h_ctx, False)
    k_l, v_l = kv_proj(h_lat, True)
    o_l = block_attention(q_proj(h_lat, True), jnp.concatenate([k_c, k_l], 1),
                          jnp.concatenate([v_c, v_l], 1), scale)
    out_l = o_l.reshape(h_lat.shape[0], h_lat.shape[1], -1) @ wo
    out_c = None
    if need_ctx_out:
        o_c = block_attention(q_proj(h_ctx, False), k_c, v_c, scale)
        out_c = o_c.reshape(h_ctx.shape[0], h_ctx.shape[1], -1) @ wo
    return out_l, out_c


def sq_relu_mlp(h, w1, w2):
    return jnp.square(jax.nn.relu(h @ w1)) @ w2


def setup_inputs(seed: int = 0) -> dict:
    key = jax.random.key(seed)
    ks = iter(jax.random.split(key, 48))
    f32 = jnp.float32
    d = D_MODEL

    def nrm(shape, scale):
        return jax.random.normal(next(ks), shape, f32) * scale

    x = nrm((BATCH, SEQ, d), 1.0)
    c = nrm((BATCH, d), 1.0)
    ctx = nrm((BATCH, CTX_LEN, d), 1.0)
    c_ctx = nrm((d,), 1.0)
    ada_w = nrm((DEPTH, d, 6 * d), 0.5 * d ** -0.5)
    ada_b = nrm((DEPTH, 6 * d), 0.01)
    ln_g = 1.0 + nrm((DEPTH, 2, d), 0.01)
    ln_b = nrm((DEPTH, 2, d), 0.01)
    mlp_w1 = nrm((DEPTH, d, D_FF), d ** -0.5)
    mlp_w2 = nrm((DEPTH, D_FF, d), D_FF ** -0.5 * DN_BETA)
    gqa_wq = nrm((N_A, d, A_HEADS * A_HEAD_DIM), d ** -0.5)
    gqa_wk = nrm((N_A, d, A_KV_HEADS * A_HEAD_DIM), d ** -0.5)
    gqa_wv = nrm((N_A, d, A_KV_HEADS * A_HEAD_DIM), d ** -0.5)
    gqa_wo = nrm((N_A, A_HEADS * A_HEAD_DIM, d), (A_HEADS * A_HEAD_DIM) ** -0.5 * DN_BETA)
    gqa_q_g = 1.0 + nrm((N_A, A_HEAD_DIM), 0.01)
    gqa_k_g = 1.0 + nrm((N_A, A_HEAD_DIM), 0.01)
    lru_wx = nrm((N_B, d, LRU_WIDTH), d ** -0.5)
    lru_wy = nrm((N_B, d, LRU_WIDTH), d ** -0.5)
    lru_conv_w = nrm((N_B, CONV_W, LRU_WIDTH), CONV_W ** -0.5)
    lru_conv_b = nrm((N_B, LRU_WIDTH), 0.01)
    lru_ra_w = nrm((N_B, 2, LRU_BLOCKS, LRU_BLOCK, LRU_BLOCK), LRU_BLOCK ** -0.5)
    lru_ra_b = nrm((N_B, 2, LRU_WIDTH), 0.01)
    lru_ix_w = nrm((N_B, 2, LRU_BLOCKS, LRU_BLOCK, LRU_BLOCK), LRU_BLOCK ** -0.5)
    lru_ix_b = nrm((N_B, 2, LRU_WIDTH), 0.01)
    a0 = jax.random.uniform(next(ks), (N_B, 2, LRU_WIDTH), f32, 0.9, 0.999)
    s = a0 ** (1.0 / LRU_C)
    lru_lam = jnp.log(s) - jnp.log1p(-s)
    lru_wo = nrm((N_B, LRU_WIDTH, d), LRU_WIDTH ** -0.5 * DN_BETA)
    mla_wq_a = nrm((N_C, d, MLA_Q_LORA), d ** -0.5)
    mla_q_a_g = 1.0 + nrm((N_C, MLA_Q_LORA), 0.01)
    mla_wq_b = nrm((N_C, MLA_Q_LORA, MLA_HEADS * (MLA_QK_NOPE + MLA_QK_ROPE)), MLA_Q_LORA ** -0.5)
    mla_wkv_a = nrm((N_C, d, MLA_KV_LORA + MLA_QK_ROPE), d ** -0.5)
    mla_kv_a_g = 1.0 + nrm((N_C, MLA_KV_LORA), 0.01)
    mla_wkv_b = nrm((N_C, MLA_KV_LORA, MLA_HEADS * (MLA_QK_NOPE + MLA_V_DIM)), MLA_KV_LORA ** -0.5)
    mla_wo = nrm((N_C, MLA_HEADS * MLA_V_DIM, d), (MLA_HEADS * MLA_V_DIM) ** -0.5 * DN_BETA)
    return {'x': x, 'c': c, 'ctx': ctx, 'c_ctx': c_ctx, 'ada_w': ada_w, 'ada_b': ada_b,
            'ln_g': ln_g, 'ln_b': ln_b, 'mlp_w1': mlp_w1, 'mlp_w2': mlp_w2,
            'gqa_wq': gqa_wq, 'gqa_wk': gqa_wk, 'gqa_wv': gqa_wv, 'gqa_wo': gqa_wo,
            'gqa_q_g': gqa_q_g, 'gqa_k_g': gqa_k_g,
            'lru_wx': lru_wx, 'lru_wy': lru_wy, 'lru_conv_w': lru_conv_w, 'lru_conv_b': lru_conv_b,
            'lru_ra_w': lru_ra_w, 'lru_ra_b': lru_ra_b, 'lru_ix_w': lru_ix_w, 'lru_ix_b': lru_ix_b,
            'lru_lam': lru_lam, 'lru_wo': lru_wo,
            'mla_wq_a': mla_wq_a, 'mla_q_a_g': mla_q_a_g, 'mla_wq_b': mla_wq_b, 'mla_wkv_a': mla_wkv_a,
            'mla_kv_a_g': mla_kv_a_g, 'mla_wkv_b': mla_wkv_b, 'mla_wo': mla_wo}


def reference(x, c, ctx, c_ctx, ada_w, ada_b, ln_g, ln_b, mlp_w1, mlp_w2,
              gqa_wq, gqa_wk, gqa_wv, gqa_wo, gqa_q_g, gqa_k_g,
              lru_wx, lru_wy, lru_conv_w, lru_conv_b, lru_ra_w, lru_ra_b, lru_ix_w, lru_ix_b,
              lru_lam, lru_wo,
              mla_wq_a, mla_q_a_g, mla_wq_b, mla_wkv_a, mla_kv_a_g, mla_wkv_b, mla_wo):
    n_lat = x.shape[1]
    rows = n_lat // GRID_W
    tabs_gqa = axial_tables(rows, A_HEAD_DIM)
    tabs_mla = axial_tables(rows, MLA_QK_ROPE)
    sc_c = jax.nn.silu(c)
    sc_ctx = jax.nn.silu(c_ctx)
    hctx = ctx
    for i in range(DEPTH):
        kind = i % N_MIXERS
        slot = i // N_MIXERS
        last = i == DEPTH - 1
        m_l = jnp.split((sc_c @ ada_w[i] + ada_b[i])[:, None, :], 6, axis=-1)
        m_c = jnp.split(sc_ctx @ ada_w[i] + ada_b[i], 6, axis=-1)
        h_l = x * (1.0 + m_l[1]) + m_l[0]
        h_c = hctx * (1.0 + m_c[1]) + m_c[0]
        if kind == 0:
            o_l, o_c = gqa_mixer(h_l, h_c, gqa_wq[slot], gqa_wk[slot], gqa_wv[slot], gqa_wo[slot],
                                 gqa_q_g[slot], gqa_k_g[slot], tabs_gqa, not last)
        elif kind == 1:
            o_l, o_c = rglru_mixer(h_l, h_c, lru_wx[slot], lru_wy[slot], lru_conv_w[slot], lru_conv_b[slot],
                                   lru_ra_w[slot], lru_ra_b[slot], lru_ix_w[slot], lru_ix_b[slot],
                                   lru_lam[slot], lru_wo[slot], not last)
        else:
            o_l, o_c = mla_mixer(h_l, h_c, mla_wq_a[slot], mla_q_a_g[slot], mla_wq_b[slot], mla_wkv_a[slot],
                                 mla_kv_a_g[slot], mla_wkv_b[slot], mla_wo[slot], tabs_mla, not last)
        x = layer_norm(DN_ALPHA * x + (1.0 + m_l[2]) * o_l, ln_g[i, 0], ln_b[i, 0])
        f_l = sq_relu_mlp(x * (1.0 + m_l[4]) + m_l[3], mlp_w1[i], mlp_w2[i])
        x = layer_norm(DN_ALPHA * x + (1.0 + m_l[5]) * f_l, ln_g[i, 1], ln_b[i, 1])
        if not last:
            hctx = layer_norm(DN_ALPHA * hctx + (1.0 + m_c[2]) * o_c, ln_g[i, 0], ln_b[i, 0])
            f_c = sq_relu_mlp(hctx * (1.0 + m_c[4]) + m_c[3], mlp_w1[i], mlp_w2[i])
            hctx = layer_norm(DN_ALPHA * hctx + (1.0 + m_c[5]) * f_c, ln_g[i, 1], ln_b[i, 1])
    return x
```

```python
import contextlib
import math
import numpy as np
import ml_dtypes
import concourse.bass as bass
import concourse.mybir as mybir
from concourse.bass_utils import run_bass_kernel_spmd

F32 = mybir.dt.float32
BF16 = mybir.dt.bfloat16
AF = mybir.ActivationFunctionType
ALU = mybir.AluOpType
AX = mybir.AxisListType

D = 2048
DFF = 8192
NTF = 34
NTL = 17
CTXF = (0, 17)
TIME_ORDER = [0, 17] + list(range(1, 17)) + list(range(18, 34))
EPS = 1e-6
DEPTH = 4
ALPHA = (2 * DEPTH) ** 0.25
NCORES = 8


class Sched:
    ENGS = ('pe', 'act', 'dve', 'pool', 'sp')
    NDMA = 48

    def __init__(self, nc):
        self.nc = nc
        self.es = contextlib.ExitStack()
        self.eng = {'pe': nc.tensor, 'act': nc.scalar, 'dve': nc.vector, 'pool': nc.gpsimd, 'sp': nc.sync}
        self.sem = {e: self.es.enter_context(nc.semaphore("s_" + e)) for e in self.ENGS if e != 'sp'}
        self.dsem = [self.es.enter_context(nc.semaphore("d%d" % i)) for i in range(self.NDMA)]
        self.seq = {e: 0 for e in self.ENGS}
        self.waited = {e: {} for e in self.ENGS}
        self.buf = {}
        self.dma_i = 0
        self.nwait = 0
        self.nops = 0
        self.last_dma = {}
        self.stack = [self.es]
        self.uid = 0
        self.ns = 0
        self.NCC = 4
        for i in range(self.NCC):
            self.dsem.append(self.es.enter_context(nc.semaphore("cc%d" % i)))
        self.cc_i = 0
        self.pool_dma_out = {}
        self.cc_out = {}
        self.limit = 1 << 60
        self.marks = []
        self.any_dma = {}

    def sb(self, name, shape, dt):
        self.uid += 1
        return self.stack[-1].enter_context(self.nc.sbuf_tensor("%s_%d" % (name, self.uid), shape, dt))

    def ps(self, name, shape, dt=F32):
        self.uid += 1
        return self.stack[-1].enter_context(self.nc.psum_tensor("%s_%d" % (name, self.uid), shape, dt))

    @contextlib.contextmanager
    def phase(self):
        es = contextlib.ExitStack()
        self.marks.append((self.ns, self.nops))
        self.stack.append(es)
        try:
            yield
        finally:
            self.barrier()
            self.stack.pop()
            es.close()

    def _key(self, b):
        if isinstance(b, tuple):
            return b if b[0] == 'abs' else (self.ns, b)
        if isinstance(b, str):
            return (self.ns, b)
        return b.name

    def _deps(self, eng, reads, writes, is_dma):
        need = {}

        def add(ref, kind):
            if ref[0] == 'e':
                _, E, s = ref
                if E == eng and not is_dma and (eng == 'pe' or kind != 'raw'):
                    return
                k = ('e', E)
                v = s
            else:
                k = ('d', ref[1])
                v = ref[2]
            if need.get(k, 0) < v:
                need[k] = v

        for r in reads:
            st = self.buf.get(self._key(r))
            if st and st[0] is not None:
                add(st[0], 'raw')
        for w in writes:
            st = self.buf.get(self._key(w))
            if st:
                if st[0] is not None:
                    add(st[0], 'waw')
                for rr in st[1]:
                    add(rr, 'war')
        return need

    def _emit_waits(self, eng, need):
        e = self.eng[eng]
        wd = self.waited[eng]
        for k, v in need.items():
            if wd.get(k, 0) >= v:
                continue
            sem = self.sem[k[1]] if k[0] == 'e' else self.dsem[k[1]]
            e.wait_ge(sem, v)
            wd[k] = v
            self.nwait += 1

    def _update(self, ref, reads, writes):
        for r in reads:
            st = self.buf.setdefault(self._key(r), [None, []])
            st[1].append(ref)
            if len(st[1]) > 64:
                best = {}
                for x in st[1]:
                    k = (x[0], x[1])
                    if k not in best or best[k][2] < x[2]:
                        best[k] = x
                st[1] = list(best.values())
        for w in writes:
            st = self.buf.setdefault(self._key(w), [None, []])
            st[0] = ref
            st[1] = []

    def op(self, eng, fn, reads=(), writes=()):
        if self.nops >= self.limit:
            return None
        need = self._deps(eng, reads, writes, False)
        self._emit_waits(eng, need)
        ins = fn(self.eng[eng])
        self.seq[eng] += 1
        ins.then_inc(self.sem[eng], 1)
        self._update(('e', eng, self.seq[eng]), reads, writes)
        self.nops += 1
        return ins

    def dma(self, q, out, in_, reads=(), writes=(), bg=False, **kw):
        if self.nops >= self.limit:
            return None
        need = self._deps(q, reads, writes, True)
        i = self.dma_i
        self.dma_i += 1
        idx = i % self.NDMA
        val = 16 * (i // self.NDMA + 1)
        if val > 16:
            k = ('d', idx)
            need[k] = max(need.get(k, 0), val - 16)
        if q == 'pool':
            for k, v in self.cc_out.items():
                need[k] = max(need.get(k, 0), v)
            self.pool_dma_out[('d', idx)] = val
        self._emit_waits(q, need)
        ins = self.eng[q].dma_start(out=out, in_=in_, **kw)
        ins.then_inc(self.dsem[idx], 16)
        ref = ('d', idx, val)
        self._update(ref, reads, writes)
        self.any_dma[idx] = val
        if bg:
            self.last_dma.pop(idx, None)
        else:
            self.last_dma[idx] = val
        self.nops += 1
        return ins

    def collective(self, kind, ins, outs, groups, reads=(), writes=()):
        need = self._deps('pool', reads, writes, True)
        i = self.cc_i
        self.cc_i += 1
        idx = self.NDMA + i % self.NCC
        val = i // self.NCC + 1
        if val > 1:
            k = ('d', idx)
            need[k] = max(need.get(k, 0), val - 1)
        for k, v in self.pool_dma_out.items():
            need[k] = max(need.get(k, 0), v)
        self.cc_out[('d', idx)] = val
        self._emit_waits('pool', need)
        ins_ = self.nc.gpsimd.collective_compute(kind, ALU.bypass, replica_groups=groups, ins=ins, outs=outs)
        ins_.then_inc(self.dsem[idx])
        ref = ('d', idx, val)
        self._update(ref, reads, writes)
        self.last_dma[idx] = val
        self.nops += 1
        return ins_

    def barrier(self):
        need = {}
        for e in self.ENGS:
            if e != 'sp' and self.seq[e] > 0:
                need[('e', e)] = self.seq[e]
        for idx, val in self.last_dma.items():
            need[('d', idx)] = val
        for e in self.ENGS:
            self._emit_waits(e, dict(need))

    def finish(self):
        self.last_dma.update(self.any_dma)
        self.barrier()
        self.es.close()


class Ring:
    def __init__(self, S, name, n, shape, dt, psum=False):
        self.bufs = [(S.ps if psum else S.sb)("%s%d" % (name, i), shape, dt) for i in range(n)]
        self.i = 0

    def next(self):
        b = self.bufs[self.i % len(self.bufs)]
        self.i += 1
        return b


class LayerProg:
    def __init__(self, S, nc, li, kind, last, io, consts):
        self.S, self.nc, self.li, self.kind, self.last = S, nc, li, kind, last
        self.io = io
        self.c = consts
        self.ltiles = list(range(1, NTL)) if last else list(range(NTL))
        self.nscr = 0
        self.wkeys = {}

    def dram(self, name, shape, dt):
        self.nscr += 1
        return self.nc.dram_tensor("L%d_%s_%d" % (self.li, name, self.nscr), shape, dt).ap()

    def cast2d(self, name, src, piece_cols=2048):
        shape = list(src.shape)
        dst = self.dram(name, shape, BF16)
        nd = len(shape)
        names = " ".join("d%d" % i for i in range(nd))
        pat = "%s -> (%s)" % (names, names)
        fs = src.rearrange(pat).rearrange("(r c) -> r c", c=16384)
        fd = dst.rearrange(pat).rearrange("(r c) -> r c", c=16384)
        R = fs.shape[0]
        keys = []
        for r0 in range(0, R, 128):
            r1 = min(R, r0 + 128)
            k = (name, 'w', r0)
            self.S.dma('pool', fd[r0:r1, :], fs[r0:r1, :], writes=[k], bg=True)
            keys.append(k)
        self.wkeys[name] = keys
        return dst

    def load_w(self, wt, wb, key):
        kc = wt.shape[1]
        src = wb.rearrange("(kc p) n -> p kc n", p=128)
        step = max(1, kc // 4)
        for k0 in range(0, kc, step):
            self.S.dma('sp', wt[:, k0:k0 + step, :], src[:, k0:k0 + step, :], reads=self.wkeys[key], writes=[(wt.name, k0 // step)])

    @staticmethod
    def wk(wt, kc):
        step = max(1, wt.shape[1] // 4)
        return (wt.name, kc // step)

    def phase_mod(self):
        S, io = self.S, self.io
        mod = S.sb("mod", [128, 96, 2], F32)
        self.mod = mod
        with S.phase():
            cT = S.sb("cT", [128, 16, 2], F32)
            scT = S.sb("scT", [128, 16, 2], F32)
            adab = S.sb("adab", [128, 96], F32)
            S.dma('sp', cT[:], io['cT'].rearrange("p (k s) -> p k s", s=2), writes=[cT])
            S.dma('sp', adab[:], io['adabT'], writes=[adab])
            S.op('act', lambda e: e.activation(scT[:], cT[:], AF.Silu), reads=[cT], writes=[scT])
            psM = S.ps("psM", [128, 96, 2], F32)
            ring = Ring(S, "wa", 2, [128, 16, 512], F32)
            src = io['ada_w'].rearrange("(kc p) n -> p kc n", p=128)
            for ng in range(24):
                wa = ring.next()
                for k0 in range(0, 16, 4):
                    S.dma('sp', wa[:, k0:k0 + 4, :], src[:, k0:k0 + 4, ng * 512:(ng + 1) * 512], writes=[(wa.name, k0 // 4)])
                for nb in range(4):
                    q = ng * 4 + nb
                    for kc in range(16):
                        S.op('pe', lambda e, q=q, kc=kc, nb=nb, wa=wa: e.matmul(
                            psM[:, q, :], wa[:, kc, nb * 128:(nb + 1) * 128], scT[:, kc, :],
                            start=(kc == 0), stop=(kc == 15)), reads=[(wa.name, kc // 4), scT], writes=[psM])
            S.op('dve', lambda e: e.tensor_tensor(mod[:], psM[:], adab[:, :, None].to_broadcast([128, 96, 2]), ALU.add),
                 reads=[psM, adab], writes=[mod])
            for j in (1, 2, 4, 5):
                S.op('dve', lambda e, j=j: e.tensor_scalar_add(mod[:, j * 16:(j + 1) * 16, :], mod[:, j * 16:(j + 1) * 16, :], 1.0),
                     reads=[mod], writes=[mod])

    def mcol(self, j, c, s):
        return self.mod[:, j * 16 + c, s:s + 1]

    def bcast_rows(self, dst, j, s, psB_ring, diag_ring):
        S = self.S
        for g in range(4):
            psB = psB_ring.next()
            for cc in range(4):
                c = g * 4 + cc
                dg = diag_ring.next()
                S.op('dve', lambda e, dg=dg, c=c: e.tensor_scalar_mul(dg[:], self.c['ident_f'][:], self.mcol(j, c, s)),
                     reads=[self.c['ident_f'], self.mod], writes=[dg])
                S.op('pe', lambda e, dg=dg, cc=cc, psB=psB: e.matmul(psB[:, cc * 128:(cc + 1) * 128], self.c['ones_f'][:], dg[:],
                                                                      start=True, stop=True),
                     reads=[dg, self.c['ones_f']], writes=[psB])
            S.op('act', lambda e, g=g, psB=psB: e.activation(dst[:, g * 512:(g + 1) * 512], psB[:], AF.Copy),
                 reads=[psB], writes=[dst])

    def make_hT_tools(self, nx=2):
        S = self.S
        return dict(xt=Ring(S, "hx", nx, [128, D], F32), ps=Ring(S, "hps", 2, [128, 512], F32, psum=True))

    def load_hT(self, tools, src_tile, s, jshift, jscale, dst, col0=None, src_key=None):
        S = self.S
        xt = tools['xt'].next()
        S.dma('sp', xt[:], src_tile, reads=[src_key] if src_key else [], writes=[xt])
        for g in range(4):
            ps = tools['ps'].next()
            for cc in range(4):
                c = g * 4 + cc
                S.op('pe', lambda e, c=c, cc=cc, ps=ps, xt=xt: e.transpose(ps[:, cc * 128:(cc + 1) * 128], xt[:, c * 128:(c + 1) * 128],
                                                                        self.c['ident_f'][:]),
                     reads=[xt, self.c['ident_f']], writes=[ps])
            for cc in range(4):
                c = g * 4 + cc
                o = dst[:, c, :] if col0 is None else dst[:, c, col0:col0 + 128]
                S.op('act', lambda e, o=o, cc=cc, ps=ps, c=c: e.activation(o, ps[:, cc * 128:(cc + 1) * 128], AF.Identity,
                                                                         bias=self.mcol(jshift, c, s), scale=self.mcol(jscale, c, s)),
                     reads=[ps, self.mod], writes=[dst])

    def rope(self, xin, out_bf, tb, H, rd, t1, t2):
        S = self.S
        hw = rd // 4
        cosb = tb[:, 0:rd][:, None, :].to_broadcast([128, H, rd])
        S.op('dve', lambda e: e.tensor_tensor(t1, xin, cosb, ALU.mult), reads=[xin, tb], writes=[t1])
        xv = xin.rearrange("p h (a f w) -> p h a f w", a=2, f=2)
        tv = t2.rearrange("p h (a f w) -> p h a f w", a=2, f=2)
        sv = tb[:, rd:2 * rd].rearrange("p (a f w) -> p a f w", a=2, f=2)
        for a in range(2):
            for f in range(2):
                S.op('pool', lambda e, a=a, f=f: e.tensor_tensor(tv[:, :, a, f, :], xv[:, :, a, 1 - f, :],
                                                                 sv[:, a, f, :][:, None, :].to_broadcast([128, H, hw]), ALU.mult),
                     reads=[xin, tb], writes=[t2])
        S.op('dve', lambda e: e.tensor_tensor(out_bf, t1, t2, ALU.add), reads=[t1, t2], writes=[out_bf])

    def rstd_from_ss(self, ss, rstd, n, tmp):
        S = self.S
        S.op('act', lambda e: e.activation(tmp, ss, AF.Sqrt, bias=EPS, scale=1.0 / n), reads=[ss], writes=[tmp])
        S.op('dve', lambda e: e.reciprocal(rstd, tmp), reads=[tmp], writes=[rstd])

    def epilogue(self, yg, xt, lnG, lnB, out_ap, out_key, small):
        S = self.S
        st, mv, sd, rstd, nmr = small
        S.op('dve', lambda e: e.scalar_tensor_tensor(yg[:], xt[:], float(ALPHA), yg[:], ALU.mult, ALU.add),
             reads=[xt, yg], writes=[yg])
        for c in range(4):
            S.op('dve', lambda e, c=c: e.bn_stats(st[:, c, :], yg[:, c * 512:(c + 1) * 512]), reads=[yg], writes=[st])
        S.op('dve', lambda e: e.bn_aggr(mv[:], st[:]), reads=[st], writes=[mv])
        S.op('act', lambda e: e.activation(sd[:], mv[:, 1:2], AF.Sqrt, bias=EPS, scale=1.0), reads=[mv], writes=[sd])
        S.op('dve', lambda e: e.reciprocal(rstd[:], sd[:]), reads=[sd], writes=[rstd])
        S.op('dve', lambda e: e.scalar_tensor_tensor(nmr[:], mv[:, 0:1], -1.0, rstd[:], ALU.mult, ALU.mult),
             reads=[mv, rstd], writes=[nmr])
        S.op('act', lambda e: e.activation(yg[:], yg[:], AF.Identity, bias=nmr[:], scale=rstd[:]),
             reads=[yg, nmr, rstd], writes=[yg])
        S.op('dve', lambda e: e.tensor_tensor(yg[:], yg[:], lnG[:], ALU.mult), reads=[yg, lnG], writes=[yg])
        S.op('pool', lambda e: e.tensor_tensor(yg[:], yg[:], lnB[:], ALU.add), reads=[yg, lnB], writes=[yg])
        S.dma('sp', out_ap, yg[:], reads=[yg], writes=[out_key])

    def epi_small(self):
        S = self.S
        return (S.sb("st", [128, 4, 6], F32), S.sb("mv", [128, 2], F32), S.sb("sd", [128, 1], F32),
                S.sb("rstd", [128, 1], F32), S.sb("nmr", [128, 1], F32))

    def phase_out(self, oT_d, wo_b, wo_key, xloc, xmid):
        S = self.S
        with S.phase():
            Wo = S.sb("Wo", [128, 16, D], BF16)
            self.load_w(Wo, wo_b, wo_key)
            lnG = S.sb("lnG", [128, D], F32)
            lnB = S.sb("lnB", [128, D], F32)
            S.dma('sp', lnG[:], self.io['ln_g'][0:1, :].partition_broadcast(128), writes=[lnG])
            S.dma('sp', lnB[:], self.io['ln_b'][0:1, :].partition_broadcast(128), writes=[lnB])
            gB = [S.sb("gB0", [128, D], F32), S.sb("gB1", [128, D], F32)]
            psB_ring = Ring(S, "psB", 1, [128, 512], F32, psum=True)
            diag_ring = Ring(S, "diag", 2, [128, 128], F32)
            self.bcast_rows(gB[0], 2, 0, psB_ring, diag_ring)
            if not self.last:
                self.bcast_rows(gB[1], 2, 1, psB_ring, diag_ring)
            oring = Ring(S, "oT", 2, [128, 16, 128], BF16)
            xring = Ring(S, "xr", 2, [128, D], F32)
            yring = Ring(S, "yg", 2, [128, D], F32)
            psY = Ring(S, "psY", 1, [128, D], F32, psum=True)
            small = self.epi_small()
            for lt in self.ltiles:
                s = 1 if lt == 0 else 0
                oT = oring.next()
                S.dma('sp', oT[:], oT_d[:, :, lt * 128:(lt + 1) * 128].rearrange("h p t -> p h t"), reads=['oT_d'], writes=[oT])
                xt = xring.next()
                S.dma('sp', xt[:], xloc(lt), reads=[self.xl_key(lt)], writes=[xt])
                ps = psY.next()
                for ng in range(4):
                    for hd in range(16):
                        S.op('pe', lambda e, ng=ng, hd=hd, oT=oT, ps=ps: e.matmul(ps[:, ng * 512:(ng + 1) * 512], oT[:, hd, :],
                                                                                 Wo[:, hd, ng * 512:(ng + 1) * 512],
                                                                                 start=(hd == 0), stop=(hd == 15)),
                             reads=[oT, self.wk(Wo, hd)], writes=[ps])
                yg = yring.next()
                S.op('dve', lambda e, yg=yg, ps=ps, s=s: e.tensor_tensor(yg[:], ps[:], gB[s][:], ALU.mult),
                     reads=[ps, gB[s]], writes=[yg])
                self.epilogue(yg, xt, lnG, lnB, xmid[lt * 128:(lt + 1) * 128, :], ('xmid', lt), small)

    def phase_mlp(self, xmid, xout):
        S, io = self.S, self.io
        w1v = self.w1b.rearrange("(kc p) f -> p kc f", p=128)
        w2v = self.w2b.rearrange("(fc p) d -> p fc d", p=128)
        yd = self.dram("yd", [NTL * 128, D], F32)
        with S.phase():
            lnG = S.sb("lnG", [128, D], F32)
            lnB = S.sb("lnB", [128, D], F32)
            S.dma('sp', lnG[:], io['ln_g'][1:2, :].partition_broadcast(128), writes=[lnG])
            S.dma('sp', lnB[:], io['ln_b'][1:2, :].partition_broadcast(128), writes=[lnB])
            gB = S.sb("gB", [128, D], F32)
            psA_ring = Ring(S, "psA", 2, [128, 512], F32, psum=True)
            psO = [S.ps("psO%d" % i, [128, 512], F32) for i in range(4)]
            diag_ring = Ring(S, "diag", 2, [128, 128], F32)
            tools = self.make_hT_tools(nx=1)
            hT2 = S.sb("hT2", [128, 16, 512], BF16)
            aT = S.sb("aT", [128, 64, 512], BF16)
            w1r = Ring(S, "w1p", 2, [128, 16, 256], BF16)
            w2r = Ring(S, "w2p", 2, [128, 8, 512], BF16)
            sqr = Ring(S, "sq", 2, [128, 512], F32)
            ysr = Ring(S, "ys", 2, [128, 512], F32)
            xt_e = S.sb("xte", [128, D], F32)
            yg_e = S.sb("yge", [128, D], F32)
            small = self.epi_small()
            blocks = []
            if not self.last:
                blocks.append(([0], 1))
            for b in range(4):
                blocks.append((list(range(1 + 4 * b, 5 + 4 * b)), 0))
            cur_s = None
            for tiles, s in blocks:
                T = 128 * len(tiles)
                if cur_s != s:
                    self.bcast_rows(gB, 5, s, psA_ring, diag_ring)
                    cur_s = s
                for ti, lt in enumerate(tiles):
                    self.load_hT(tools, xmid[lt * 128:(lt + 1) * 128, :], s, 3, 4, hT2, col0=ti * 128, src_key=('xmid', lt))
                for fp in range(32):
                    w1p = w1r.next()
                    S.dma('sp', w1p[:], w1v[:, :, fp * 256:(fp + 1) * 256], reads=self.wkeys['w1b'], writes=[w1p])
                    for fcl in range(2):
                        fc = fp * 2 + fcl
                        ps = psA_ring.next()
                        for kc in range(16):
                            S.op('pe', lambda e, kc=kc, fcl=fcl, ps=ps, w1p=w1p: e.matmul(
                                ps[:, :T], w1p[:, kc, fcl * 128:(fcl + 1) * 128], hT2[:, kc, :T],
                                start=(kc == 0), stop=(kc == 15)), reads=[w1p, hT2], writes=[ps])
                        sq = sqr.next()
                        S.op('act', lambda e, sq=sq, ps=ps: e.activation(sq[:, :T], ps[:, :T], AF.Square), reads=[ps], writes=[sq])
                        S.op('dve', lambda e, sq=sq, ps=ps, fc=fc: e.scalar_tensor_tensor(aT[:, fc, :T], ps[:, :T], 0.0, sq[:, :T],
                                                                                        ALU.is_gt, ALU.mult),
                             reads=[ps, sq], writes=[aT])
                for dg in range(4):
                    for pc in range(8):
                        w2p = w2r.next()
                        S.dma('sp', w2p[:], w2v[:, pc * 8:(pc + 1) * 8, dg * 512:(dg + 1) * 512], reads=self.wkeys['w2b'], writes=[w2p])
                        for ti in range(len(tiles)):
                            for fcl in range(8):
                                fc = pc * 8 + fcl
                                S.op('pe', lambda e, ti=ti, fc=fc, fcl=fcl, w2p=w2p: e.matmul(
                                    psO[ti][:], aT[:, fc, ti * 128:(ti + 1) * 128], w2p[:, fcl, :],
                                    start=(fc == 0), stop=(fc == 63)), reads=[aT, w2p], writes=[psO[ti]])
                    for ti, lt in enumerate(tiles):
                        ys = ysr.next()
                        S.op('dve', lambda e, ys=ys, ti=ti, dg=dg: e.tensor_tensor(ys[:], psO[ti][:], gB[:, dg * 512:(dg + 1) * 512], ALU.mult),
                             reads=[psO[ti], gB], writes=[ys])
                        S.dma('sp', yd[lt * 128:(lt + 1) * 128, dg * 512:(dg + 1) * 512], ys[:], reads=[ys], writes=[('yd', lt)])
                for lt in tiles:
                    S.dma('sp', xt_e[:], xmid[lt * 128:(lt + 1) * 128, :], reads=[('xmid', lt)], writes=[xt_e])
                    S.dma('sp', yg_e[:], yd[lt * 128:(lt + 1) * 128, :], reads=[('yd', lt)], writes=[yg_e])
                    self.epilogue(yg_e, xt_e, lnG, lnB, xout(lt), self.xo_key(lt), small)
                    if self.on_tile_done is not None:
                        self.on_tile_done(lt)

    def attention(self, blocks, nheads, score_fn, v_fn, load_q_fn, oT_d, scale, pre_head=None, head_outer=False):
        S = self.S
        pT_ring = Ring(S, "pT", 4, [128, 512], BF16)
        psS = Ring(S, "psS", 2, [128, 512], F32, psum=True)
        psOr = Ring(S, "psOa", 2, [128, 512], F32, psum=True)
        psSum = Ring(S, "psSum", 2, [128, 512], F32, psum=True)
        rec_ring = Ring(S, "rec", 2, [128, 512], F32)
        o_ring = Ring(S, "oblk", 2, [128, 512], BF16)
        ones_b = self.c['ones_b']
        order = []
        if head_outer:
            for hd in range(nheads):
                for bi in range(len(blocks)):
                    order.append((hd, bi))
        else:
            for bi in range(len(blocks)):
                for hd in range(nheads):
                    order.append((hd, bi))
        prev_hd, prev_bi = None, None
        for hd, bi in order:
            tiles, ktiles = blocks[bi]
            T = 128 * len(tiles)
            if head_outer and hd != prev_hd and pre_head is not None:
                pre_head(hd)
            qb = load_q_fn(hd, bi, tiles, new_block=(bi != prev_bi), new_head=(hd != prev_hd))
            prev_hd, prev_bi = hd, bi
            pso = psOr.next()
            pss = psSum.next()
            nk = len(ktiles)
            for ki, kt in enumerate(ktiles):
                ps = psS.next()
                sreads = score_fn(ps, hd, kt, qb, T)
                pT = pT_ring.next()
                S.op('act', lambda e, pT=pT, ps=ps: e.activation(pT[:, :T], ps[:, :T], AF.Exp, scale=float(scale)),
                     reads=[ps], writes=[pT])
                vap, vkeys = v_fn(hd, kt)
                S.op('pe', lambda e, pT=pT, vap=vap, pso=pso, ki=ki: e.matmul(pso[:, :T], vap, pT[:, :T], start=(ki == 0), stop=(ki == nk - 1)),
                     reads=[pT] + vkeys, writes=[pso])
                S.op('pe', lambda e, pT=pT, pss=pss, ki=ki: e.matmul(pss[:, :T], ones_b[:], pT[:, :T], start=(ki == 0), stop=(ki == nk - 1)),
                     reads=[pT, ones_b], writes=[pss])
            rec = rec_ring.next()
            S.op('dve', lambda e, rec=rec, pss=pss: e.reciprocal(rec[:, :T], pss[:, :T]), reads=[pss], writes=[rec])
            ob = o_ring.next()
            S.op('dve', lambda e, ob=ob, pso=pso, rec=rec: e.tensor_tensor(ob[:, :T], pso[:, :T], rec[:, :T], ALU.mult),
                 reads=[pso, rec], writes=[ob])
            t0 = tiles[0] * 128
            S.dma('sp', oT_d[hd, :, t0:t0 + T], ob[:, :T], reads=[ob], writes=['oT_d'])

    def mixer_gqa(self, xfull, xloc, oT_d):
        S, io = self.S, self.io
        wq_b = self.wb['wq']
        qT_d = self.dram("qT", [16, 128, NTL * 128], BF16)
        with S.phase():
            kT_all = S.sb("kT_all", [128, 4, NTF * 128], BF16)
            V_all = S.sb("V_all", [128, NTF, 512], BF16)
            with S.phase():
                Wkv = S.sb("Wkv", [128, 16, 1024], BF16)
                for nm, c0 in (('wk', 0), ('wv', 512)):
                    srcw = self.wb[nm].rearrange("(kc p) n -> p kc n", p=128)
                    for k0 in range(0, 16, 4):
                        S.dma('sp', Wkv[:, k0:k0 + 4, c0:c0 + 512], srcw[:, k0:k0 + 4, :], reads=self.wkeys[nm], writes=[(Wkv.name, k0 // 4, c0)])
                kgB = S.sb("kgB", [128, 128], F32)
                S.dma('sp', kgB[:], io['k_g'][0:1, :].partition_broadcast(128), writes=[kgB])
                tools = self.make_hT_tools()
                hring = Ring(S, "hT", 2, [128, 16, 128], BF16)
                psKV = Ring(S, "psKV", 1, [128, 1024], F32, psum=True)
                psKT = Ring(S, "psKT", 1, [128, 4, 128], BF16, psum=True)
                tbr = Ring(S, "tb", 2, [128, 256], F32)
                sqk = S.sb("sqk", [128, 512], F32)
                ss = S.sb("ss", [128, 4], F32); tmp4 = S.sb("tmp4", [128, 4], F32); rstd = S.sb("rstd4", [128, 4], F32)
                kn = S.sb("kn", [128, 4, 128], F32); t1 = S.sb("t1", [128, 4, 128], F32); t2 = S.sb("t2", [128, 4, 128], F32)
                kr_ring = Ring(S, "kr", 2, [128, 4, 128], BF16)
                for t in range(NTF):
                    s = 1 if t in CTXF else 0
                    hT = hring.next()
                    self.load_hT(tools, xfull(t), s, 0, 1, hT, src_key=self.xf_key(t))
                    ps = psKV.next()
                    for half in range(2):
                        for kc in range(16):
                            S.op('pe', lambda e, half=half, kc=kc, hT=hT, ps=ps: e.matmul(
                                ps[:, half * 512:(half + 1) * 512], hT[:, kc, :], Wkv[:, kc, half * 512:(half + 1) * 512],
                                start=(kc == 0), stop=(kc == 15)), reads=[hT, (Wkv.name, kc // 4, half * 512)], writes=[ps])
                    S.op('act', lambda e, t=t, ps=ps: e.activation(V_all[:, t, :], ps[:, 512:1024], AF.Copy), reads=[ps], writes=[V_all])
                    S.op('act', lambda e, ps=ps: e.activation(sqk[:], ps[:, 0:512], AF.Square), reads=[ps], writes=[sqk])
                    S.op('dve', lambda e: e.reduce_sum(ss[:], sqk[:].rearrange("p (h d) -> p h d", h=4), AX.X), reads=[sqk], writes=[ss])
                    self.rstd_from_ss(ss[:], rstd[:], 128, tmp4[:])
                    S.op('dve', lambda e, ps=ps: e.tensor_tensor(kn[:], ps[:, 0:512].rearrange("p (h d) -> p h d", h=4),
                                                                rstd[:, :, None].to_broadcast([128, 4, 128]), ALU.mult),
                         reads=[ps, rstd], writes=[kn])
                    S.op('dve', lambda e: e.tensor_tensor(kn[:], kn[:], kgB[:, None, :].to_broadcast([128, 4, 128]), ALU.mult),
                         reads=[kn, kgB], writes=[kn])
                    tb = tbr.next()
                    S.dma('sp', tb[:], io['ropeF'][t * 128:(t + 1) * 128, :], writes=[tb])
                    kr = kr_ring.next()
                    self.rope(kn[:], kr[:], tb, 4, 128, t1[:], t2[:])
                    pk = psKT.next()
                    for g in range(4):
                        S.op('pe', lambda e, g=g, kr=kr, pk=pk: e.transpose(pk[:, g, :], kr[:, g, :], self.c['ident_b'][:]),
                             reads=[kr, self.c['ident_b']], writes=[pk])
                    S.op('act', lambda e, t=t, pk=pk: e.activation(kT_all[:, :, t * 128:(t + 1) * 128], pk[:], AF.Copy), reads=[pk], writes=[kT_all])
            with S.phase():
                Wq = S.sb("Wq", [128, 16, D], BF16)
                self.load_w(Wq, wq_b, 'wq')
                qgB = S.sb("qgB", [128, 128], F32)
                S.dma('sp', qgB[:], io['q_g'][0:1, :].partition_broadcast(128), writes=[qgB])
                tools = self.make_hT_tools(nx=1)
                hring = Ring(S, "hT", 2, [128, 16, 128], BF16)
                psQ = Ring(S, "psQ", 1, [128, D], F32, psum=True)
                psQT = Ring(S, "psQT", 1, [128, 16, 128], BF16, psum=True)
                tbr = Ring(S, "tb", 2, [128, 256], F32)
                sq = S.sb("sq", [128, D], F32)
                ss = S.sb("ss", [128, 16], F32); tmp16 = S.sb("tmp16", [128, 16], F32); rstd = S.sb("rstd16", [128, 16], F32)
                qn = S.sb("qn", [128, 16, 128], F32); t1 = S.sb("t1", [128, 16, 128], F32)
                qr_ring = Ring(S, "qr", 2, [128, 16, 128], BF16)
                qo_ring = Ring(S, "qo", 2, [128, 16, 128], BF16)
                for lt in self.ltiles:
                    s = 1 if lt == 0 else 0
                    hT = hring.next()
                    self.load_hT(tools, xloc(lt), s, 0, 1, hT, src_key=self.xl_key(lt))
                    ps = psQ.next()
                    for ng in range(4):
                        for kc in range(16):
                            S.op('pe', lambda e, ng=ng, kc=kc, hT=hT, ps=ps: e.matmul(
                                ps[:, ng * 512:(ng + 1) * 512], hT[:, kc, :], Wq[:, kc, ng * 512:(ng + 1) * 512],
                                start=(kc == 0), stop=(kc == 15)), reads=[hT, self.wk(Wq, kc)], writes=[ps])
                    S.op('act', lambda e, ps=ps: e.activation(sq[:], ps[:], AF.Square), reads=[ps], writes=[sq])
                    S.op('dve', lambda e: e.reduce_sum(ss[:], sq[:].rearrange("p (h d) -> p h d", h=16), AX.X), reads=[sq], writes=[ss])
                    self.rstd_from_ss(ss[:], rstd[:], 128, tmp16[:])
                    S.op('dve', lambda e, ps=ps: e.tensor_tensor(qn[:], ps[:].rearrange("p (h d) -> p h d", h=16),
                                                                rstd[:, :, None].to_broadcast([128, 16, 128]), ALU.mult),
                         reads=[ps, rstd], writes=[qn])
                    S.op('dve', lambda e: e.tensor_tensor(qn[:], qn[:], qgB[:, None, :].to_broadcast([128, 16, 128]), ALU.mult),
                         reads=[qn, qgB], writes=[qn])
                    tb = tbr.next()
                    S.dma('sp', tb[:], io['ropeL'][lt * 128:(lt + 1) * 128, :], writes=[tb])
                    qr = qr_ring.next()
                    t2 = sq[:].rearrange("p (h d) -> p h d", h=16)
                    self.rope(qn[:], qr[:], tb, 16, 128, t1[:], t2)
                    pq = psQT.next()
                    for hd in range(16):
                        S.op('pe', lambda e, hd=hd, qr=qr, pq=pq: e.transpose(pq[:, hd, :], qr[:, hd, :], self.c['ident_b'][:]),
                             reads=[qr, self.c['ident_b']], writes=[pq])
                    qo = qo_ring.next()
                    S.op('act', lambda e, qo=qo, pq=pq: e.activation(qo[:], pq[:], AF.Copy), reads=[pq], writes=[qo])
                    S.dma('sp', qT_d[:, :, lt * 128:(lt + 1) * 128].rearrange("h p t -> p h t"), qo[:], reads=[qo], writes=['qT_d'])
            with S.phase():
                blocks = []
                if not self.last:
                    blocks.append(([0], list(CTXF)))
                for b in range(4):
                    blocks.append((list(range(1 + 4 * b, 5 + 4 * b)), list(range(NTF))))
                qring = Ring(S, "qblk", 2, [128, 16, 512], BF16)
                state = {}

                def load_q(hd, bi, tiles, new_block, new_head):
                    if new_block:
                        qb = qring.next()
                        T = 128 * len(tiles)
                        t0 = tiles[0] * 128
                        for h0 in range(0, 16, 4):
                            S.dma('sp', qb[:, h0:h0 + 4, :T], qT_d[h0:h0 + 4, :, t0:t0 + T].rearrange("h p t -> p h t"),
                                  reads=['qT_d'], writes=[qb])
                        state['qb'] = qb
                    return state['qb']

                def score(ps, hd, kt, qb, T):
                    g = hd // 4
                    S.op('pe', lambda e: e.matmul(ps[:, :T], kT_all[:, g, kt * 128:(kt + 1) * 128], qb[:, hd, :T], start=True, stop=True),
                         reads=[kT_all, qb], writes=[ps])

                def vfn(hd, kt):
                    g = hd // 4
                    return V_all[:, kt, g * 128:(g + 1) * 128], [V_all]
                self.attention(blocks, 16, score, vfn, load_q, oT_d, 128 ** -0.5)
        return self.wb['wo'], 'wo'

    def prep_gqa(self):
        io = self.io
        self.wb = {'wk': self.cast2d("wk", io['wk']), 'wv': self.cast2d("wv", io['wv']),
                   'wq': self.cast2d("wq", io['wq']), 'wo': self.cast2d("wo", io['wo'])}

    def prep_mlp(self):
        io = self.io
        self.w1b = self.cast2d("w1b", io['mlp_w1'])
        self.w2b = self.cast2d("w2b", io['mlp_w2'])

    def prep_lru(self):
        io = self.io
        self.wb = {'wx': self.cast2d("wx", io['wx']), 'wy': self.cast2d("wy", io['wy']), 'wo': self.cast2d("wo", io['wo']),
                   'ra_w': self.cast2d("ra_w", io['ra_w']), 'ix_w': self.cast2d("ix_w", io['ix_w'])}

    def mixer_lru(self, xfull, xloc, oT_d):
        S, io = self.S, self.io
        uT_d = self.dram("uT", [16, 128, NTF * 128], F32)
        yT_d = self.dram("yT", [16, 128, NTL * 128], F32)
        NT = NTF * 128
        with S.phase():
            Wx = S.sb("Wx", [128, 16, D], BF16)
            self.load_w(Wx, self.wb['wx'], 'wx')
            tools = self.make_hT_tools()
            hring = Ring(S, "hTb", 2, [128, 16, 512], BF16)
            psU = Ring(S, "psU", 2, [128, 512], F32, psum=True)
            ust = Ring(S, "ust", 3, [128, 512], F32)
            blocks = [([0, 17], 1, 0)]
            for b in range(8):
                f0 = 1 + 4 * b if b < 4 else 18 + 4 * (b - 4)
                blocks.append((list(range(f0, f0 + 4)), 0, 256 + 512 * b))
            for tiles, s, t0 in blocks:
                T = 128 * len(tiles)
                hb = hring.next()
                for ti, t in enumerate(tiles):
                    self.load_hT(tools, xfull(t), s, 0, 1, hb, col0=ti * 128, src_key=self.xf_key(t))
                for cc in range(16):
                    ps = psU.next()
                    for kc in range(16):
                        S.op('pe', lambda e, kc=kc, cc=cc, ps=ps, hb=hb: e.matmul(ps[:, :T], Wx[:, kc, cc * 128:(cc + 1) * 128], hb[:, kc, :T],
                                                                                 start=(kc == 0), stop=(kc == 15)),
                             reads=[self.wk(Wx, kc), hb], writes=[ps])
                    u = ust.next()
                    S.op('act', lambda e, u=u, ps=ps: e.activation(u[:, :T], ps[:, :T], AF.Copy), reads=[ps], writes=[u])
                    S.dma('sp', uT_d[cc, :, t0:t0 + T], u[:, :T], reads=[u], writes=[('uT', cc)])
        with S.phase():
            convw = S.sb("convw", [128, 16, 4], F32); convb = S.sb("convb", [128, 16], F32)
            rab = S.sb("rab", [128, 16, 2], F32); ixb = S.sb("ixb", [128, 16, 2], F32)
            lam = S.sb("lam", [128, 16, 2], F32); cd = S.sb("cd", [128, 16, 2], F32)
            ee = S.sb("ee", [128, 16, 2], F32); tt = S.sb("tt", [128, 16, 2], F32)
            hmask = S.sb("hmask", [128, 2], F32)
            for tl, nm in ((convw, 'convwT'), (convb, 'convbT'), (rab, 'rabT'), (ixb, 'ixbT'), (lam, 'lamT'), (hmask, 'hmask')):
                S.dma('sp', tl[:], io[nm], writes=[tl])
            S.op('act', lambda e: e.activation(ee[:], lam[:], AF.Exp, scale=-1.0), reads=[lam], writes=[ee])
            S.op('dve', lambda e: e.tensor_scalar(tt[:], ee[:], -0.25, 1.0 / 3.0, ALU.mult, ALU.add), reads=[ee], writes=[tt])
            S.op('dve', lambda e: e.tensor_tensor(tt[:], tt[:], ee[:], ALU.mult), reads=[tt, ee], writes=[tt])
            S.op('dve', lambda e: e.tensor_scalar(tt[:], tt[:], -1.0, 0.5, ALU.mult, ALU.add), reads=[tt], writes=[tt])
            S.op('dve', lambda e: e.tensor_tensor(tt[:], tt[:], ee[:], ALU.mult), reads=[tt, ee], writes=[tt])
            S.op('dve', lambda e: e.tensor_scalar(tt[:], tt[:], -1.0, 1.0, ALU.mult, ALU.add), reads=[tt], writes=[tt])
            S.op('dve', lambda e: e.tensor_tensor(tt[:], tt[:], ee[:], ALU.mult), reads=[tt, ee], writes=[tt])
            S.op('dve', lambda e: e.tensor_scalar_mul(cd[:], tt[:], -8.0), reads=[tt], writes=[cd])
            up = S.sb("up", [128, 4360], F32)
            S.op('pool', lambda e: e.memset(up[:], 0.0), writes=[up])
            uc = S.sb("uc", [128, 2, NT], F32)
            ucb = S.sb("ucb", [128, 2, NT], BF16)
            R = S.sb("R", [128, NT], F32); G = S.sb("G", [128, NT], F32); M = S.sb("M", [128, NT], F32)
            Y = S.sb("Y", [128, NT], F32); H = S.sb("H", [128, NT], F32)
            ysel = S.sb("ysel", [128, NTL * 128], F32)
            gwr = Ring(S, "gw", 2, [128, 2, 2, 2, 256], BF16)
            psR = Ring(S, "psR", 2, [128, 512], F32, psum=True)
            psI = Ring(S, "psI", 2, [128, 512], F32, psum=True)
            tblocks = [(t0, min(512, NT - t0)) for t0 in range(0, NT, 512)]
            for n in range(8):
                gw = gwr.next()
                for d in range(2):
                    for m, nm in enumerate(('ra_w', 'ix_w')):
                        S.dma('sp', gw[:, :, d, m, :], self.wb[nm][d, n].rearrange("(kk p) j -> p kk j", p=128),
                              reads=self.wkeys[nm], writes=[(gw.name, d, m)])
                for oc in range(2):
                    cc = 2 * n + oc
                    S.dma('sp', up[:, 2:258], uT_d[cc, :, 0:256], reads=[('uT', cc)], writes=[up])
                    S.dma('sp', up[:, 262:4358], uT_d[cc, :, 256:NT], reads=[('uT', cc)], writes=[up])
                    for (o0, n_, u0) in ((0, 256, 0), (256, 4096, 260)):
                        S.op('dve', lambda e, o0=o0, n_=n_, u0=u0, oc=oc, cc=cc: e.tensor_scalar(
                            uc[:, oc, o0:o0 + n_], up[:, u0:u0 + n_], convw[:, cc, 0:1], convb[:, cc:cc + 1], ALU.mult, ALU.add),
                            reads=[up, convw, convb], writes=[uc])
                        for j in range(1, 4):
                            S.op('dve', lambda e, o0=o0, n_=n_, u0=u0, oc=oc, cc=cc, j=j: e.scalar_tensor_tensor(
                                uc[:, oc, o0:o0 + n_], up[:, u0 + j:u0 + j + n_], convw[:, cc, j:j + 1], uc[:, oc, o0:o0 + n_],
                                ALU.mult, ALU.add), reads=[up, convw, uc], writes=[uc])
                    S.op('pool', lambda e, oc=oc: e.tensor_copy(ucb[:, oc, :], uc[:, oc, :]), reads=[uc], writes=[ucb])
                for oc in range(2):
                    cc = 2 * n + oc
                    for d in range(2):
                        for (t0, T) in tblocks:
                            pr = psR.next(); pi = psI.next()
                            for kk in range(2):
                                S.op('pe', lambda e, kk=kk, d=d, oc=oc, pr=pr, t0=t0, T=T, gw=gw: e.matmul(
                                    pr[:, :T], gw[:, kk, d, 0, oc * 128:(oc + 1) * 128], ucb[:, kk, t0:t0 + T],
                                    start=(kk == 0), stop=(kk == 1)), reads=[(gw.name, d, 0), ucb], writes=[pr])
                            for kk in range(2):
                                S.op('pe', lambda e, kk=kk, d=d, oc=oc, pi=pi, t0=t0, T=T, gw=gw: e.matmul(
                                    pi[:, :T], gw[:, kk, d, 1, oc * 128:(oc + 1) * 128], ucb[:, kk, t0:t0 + T],
                                    start=(kk == 0), stop=(kk == 1)), reads=[(gw.name, d, 1), ucb], writes=[pi])
                            S.op('act', lambda e, pr=pr, t0=t0, T=T, cc=cc, d=d: e.activation(R[:, t0:t0 + T], pr[:, :T], AF.Sigmoid,
                                                                                           bias=rab[:, cc, d:d + 1]),
                                 reads=[pr, rab], writes=[R])
                            S.op('act', lambda e, pi=pi, t0=t0, T=T, cc=cc, d=d: e.activation(G[:, t0:t0 + T], pi[:, :T], AF.Sigmoid,
                                                                                           bias=ixb[:, cc, d:d + 1]),
                                 reads=[pi, ixb], writes=[G])
                        S.op('act', lambda e, cc=cc, d=d: e.activation(R[:], R[:], AF.Exp, scale=cd[:, cc, d:d + 1]), reads=[R, cd], writes=[R])
                        S.op('pool', lambda e: e.tensor_tensor(M[:], R[:], R[:], ALU.mult), reads=[R], writes=[M])
                        S.op('act', lambda e: e.activation(M[:], M[:], AF.Sqrt, bias=1.0, scale=-1.0), reads=[M], writes=[M])
                        S.op('dve', lambda e, oc=oc: e.tensor_tensor(G[:], G[:], uc[:, oc, :], ALU.mult), reads=[G, uc], writes=[G])
                        S.op('pool', lambda e: e.tensor_tensor(G[:], G[:], M[:], ALU.mult), reads=[G, M], writes=[G])
                        if d == 0:
                            S.op('dve', lambda e: e.tensor_tensor_scan(Y[:, 0:256], R[:, 0:256], G[:, 0:256], 0.0, ALU.mult, ALU.add),
                                 reads=[R, G], writes=[Y])
                            S.op('dve', lambda e: e.tensor_tensor_scan(Y[:, 256:NT], R[:, 256:NT], G[:, 256:NT], Y[:, 255:256], ALU.mult, ALU.add),
                                 reads=[R, G, Y], writes=[Y])
                        else:
                            S.op('dve', lambda e: e.tensor_tensor_scan(H[:, 0:256][:, ::-1], R[:, 0:256][:, ::-1], G[:, 0:256][:, ::-1], 0.0,
                                                                       ALU.mult, ALU.add), reads=[R, G], writes=[H])
                            S.op('dve', lambda e: e.tensor_tensor_scan(H[:, 256:NT][:, ::-1], R[:, 256:NT][:, ::-1], G[:, 256:NT][:, ::-1],
                                                                       H[:, 0:1], ALU.mult, ALU.add), reads=[R, G, H], writes=[H])
                            S.op('pool', lambda e: e.tensor_tensor(Y[:], Y[:], H[:], ALU.add), reads=[Y, H], writes=[Y])
                    for (o0, n_, a0, a1) in ((0, 128, 0, 128), (128, 2048, 256, 2304)):
                        S.op('dve', lambda e, o0=o0, n_=n_, a0=a0: e.tensor_scalar_mul(ysel[:, o0:o0 + n_], Y[:, a0:a0 + n_], hmask[:, 0:1]),
                             reads=[Y, hmask], writes=[ysel])
                        S.op('dve', lambda e, o0=o0, n_=n_, a1=a1: e.scalar_tensor_tensor(ysel[:, o0:o0 + n_], Y[:, a1:a1 + n_], hmask[:, 1:2],
                                                                                         ysel[:, o0:o0 + n_], ALU.mult, ALU.add),
                             reads=[Y, hmask, ysel], writes=[ysel])
                    S.dma('sp', yT_d[cc], ysel[:], reads=[ysel], writes=[('yT', cc)])
        with S.phase():
            Wy = S.sb("Wy", [128, 16, D], BF16)
            self.load_w(Wy, self.wb['wy'], 'wy')
            tools = self.make_hT_tools()
            hring = Ring(S, "hTb", 2, [128, 16, 512], BF16)
            psG = Ring(S, "psG", 2, [128, 512], F32, psum=True)
            gst = Ring(S, "gst", 2, [128, 512], F32)
            ysl = Ring(S, "ysl", 2, [128, 512], F32)
            zb = Ring(S, "zb", 2, [128, 512], BF16)
            blocks = []
            if not self.last:
                blocks.append(([0], 1))
            for b in range(4):
                blocks.append((list(range(1 + 4 * b, 5 + 4 * b)), 0))
            for tiles, s in blocks:
                T = 128 * len(tiles)
                t0 = tiles[0] * 128
                hb = hring.next()
                for ti, lt in enumerate(tiles):
                    self.load_hT(tools, xloc(lt), s, 0, 1, hb, col0=ti * 128, src_key=self.xl_key(lt))
                for cc in range(16):
                    ps = psG.next()
                    for kc in range(16):
                        S.op('pe', lambda e, kc=kc, cc=cc, ps=ps, hb=hb: e.matmul(ps[:, :T], Wy[:, kc, cc * 128:(cc + 1) * 128], hb[:, kc, :T],
                                                                                 start=(kc == 0), stop=(kc == 15)),
                             reads=[self.wk(Wy, kc), hb], writes=[ps])
                    g = gst.next()
                    S.op('act', lambda e, g=g, ps=ps: e.activation(g[:, :T], ps[:, :T], AF.Gelu_apprx_tanh), reads=[ps], writes=[g])
                    yl = ysl.next()
                    S.dma('sp', yl[:, :T], yT_d[cc, :, t0:t0 + T], reads=[('yT', cc)], writes=[yl])
                    z = zb.next()
                    S.op('dve', lambda e, z=z, g=g, yl=yl: e.tensor_tensor(z[:, :T], g[:, :T], yl[:, :T], ALU.mult), reads=[g, yl], writes=[z])
                    S.dma('sp', oT_d[cc, :, t0:t0 + T], z[:, :T], reads=[z], writes=['oT_d'])
        return self.wb['wo'], 'wo'

    def prep_mla(self):
        io = self.io
        self.wb = {'wq_a': self.cast2d("wq_a", io['wq_a']), 'wq_b': self.cast2d("wq_b", io['wq_b'], piece_cols=1536),
                   'wkv_a': self.cast2d("wkv_a", io['wkv_a']), 'wkv_b': self.cast2d("wkv_b", io['wkv_b']),
                   'wo': self.cast2d("wo", io['wo'])}

    def mixer_mla(self, xfull, xloc, oT_d):
        S, io = self.S, self.io
        NT = NTF * 128
        qnT_d = self.dram("qnT", [16, 128, NTL * 128], BF16)
        qpT_d = self.dram("qpT", [16, 64, NTL * 128], BF16)
        with S.phase():
            ckvT = S.sb("ckvT", [128, 4, NT], BF16)
            kpeT = S.sb("kpeT", [64, NT], BF16)
            with S.phase():
                Wkva = S.sb("Wkva", [128, 16, 576], BF16)
                self.load_w(Wkva, self.wb['wkv_a'], 'wkv_a')
                kvgB = S.sb("kvgB", [128, 512], F32)
                S.dma('sp', kvgB[:], io['kv_a_g'][0:1, :].partition_broadcast(128), writes=[kvgB])
                tools = self.make_hT_tools()
                hring = Ring(S, "hT", 2, [128, 16, 128], BF16)
                psC = Ring(S, "psC", 1, [128, 1024], F32, psum=True)
                psCT = Ring(S, "psCT", 1, [128, 4, 128], BF16, psum=True)
                psKP = Ring(S, "psKP", 1, [128, 1024], BF16, psum=True)
                tbr = Ring(S, "tb", 2, [128, 128], F32)
                sqc = S.sb("sqc", [128, 512], F32)
                ss = S.sb("ss", [128, 1], F32); tmp1 = S.sb("tmp1", [128, 1], F32); rstd = S.sb("rstd1", [128, 1], F32)
                cnr = Ring(S, "cn", 2, [128, 512], BF16)
                kp = S.sb("kp", [128, 1, 64], F32); t1 = S.sb("t1", [128, 1, 64], F32); t2 = S.sb("t2", [128, 1, 64], F32)
                krr = Ring(S, "kr", 2, [128, 1, 64], BF16)
                for t in range(NTF):
                    s = 1 if t in CTXF else 0
                    hT = hring.next()
                    self.load_hT(tools, xfull(t), s, 0, 1, hT, src_key=self.xf_key(t))
                    ps = psC.next()
                    for (c0, c1) in ((0, 512), (512, 576)):
                        for kc in range(16):
                            S.op('pe', lambda e, kc=kc, c0=c0, c1=c1, hT=hT, ps=ps: e.matmul(ps[:, c0:c1], hT[:, kc, :], Wkva[:, kc, c0:c1],
                                                                                            start=(kc == 0), stop=(kc == 15)),
                                 reads=[hT, self.wk(Wkva, kc)], writes=[ps])
                    S.op('act', lambda e, ps=ps: e.activation(sqc[:], ps[:, 0:512], AF.Square, accum_out=ss[:]), reads=[ps], writes=[sqc, ss])
                    self.rstd_from_ss(ss[:], rstd[:], 512, tmp1[:])
                    cn = cnr.next()
                    S.op('dve', lambda e, cn=cn, ps=ps: e.scalar_tensor_tensor(cn[:], ps[:, 0:512], rstd[:], kvgB[:], ALU.mult, ALU.mult),
                         reads=[ps, rstd, kvgB], writes=[cn])
                    S.op('act', lambda e, ps=ps: e.activation(kp[:, 0, :], ps[:, 512:576], AF.Copy), reads=[ps], writes=[kp])
                    tb = tbr.next()
                    S.dma('sp', tb[:], io['ropeF'][t * 128:(t + 1) * 128, :], writes=[tb])
                    kr = krr.next()
                    self.rope(kp[:], kr[:], tb, 1, 64, t1[:], t2[:])
                    pc = psCT.next()
                    for c in range(4):
                        S.op('pe', lambda e, c=c, cn=cn, pc=pc: e.transpose(pc[:, c, :], cn[:, c * 128:(c + 1) * 128], self.c['ident_b'][:]),
                             reads=[cn, self.c['ident_b']], writes=[pc])
                    S.op('act', lambda e, t=t, pc=pc: e.activation(ckvT[:, :, t * 128:(t + 1) * 128], pc[:], AF.Copy), reads=[pc], writes=[ckvT])
                    pk = psKP.next()
                    S.op('pe', lambda e, kr=kr, pk=pk: e.transpose(pk[0:64, 0:128], kr[:, 0, :], self.c['ident_b'][:]),
                         reads=[kr, self.c['ident_b']], writes=[pk])
                    S.op('dve', lambda e, t=t, pk=pk: e.tensor_copy(kpeT[:, t * 128:(t + 1) * 128], pk[0:64, 0:128]), reads=[pk], writes=[kpeT])
            with S.phase():
                Wqa = S.sb("Wqa", [128, 16, 512], BF16)
                self.load_w(Wqa, self.wb['wq_a'], 'wq_a')
                Wqb = S.sb("Wqb", [128, 4, 3072], BF16)
                self.load_w(Wqb, self.wb['wq_b'], 'wq_b')
                qagB = S.sb("qagB", [128, 512], F32)
                S.dma('sp', qagB[:], io['q_a_g'][0:1, :].partition_broadcast(128), writes=[qagB])
                tools = self.make_hT_tools()
                hring = Ring(S, "hT", 2, [128, 16, 128], BF16)
                psA = Ring(S, "psA", 1, [128, 512], F32, psum=True)
                psAT = Ring(S, "psAT", 1, [128, 4, 128], BF16, psum=True)
                psQh = Ring(S, "psQh", 1, [128, 1536], F32, psum=True)
                psT8 = Ring(S, "psT8", 1, [128, 8, 128], BF16, psum=True)
                tbr = Ring(S, "tb", 2, [128, 128], F32)
                sqa = S.sb("sqa", [128, 512], F32)
                ss = S.sb("ss", [128, 1], F32); tmp1 = S.sb("tmp1", [128, 1], F32); rstd = S.sb("rstd1", [128, 1], F32)
                qar = Ring(S, "qa", 2, [128, 512], BF16)
                qaTr = Ring(S, "qaT", 2, [128, 4, 128], BF16)
                qn8r = Ring(S, "qn8", 2, [128, 8, 128], BF16)
                pe8 = S.sb("pe8", [128, 8, 64], F32); t1 = S.sb("t1", [128, 8, 64], F32); t2 = S.sb("t2", [128, 8, 64], F32)
                qp8r = Ring(S, "qp8", 2, [128, 8, 64], BF16)
                qnor = Ring(S, "qno", 2, [128, 8, 128], BF16)
                qpor = Ring(S, "qpo", 2, [64, 8, 128], BF16)
                for lt in self.ltiles:
                    s = 1 if lt == 0 else 0
                    hT = hring.next()
                    self.load_hT(tools, xloc(lt), s, 0, 1, hT, src_key=self.xl_key(lt))
                    ps = psA.next()
                    for kc in range(16):
                        S.op('pe', lambda e, kc=kc, hT=hT, ps=ps: e.matmul(ps[:], hT[:, kc, :], Wqa[:, kc, :], start=(kc == 0), stop=(kc == 15)),
                             reads=[hT, self.wk(Wqa, kc)], writes=[ps])
                    S.op('act', lambda e, ps=ps: e.activation(sqa[:], ps[:], AF.Square, accum_out=ss[:]), reads=[ps], writes=[sqa, ss])
                    self.rstd_from_ss(ss[:], rstd[:], 512, tmp1[:])
                    qa = qar.next()
                    S.op('dve', lambda e, qa=qa, ps=ps: e.scalar_tensor_tensor(qa[:], ps[:], rstd[:], qagB[:], ALU.mult, ALU.mult),
                         reads=[ps, rstd, qagB], writes=[qa])
                    pa = psAT.next()
                    for c in range(4):
                        S.op('pe', lambda e, c=c, qa=qa, pa=pa: e.transpose(pa[:, c, :], qa[:, c * 128:(c + 1) * 128], self.c['ident_b'][:]),
                             reads=[qa, self.c['ident_b']], writes=[pa])
                    qaT = qaTr.next()
                    S.op('act', lambda e, qaT=qaT, pa=pa: e.activation(qaT[:], pa[:], AF.Copy), reads=[pa], writes=[qaT])
                    tb = tbr.next()
                    S.dma('sp', tb[:], io['ropeL'][lt * 128:(lt + 1) * 128, :], writes=[tb])
                    for half in range(2):
                        pq = psQh.next()
                        for ng in range(3):
                            c0 = half * 1536 + ng * 512
                            for kc in range(4):
                                S.op('pe', lambda e, ng=ng, kc=kc, c0=c0, pq=pq, qaT=qaT: e.matmul(
                                    pq[:, ng * 512:(ng + 1) * 512], qaT[:, kc, :], Wqb[:, kc, c0:c0 + 512], start=(kc == 0), stop=(kc == 3)),
                                    reads=[qaT, self.wk(Wqb, kc)], writes=[pq])
                        pv = pq[:].rearrange("p (h d) -> p h d", h=8)
                        qn8 = qn8r.next()
                        S.op('act', lambda e, qn8=qn8, pv=pv: e.activation(qn8[:], pv[:, :, 0:128], AF.Copy), reads=[pq], writes=[qn8])
                        S.op('act', lambda e, pv=pv: e.activation(pe8[:], pv[:, :, 128:192], AF.Copy), reads=[pq], writes=[pe8])
                        qp8 = qp8r.next()
                        self.rope(pe8[:], qp8[:], tb, 8, 64, t1[:], t2[:])
                        pt = psT8.next()
                        for h in range(8):
                            S.op('pe', lambda e, h=h, qn8=qn8, pt=pt: e.transpose(pt[:, h, :], qn8[:, h, :], self.c['ident_b'][:]),
                                 reads=[qn8, self.c['ident_b']], writes=[pt])
                        qno = qnor.next()
                        S.op('dve', lambda e, qno=qno, pt=pt: e.tensor_copy(qno[:], pt[:]), reads=[pt], writes=[qno])
                        S.dma('sp', qnT_d[half * 8:(half + 1) * 8, :, lt * 128:(lt + 1) * 128].rearrange("h p t -> p h t"), qno[:],
                              reads=[qno], writes=['qnT_d'])
                        for h in range(8):
                            S.op('pe', lambda e, h=h, qp8=qp8, pt=pt: e.transpose(pt[0:64, h, :], qp8[:, h, :], self.c['ident_b'][:]),
                                 reads=[qp8, self.c['ident_b']], writes=[pt])
                        qpo = qpor.next()
                        S.op('dve', lambda e, qpo=qpo, pt=pt: e.tensor_copy(qpo[:], pt[0:64, :, :]), reads=[pt], writes=[qpo])
                        S.dma('sp', qpT_d[half * 8:(half + 1) * 8, :, lt * 128:(lt + 1) * 128].rearrange("h p t -> p h t"), qpo[:],
                              reads=[qpo], writes=['qpT_d'])
            with S.phase():
                Wkvb = S.sb("Wkvb", [128, 4, 4096], BF16)
                self.load_w(Wkvb, self.wb['wkv_b'], 'wkv_b')
                knTr = Ring(S, "knT", 2, [128, NT], BF16)
                Vhr = Ring(S, "Vh", 2, [128, NTF, 128], BF16)
                psKV = Ring(S, "psKVm", 2, [128, 512], F32, psum=True)
                qnr = Ring(S, "qnb", 2, [128, 512], BF16)
                qpr = Ring(S, "qpb", 2, [64, 512], BF16)
                blocks = []
                if not self.last:
                    blocks.append(([0], list(CTXF)))
                for b in range(4):
                    blocks.append((list(range(1 + 4 * b, 5 + 4 * b)), list(range(NTF))))
                st = {}

                def pre_head(hd):
                    knT = knTr.next(); Vh = Vhr.next()
                    st['knT'], st['Vh'] = knT, Vh
                    for i, t0 in enumerate(range(0, NT, 512)):
                        T = min(512, NT - t0)
                        ps = psKV.next()
                        for kc in range(4):
                            S.op('pe', lambda e, kc=kc, ps=ps, t0=t0, T=T: e.matmul(ps[:, :T], Wkvb[:, kc, hd * 256:hd * 256 + 128], ckvT[:, kc, t0:t0 + T],
                                                                                 start=(kc == 0), stop=(kc == 3)),
                                 reads=[self.wk(Wkvb, kc), ckvT], writes=[ps])
                        S.op('dve', lambda e, ps=ps, t0=t0, T=T, knT=knT: e.tensor_copy(knT[:, t0:t0 + T], ps[:, :T]), reads=[ps], writes=[knT])
                    for g4 in range(0, NTF, 4):
                        nj = min(4, NTF - g4)
                        ps = psKV.next()
                        for j in range(nj):
                            kt = g4 + j
                            for kc in range(4):
                                S.op('pe', lambda e, kc=kc, ps=ps, j=j, kt=kt: e.matmul(ps[:, j * 128:(j + 1) * 128], ckvT[:, kc, kt * 128:(kt + 1) * 128],
                                                                                      Wkvb[:, kc, hd * 256 + 128:hd * 256 + 256],
                                                                                      start=(kc == 0), stop=(kc == 3)),
                                     reads=[self.wk(Wkvb, kc), ckvT], writes=[ps])
                        S.op('dve', lambda e, ps=ps, g4=g4, nj=nj, Vh=Vh: e.tensor_copy(Vh[:, g4:g4 + nj, :],
                                                                                       ps[:, 0:nj * 128].rearrange("p (j d) -> p j d", j=nj)),
                             reads=[ps], writes=[Vh])

                def load_q(hd, bi, tiles, new_block, new_head):
                    T = 128 * len(tiles)
                    t0 = tiles[0] * 128
                    qn = qnr.next(); qp = qpr.next()
                    S.dma('sp', qn[:, :T], qnT_d[hd, :, t0:t0 + T], reads=['qnT_d'], writes=[qn])
                    S.dma('sp', qp[:, :T], qpT_d[hd, :, t0:t0 + T], reads=['qpT_d'], writes=[qp])
                    return (qn, qp)

                def score(ps, hd, kt, qb, T):
                    qn, qp = qb
                    knT = st['knT']
                    S.op('pe', lambda e: e.matmul(ps[:, :T], knT[:, kt * 128:(kt + 1) * 128], qn[:, :T], start=True, stop=False),
                         reads=[knT, qn], writes=[ps])
                    S.op('pe', lambda e: e.matmul(ps[:, :T], kpeT[:, kt * 128:(kt + 1) * 128], qp[:, :T], start=False, stop=True),
                         reads=[kpeT, qp], writes=[ps])

                def vfn(hd, kt):
                    Vh = st['Vh']
                    return Vh[:, kt, :], [Vh]
                self.attention(blocks, 16, score, vfn, load_q, oT_d, 192 ** -0.5, pre_head=pre_head, head_outer=True)
        return self.wb['wo'], 'wo'

    def prep(self):
        ns = self.S.ns
        self.S.ns = self.li
        if self.kind == 0:
            self.prep_gqa()
        elif self.kind == 1:
            self.prep_lru()
        else:
            self.prep_mla()
        self.prep_mlp()
        self.S.ns = ns

    def build(self, xfull, xloc, xout, keys, on_tile_done=None, before_mlp=None):
        S = self.S
        S.ns = self.li
        self.xf_key, self.xl_key, self.xo_key = keys
        self.on_tile_done = on_tile_done
        self.phase_mod()
        oT_d = self.dram("oT", [16, 128, NTL * 128], BF16)
        xmid = self.dram("xmid", [NTL * 128, D], F32)
        if self.kind == 0:
            wo_b, wo_key = self.mixer_gqa(xfull, xloc, oT_d)
        elif self.kind == 1:
            wo_b, wo_key = self.mixer_lru(xfull, xloc, oT_d)
        else:
            wo_b, wo_key = self.mixer_mla(xfull, xloc, oT_d)
        self.phase_out(oT_d, wo_b, wo_key, xloc, xmid)
        if before_mlp is not None:
            before_mlp()
        self.phase_mlp(xmid, xout)


def make_consts(S):
    c = {}
    c['ident_f'] = S.sb("ident_f", [128, 128], F32)
    c['ident_b'] = S.sb("ident_b", [128, 128], BF16)
    c['ones_f'] = S.sb("ones_f", [128, 128], F32)
    c['ones_b'] = S.sb("ones_b", [128, 128], BF16)
    S.op('pool', lambda e: e.memset(c['ident_f'][:], 1.0), writes=[c['ident_f']])
    S.op('pool', lambda e: e.affine_select(out=c['ident_f'][:], in_=c['ident_f'][:], pattern=[[-1, 128]],
                                           compare_op=ALU.is_equal, fill=0.0, base=0, channel_multiplier=1),
         reads=[c['ident_f']], writes=[c['ident_f']])
    S.op('dve', lambda e: e.tensor_copy(c['ident_b'][:], c['ident_f'][:]), reads=[c['ident_f']], writes=[c['ident_b']])
    S.op('pool', lambda e: e.memset(c['ones_f'][:], 1.0), writes=[c['ones_f']])
    S.op('pool', lambda e: e.memset(c['ones_b'][:], 1.0), writes=[c['ones_b']])
    return c


def layer_input_specs(kind):
    sp = {
        'cT': ([128, 32], F32), 'adabT': ([128, 96], F32), 'ada_w': ([D, 6 * D], F32),
        'ln_g': ([2, D], F32), 'ln_b': ([2, D], F32), 'mlp_w1': ([D, DFF], F32), 'mlp_w2': ([DFF, D], F32),
    }
    if kind == 0:
        sp.update({'wq': ([D, D], F32), 'wk': ([D, 512], F32), 'wv': ([D, 512], F32), 'wo': ([D, D], F32),
                   'q_g': ([1, 128], F32), 'k_g': ([1, 128], F32),
                   'ropeF': ([NTF * 128, 256], F32), 'ropeL': ([NTL * 128, 256], F32)})
    elif kind == 1:
        sp.update({'wx': ([D, D], F32), 'wy': ([D, D], F32), 'wo': ([D, D], F32),
                   'ra_w': ([2, 8, 256, 256], F32), 'ix_w': ([2, 8, 256, 256], F32),
                   'convwT': ([128, 16, 4], F32), 'convbT': ([128, 16], F32), 'rabT': ([128, 16, 2], F32),
                   'ixbT': ([128, 16, 2], F32), 'lamT': ([128, 16, 2], F32), 'hmask': ([128, 2], F32)})
    else:
        sp.update({'wq_a': ([D, 512], F32), 'wq_b': ([512, 3072], F32), 'wkv_a': ([D, 576], F32), 'wkv_b': ([512, 4096], F32),
                   'wo': ([D, D], F32), 'q_a_g': ([1, 512], F32), 'kv_a_g': ([1, 512], F32),
                   'ropeF': ([NTF * 128, 128], F32), 'ropeL': ([NTL * 128, 128], F32)})
    return sp


_DBG = {}
PAIRS = [[0, 1], [2, 3], [4, 5], [6, 7]]
CHUNKS = [(c, [2 * c, 2 * c + 1]) for c in range(8)] + [(8, [16])]


def build_program(layers=(0, 1, 2, 3)):
    nc = bass.Bass("TRN2", target_bir_lowering=False)
    S = Sched(nc)
    S.limit = _DBG.get('limit', 1 << 60)
    consts = make_consts(S)
    x_in = nc.dram_tensor("xfull", [NTF * 128, D], F32, kind="ExternalInput").ap()
    xl_in = nc.dram_tensor("xloc", [NTL * 128, D], F32, kind="ExternalInput").ap()
    final = layers[-1] == DEPTH - 1
    if final:
        out = nc.dram_tensor("out", [2048, D], F32, kind="ExternalOutput").ap()
    else:
        out = nc.dram_tensor("out", [NTL * 128, D], F32, kind="ExternalOutput").ap()
    progs = []
    for li in layers:
        io = {}
        for name, (shape, dt) in layer_input_specs(li % 3).items():
            io[name] = nc.dram_tensor("L%d_%s" % (li, name), shape, dt, kind="ExternalInput").ap()
        progs.append(LayerProg(S, nc, li, li % 3, li == DEPTH - 1, io, consts))
    xfull = lambda t: x_in[t * 128:(t + 1) * 128, :]
    xf_key = lambda t: ('abs', 'xin', t)
    xloc = lambda lt: xl_in[lt * 128:(lt + 1) * 128, :]
    xl_key = lambda lt: ('abs', 'xlin', lt)
    progs[0].prep()
    for i, lp in enumerate(progs):
        li = lp.li
        is_last_in_prog = i == len(progs) - 1
        if is_last_in_prog:
            if final:
                xout = lambda lt: out[(lt - 1) * 128:lt * 128, :]
            else:
                xout = lambda lt: out[lt * 128:(lt + 1) * 128, :]
            on_done = None
            gat = None
        else:
            xres = [nc.dram_tensor("xres%d_%d" % (li, c), [128 * len(tl), D], F32) for c, tl in CHUNKS]
            gat = [nc.dram_tensor("xgat%d_%d" % (li, c), [2 * 128 * len(tl), D], F32) for c, tl in CHUNKS]

            def xout(lt, xres=xres):
                c = min(lt // 2, 8)
                r = (lt - 2 * c) * 128
                return xres[c].ap()[r:r + 128, :]

            def on_done(lt, li=li, xres=xres, gat=gat):
                c, tl = CHUNKS[min(lt // 2, 8)]
                if lt != tl[-1]:
                    return
                if _DBG.get('nocc'):
                    n = 128 * len(tl)
                    S.dma('sp', gat[c].ap()[0:n, :], xres[c].ap()[:, :], reads=[('abs', 'xout', li, t) for t in tl], writes=[('abs', 'xgat', li, c)])
                    S.dma('sp', gat[c].ap()[n:2 * n, :], xres[c].ap()[:, :], reads=[('abs', 'xout', li, t) for t in tl], writes=[('abs', 'xgat', li, c)])
                    return
                S.collective("AllGather", [xres[c].ap().opt()], [gat[c].ap().opt()], PAIRS,
                             reads=[('abs', 'xout', li, t) for t in tl], writes=[('abs', 'xgat', li, c)])
        xo_key = lambda lt, li=li: ('abs', 'xout', li, lt)
        nxt = progs[i + 1] if not is_last_in_prog else None
        early = nxt is not None and not _DBG.get('late_prep')
        lp.build(xfull, xloc, xout, (xf_key, xl_key, xo_key), on_tile_done=on_done,
                 before_mlp=(nxt.prep if early else None))
        if nxt is not None and not early:
            nxt.prep()
        if not is_last_in_prog:
            def xfull(t, gat=gat):
                r, lt = divmod(t, NTL)
                c, tl = CHUNKS[min(lt // 2, 8)]
                row = r * 128 * len(tl) + (lt - tl[0]) * 128
                return gat[c].ap()[row:row + 128, :]
            xf_key = lambda t, li=li: ('abs', 'xgat', li, min((t % NTL) // 2, 8))
            xloc = xout
            xl_key = xo_key
    S.finish()
    return nc, S


def rope_tables(rot_dim):
    t = np.arange(4096)
    row = (t // 64).astype(np.float32)
    col = (t % 64).astype(np.float32)
    m = rot_dim // 2
    inv = (np.float32(10000.0) ** (-(np.arange(m // 2, dtype=np.float32) * np.float32(2.0)) / np.float32(m))).astype(np.float32)
    ar = (row[:, None] * inv[None, :]).astype(np.float32)
    ac = (col[:, None] * inv[None, :]).astype(np.float32)
    cr, sr, cc, sc = np.cos(ar), np.sin(ar), np.cos(ac), np.sin(ac)
    cosF = np.concatenate([cr, cr, cc, cc], 1)
    sinA = np.concatenate([-sr, sr, -sc, sc], 1)
    return np.concatenate([cosF, sinA], 1).astype(np.float32)


def ident_table(n, rot_dim):
    return np.concatenate([np.ones((n, rot_dim), np.float32), np.zeros((n, rot_dim), np.float32)], 1)


def full_order(ctx_b, lat_b):
    return np.concatenate([ctx_b[0:128], lat_b[0:2048], ctx_b[128:256], lat_b[2048:4096]], 0)


def local_order(ctx_b, lat_b, h):
    return np.concatenate([ctx_b[128 * h:128 * h + 128], lat_b[2048 * h:2048 * h + 2048]], 0)


def per_part(v, nchunk):
    return np.ascontiguousarray(v.reshape(nchunk, 128).T)


_PROG_CACHE = {}


def layer_host_inputs(inp, li):
    f = lambda a: np.ascontiguousarray(np.asarray(a, np.float32))
    kind, slot = li % 3, li // 3
    m = {
        'adabT': per_part(f(inp['ada_b'][li]), 96), 'ada_w': f(inp['ada_w'][li]),
        'ln_g': f(inp['ln_g'][li]), 'ln_b': f(inp['ln_b'][li]),
        'mlp_w1': f(inp['mlp_w1'][li]), 'mlp_w2': f(inp['mlp_w2'][li]),
    }
    percore = {}
    if kind == 0:
        tab, idt = rope_tables(128), ident_table(256, 128)
        m.update({'wq': f(inp['gqa_wq'][slot]), 'wk': f(inp['gqa_wk'][slot]), 'wv': f(inp['gqa_wv'][slot]), 'wo': f(inp['gqa_wo'][slot]),
                  'q_g': f(inp['gqa_q_g'][slot]).reshape(1, 128), 'k_g': f(inp['gqa_k_g'][slot]).reshape(1, 128),
                  'ropeF': full_order(idt, tab)})
        percore['ropeL'] = [local_order(idt, tab, h) for h in range(2)]
    elif kind == 1:
        pp2 = lambda v: np.ascontiguousarray(np.stack([per_part(f(v[0]), 16), per_part(f(v[1]), 16)], 2))
        m.update({'wx': f(inp['lru_wx'][slot]), 'wy': f(inp['lru_wy'][slot]), 'wo': f(inp['lru_wo'][slot]),
                  'ra_w': f(inp['lru_ra_w'][slot]), 'ix_w': f(inp['lru_ix_w'][slot]),
                  'convwT': np.ascontiguousarray(np.stack([per_part(f(inp['lru_conv_w'][slot][j]), 16) for j in range(4)], 2)),
                  'convbT': per_part(f(inp['lru_conv_b'][slot]), 16),
                  'rabT': pp2(inp['lru_ra_b'][slot]), 'ixbT': pp2(inp['lru_ix_b'][slot]), 'lamT': pp2(inp['lru_lam'][slot])})
        hm = []
        for h in range(2):
            a = np.zeros((128, 2), np.float32)
            a[:, h] = 1.0
            hm.append(a)
        percore['hmask'] = hm
    else:
        tab, idt = rope_tables(64), ident_table(256, 64)
        m.update({'wq_a': f(inp['mla_wq_a'][slot]), 'wq_b': f(inp['mla_wq_b'][slot]), 'wkv_a': f(inp['mla_wkv_a'][slot]),
                  'wkv_b': f(inp['mla_wkv_b'][slot]), 'wo': f(inp['mla_wo'][slot]),
                  'q_a_g': f(inp['mla_q_a_g'][slot]).reshape(1, 512), 'kv_a_g': f(inp['mla_kv_a_g'][slot]).reshape(1, 512),
                  'ropeF': full_order(idt, tab)})
        percore['ropeL'] = [local_order(idt, tab, h) for h in range(2)]
    return m, percore


def kernel(**inp):
    x = np.asarray(inp['x'], np.float32)
    ctx = np.asarray(inp['ctx'], np.float32)
    layers = tuple(_DBG.get('layers', range(DEPTH)))
    lat = [x[b] for b in range(4)]
    cx = [ctx[b] for b in range(4)]
    if 'init' in _DBG:
        lat, cx = [np.asarray(a) for a in _DBG['init'][0]], [np.asarray(a) for a in _DBG['init'][1]]
    if layers not in _PROG_CACHE:
        _PROG_CACHE[layers] = build_program(layers)[0]
    nc = _PROG_CACHE[layers]
    common = {}
    percore = {}
    for li in layers:
        m, pc = layer_host_inputs(inp, li)
        for k, v in m.items():
            common["L%d_%s" % (li, k)] = v
        for k, v in pc.items():
            percore["L%d_%s" % (li, k)] = v
    in_maps = []
    for core in range(NCORES):
        b, h = core // 2, core % 2
        m = dict(common)
        for k, v in percore.items():
            m[k] = v[h]
        cpair = np.stack([np.asarray(inp['c'][b], np.float32), np.asarray(inp['c_ctx'], np.float32)], 1)
        cT = np.ascontiguousarray(cpair.reshape(16, 128, 2).transpose(1, 0, 2).reshape(128, 32))
        for li in layers:
            m["L%d_cT" % li] = cT
        m['xfull'] = full_order(cx[b], lat[b])
        m['xloc'] = local_order(cx[b], lat[b], h)
        in_maps.append(m)
    ncr = _DBG.get('ncores', NCORES)
    res = run_bass_kernel_spmd(nc, in_maps[:ncr], core_ids=list(range(ncr)))
    final = layers[-1] == DEPTH - 1
    for b in range(ncr // 2):
        o0 = res.results[2 * b]['out']
        o1 = res.results[2 * b + 1]['out']
        if final:
            lat[b] = np.concatenate([o0, o1], 0)
        else:
            lat[b] = np.concatenate([o0[128:], o1[128:]], 0)
            cx[b] = np.concatenate([o0[:128], o1[:128]], 0)
    _DBG['lat'], _DBG['cx'] = lat, cx
    return np.stack(lat, 0).astype(np.float32)
```

```python
import contextlib
import math
import numpy as np
import ml_dtypes
import concourse.bass as bass
import concourse.mybir as mybir
from concourse.bass_utils import run_bass_kernel_spmd

F32 = mybir.dt.float32
BF16 = mybir.dt.bfloat16
AF = mybir.ActivationFunctionType
ALU = mybir.AluOpType
AX = mybir.AxisListType

D = 2048
DFF = 8192
NTF = 34
NTL = 17
CTXF = (0, 17)
TIME_ORDER = [0, 17] + list(range(1, 17)) + list(range(18, 34))
EPS = 1e-6
DEPTH = 4
ALPHA = (2 * DEPTH) ** 0.25
NCORES = 8


class Sched:
    ENGS = ('pe', 'act', 'dve', 'pool', 'sp')
    NDMA = 48

    def __init__(self, nc):
        self.nc = nc
        self.es = contextlib.ExitStack()
        self.eng = {'pe': nc.tensor, 'act': nc.scalar, 'dve': nc.vector, 'pool': nc.gpsimd, 'sp': nc.sync}
        self.sem = {e: self.es.enter_context(nc.semaphore("s_" + e)) for e in self.ENGS if e != 'sp'}
        self.dsem = [self.es.enter_context(nc.semaphore("d%d" % i)) for i in range(self.NDMA)]
        self.seq = {e: 0 for e in self.ENGS}
        self.waited = {e: {} for e in self.ENGS}
        self.buf = {}
        self.dma_i = 0
        self.nwait = 0
        self.nops = 0
        self.last_dma = {}
        self.stack = [self.es]
        self.uid = 0
        self.ns = 0
        self.NCC = 4
        for i in range(self.NCC):
            self.dsem.append(self.es.enter_context(nc.semaphore("cc%d" % i)))
        self.cc_i = 0
        self.NBG = 24
        for i in range(self.NBG):
            self.dsem.append(self.es.enter_context(nc.semaphore("bg%d" % i)))
        self.bg_i = 0
        self.pool_dma_out = {}
        self.cc_out = {}
        self.limit = 1 << 60
        self.marks = []
        self.any_dma = {}

    def sb(self, name, shape, dt):
        self.uid += 1
        return self.stack[-1].enter_context(self.nc.sbuf_tensor("%s_%d" % (name, self.uid), shape, dt))

    def ps(self, name, shape, dt=F32):
        self.uid += 1
        return self.stack[-1].enter_context(self.nc.psum_tensor("%s_%d" % (name, self.uid), shape, dt))

    @contextlib.contextmanager
    def phase(self):
        es = contextlib.ExitStack()
        self.marks.append((self.ns, self.nops))
        self.stack.append(es)
        try:
            yield
        finally:
            self.barrier()
            self.stack.pop()
            es.close()

    def _key(self, b):
        if isinstance(b, tuple):
            return b if b[0] == 'abs' else (self.ns, b)
        if isinstance(b, str):
            return (self.ns, b)
        return b.name

    def _deps(self, eng, reads, writes, is_dma):
        need = {}

        def add(ref, kind):
            if ref[0] == 'e':
                _, E, s = ref
                if E == eng and not is_dma and (eng == 'pe' or kind != 'raw'):
                    return
                k = ('e', E)
                v = s
            else:
                k = ('d', ref[1])
                v = ref[2]
            if need.get(k, 0) < v:
                need[k] = v

        for r in reads:
            st = self.buf.get(self._key(r))
            if st and st[0] is not None:
                add(st[0], 'raw')
        for w in writes:
            st = self.buf.get(self._key(w))
            if st:
                if st[0] is not None:
                    add(st[0], 'waw')
                for rr in st[1]:
                    add(rr, 'war')
        return need

    def _emit_waits(self, eng, need):
        e = self.eng[eng]
        wd = self.waited[eng]
        for k, v in need.items():
            if wd.get(k, 0) >= v:
                continue
            sem = self.sem[k[1]] if k[0] == 'e' else self.dsem[k[1]]
            e.wait_ge(sem, v)
            wd[k] = v
            self.nwait += 1

    def _update(self, ref, reads, writes):
        for r in reads:
            st = self.buf.setdefault(self._key(r), [None, []])
            st[1].append(ref)
            if len(st[1]) > 64:
                best = {}
                for x in st[1]:
                    k = (x[0], x[1])
                    if k not in best or best[k][2] < x[2]:
                        best[k] = x
                st[1] = list(best.values())
        for w in writes:
            st = self.buf.setdefault(self._key(w), [None, []])
            st[0] = ref
            st[1] = []

    def op(self, eng, fn, reads=(), writes=()):
        if self.nops >= self.limit:
            return None
        need = self._deps(eng, reads, writes, False)
        self._emit_waits(eng, need)
        ins = fn(self.eng[eng])
        self.seq[eng] += 1
        ins.then_inc(self.sem[eng], 1)
        self._update(('e', eng, self.seq[eng]), reads, writes)
        self.nops += 1
        return ins

    def dma(self, q, out, in_, reads=(), writes=(), bg=False, **kw):
        if self.nops >= self.limit:
            return None
        need = self._deps(q, reads, writes, True)
        if bg:
            i = self.bg_i
            self.bg_i += 1
            idx = self.NDMA + self.NCC + i % self.NBG
            val = 16 * (i // self.NBG + 1)
        else:
            i = self.dma_i
            self.dma_i += 1
            idx = i % self.NDMA
            val = 16 * (i // self.NDMA + 1)
        if val > 16:
            k = ('d', idx)
            need[k] = max(need.get(k, 0), val - 16)
        if q == 'pool':
            for k, v in self.cc_out.items():
                need[k] = max(need.get(k, 0), v)
            self.pool_dma_out[('d', idx)] = val
        self._emit_waits(q, need)
        ins = self.eng[q].dma_start(out=out, in_=in_, **kw)
        ins.then_inc(self.dsem[idx], 16)
        ref = ('d', idx, val)
        self._update(ref, reads, writes)
        self.any_dma[idx] = val
        if bg:
            self.last_dma.pop(idx, None)
        else:
            self.last_dma[idx] = val
        self.nops += 1
        return ins

    def collective(self, kind, ins, outs, groups, reads=(), writes=()):
        need = self._deps('pool', reads, writes, True)
        i = self.cc_i
        self.cc_i += 1
        idx = self.NDMA + i % self.NCC
        val = i // self.NCC + 1
        if val > 1:
            k = ('d', idx)
            need[k] = max(need.get(k, 0), val - 1)
        for k, v in self.pool_dma_out.items():
            need[k] = max(need.get(k, 0), v)
        self.cc_out[('d', idx)] = val
        self._emit_waits('pool', need)
        ins_ = self.nc.gpsimd.collective_compute(kind, ALU.bypass, replica_groups=groups, ins=ins, outs=outs)
        ins_.then_inc(self.dsem[idx])
        ref = ('d', idx, val)
        self._update(ref, reads, writes)
        self.last_dma[idx] = val
        self.nops += 1
        return ins_

    def barrier(self):
        need = {}
        for e in self.ENGS:
            if e != 'sp' and self.seq[e] > 0:
                need[('e', e)] = self.seq[e]
        for idx, val in self.last_dma.items():
            need[('d', idx)] = val
        for e in self.ENGS:
            self._emit_waits(e, dict(need))

    def finish(self):
        self.last_dma.update(self.any_dma)
        self.barrier()
        self.es.close()


class Ring:
    def __init__(self, S, name, n, shape, dt, psum=False):
        self.bufs = [(S.ps if psum else S.sb)("%s%d" % (name, i), shape, dt) for i in range(n)]
        self.i = 0

    def next(self):
        b = self.bufs[self.i % len(self.bufs)]
        self.i += 1
        return b


class LayerProg:
    def __init__(self, S, nc, li, kind, last, io, consts):
        self.S, self.nc, self.li, self.kind, self.last = S, nc, li, kind, last
        self.io = io
        self.c = consts
        self.ltiles = list(range(1, NTL)) if last else list(range(NTL))
        self.nscr = 0
        self.wkeys = {}

    def dram(self, name, shape, dt):
        self.nscr += 1
        return self.nc.dram_tensor("L%d_%s_%d" % (self.li, name, self.nscr), shape, dt).ap()

    def cast2d(self, name, src, piece_cols=2048):
        shape = list(src.shape)
        dst = self.dram(name, shape, BF16)
        nd = len(shape)
        names = " ".join("d%d" % i for i in range(nd))
        pat = "%s -> (%s)" % (names, names)
        fs = src.rearrange(pat).rearrange("(r c) -> r c", c=16384)
        fd = dst.rearrange(pat).rearrange("(r c) -> r c", c=16384)
        R = fs.shape[0]
        keys = []
        for r0 in range(0, R, 128):
            r1 = min(R, r0 + 128)
            k = (name, 'w', r0)
            self.S.dma('pool', fd[r0:r1, :], fs[r0:r1, :], writes=[k], bg=True)
            keys.append(k)
        self.wkeys[name] = keys
        return dst

    def load_w(self, wt, wb, key):
        kc = wt.shape[1]
        src = wb.rearrange("(kc p) n -> p kc n", p=128)
        step = max(1, kc // 4)
        for k0 in range(0, kc, step):
            self.S.dma('sp', wt[:, k0:k0 + step, :], src[:, k0:k0 + step, :], reads=self.wkeys[key], writes=[(wt.name, k0 // step)])

    @staticmethod
    def wk(wt, kc):
        step = max(1, wt.shape[1] // 4)
        return (wt.name, kc // step)

    def phase_mod(self):
        S, io = self.S, self.io
        mod = S.sb("mod", [128, 96, 2], F32)
        self.mod = mod
        with S.phase():
            cT = S.sb("cT", [128, 16, 2], F32)
            scT = S.sb("scT", [128, 16, 2], F32)
            adab = S.sb("adab", [128, 96], F32)
            S.dma('sp', cT[:], io['cT'].rearrange("p (k s) -> p k s", s=2), writes=[cT])
            S.dma('sp', adab[:], io['adabT'], writes=[adab])
            S.op('act', lambda e: e.activation(scT[:], cT[:], AF.Silu), reads=[cT], writes=[scT])
            psM = S.ps("psM", [128, 96, 2], F32)
            ring = Ring(S, "wa", 2, [128, 16, 512], F32)
            src = io['ada_w'].rearrange("(kc p) n -> p kc n", p=128)
            for ng in range(24):
                wa = ring.next()
                for k0 in range(0, 16, 4):
                    S.dma('sp', wa[:, k0:k0 + 4, :], src[:, k0:k0 + 4, ng * 512:(ng + 1) * 512], writes=[(wa.name, k0 // 4)])
                for nb in range(4):
                    q = ng * 4 + nb
                    for kc in range(16):
                        S.op('pe', lambda e, q=q, kc=kc, nb=nb, wa=wa: e.matmul(
                            psM[:, q, :], wa[:, kc, nb * 128:(nb + 1) * 128], scT[:, kc, :],
                            start=(kc == 0), stop=(kc == 15)), reads=[(wa.name, kc // 4), scT], writes=[psM])
            S.op('dve', lambda e: e.tensor_tensor(mod[:], psM[:], adab[:, :, None].to_broadcast([128, 96, 2]), ALU.add),
                 reads=[psM, adab], writes=[mod])
            for j in (1, 2, 4, 5):
                S.op('dve', lambda e, j=j: e.tensor_scalar_add(mod[:, j * 16:(j + 1) * 16, :], mod[:, j * 16:(j + 1) * 16, :], 1.0),
                     reads=[mod], writes=[mod])

    def mcol(self, j, c, s):
        return self.mod[:, j * 16 + c, s:s + 1]

    def bcast_rows(self, dst, j, s, psB_ring, diag_ring):
        S = self.S
        for g in range(4):
            psB = psB_ring.next()
            for cc in range(4):
                c = g * 4 + cc
                dg = diag_ring.next()
                S.op('dve', lambda e, dg=dg, c=c: e.tensor_scalar_mul(dg[:], self.c['ident_f'][:], self.mcol(j, c, s)),
                     reads=[self.c['ident_f'], self.mod], writes=[dg])
                S.op('pe', lambda e, dg=dg, cc=cc, psB=psB: e.matmul(psB[:, cc * 128:(cc + 1) * 128], self.c['ones_f'][:], dg[:],
                                                                      start=True, stop=True),
                     reads=[dg, self.c['ones_f']], writes=[psB])
            S.op('act', lambda e, g=g, psB=psB: e.activation(dst[:, g * 512:(g + 1) * 512], psB[:], AF.Copy),
                 reads=[psB], writes=[dst])

    def make_hT_tools(self, nx=2):
        S = self.S
        return dict(xt=Ring(S, "hx", nx, [128, D], F32), ps=Ring(S, "hps", 2, [128, 512], F32, psum=True))

    def load_hT(self, tools, src_tile, s, jshift, jscale, dst, col0=None, src_key=None):
        S = self.S
        xt = tools['xt'].next()
        S.dma('sp', xt[:], src_tile, reads=[src_key] if src_key else [], writes=[xt])
        for g in range(4):
            ps = tools['ps'].next()
            for cc in range(4):
                c = g * 4 + cc
                S.op('pe', lambda e, c=c, cc=cc, ps=ps, xt=xt: e.transpose(ps[:, cc * 128:(cc + 1) * 128], xt[:, c * 128:(c + 1) * 128],
                                                                        self.c['ident_f'][:]),
                     reads=[xt, self.c['ident_f']], writes=[ps])
            for cc in range(4):
                c = g * 4 + cc
                o = dst[:, c, :] if col0 is None else dst[:, c, col0:col0 + 128]
                S.op('act', lambda e, o=o, cc=cc, ps=ps, c=c: e.activation(o, ps[:, cc * 128:(cc + 1) * 128], AF.Identity,
                                                                         bias=self.mcol(jshift, c, s), scale=self.mcol(jscale, c, s)),
                     reads=[ps, self.mod], writes=[dst])

    def rope(self, xin, out_bf, tb, H, rd, t1, t2):
        S = self.S
        hw = rd // 4
        cosb = tb[:, 0:rd][:, None, :].to_broadcast([128, H, rd])
        S.op('dve', lambda e: e.tensor_tensor(t1, xin, cosb, ALU.mult), reads=[xin, tb], writes=[t1])
        xv = xin.rearrange("p h (a f w) -> p h a f w", a=2, f=2)
        tv = t2.rearrange("p h (a f w) -> p h a f w", a=2, f=2)
        sv = tb[:, rd:2 * rd].rearrange("p (a f w) -> p a f w", a=2, f=2)
        for a in range(2):
            for f in range(2):
                S.op('pool', lambda e, a=a, f=f: e.tensor_tensor(tv[:, :, a, f, :], xv[:, :, a, 1 - f, :],
                                                                 sv[:, a, f, :][:, None, :].to_broadcast([128, H, hw]), ALU.mult),
                     reads=[xin, tb], writes=[t2])
        S.op('dve', lambda e: e.tensor_tensor(out_bf, t1, t2, ALU.add), reads=[t1, t2], writes=[out_bf])

    def rstd_from_ss(self, ss, rstd, n, tmp):
        S = self.S
        S.op('act', lambda e: e.activation(tmp, ss, AF.Sqrt, bias=EPS, scale=1.0 / n), reads=[ss], writes=[tmp])
        S.op('dve', lambda e: e.reciprocal(rstd, tmp), reads=[tmp], writes=[rstd])

    def epilogue(self, yg, xt, lnG, lnB, out_ap, out_key, small):
        S = self.S
        st, mv, sd, rstd, nmr = small
        S.op('dve', lambda e: e.scalar_tensor_tensor(yg[:], xt[:], float(ALPHA), yg[:], ALU.mult, ALU.add),
             reads=[xt, yg], writes=[yg])
        for c in range(4):
            S.op('dve', lambda e, c=c: e.bn_stats(st[:, c, :], yg[:, c * 512:(c + 1) * 512]), reads=[yg], writes=[st])
        S.op('dve', lambda e: e.bn_aggr(mv[:], st[:]), reads=[st], writes=[mv])
        S.op('act', lambda e: e.activation(sd[:], mv[:, 1:2], AF.Sqrt, bias=EPS, scale=1.0), reads=[mv], writes=[sd])
        S.op('dve', lambda e: e.reciprocal(rstd[:], sd[:]), reads=[sd], writes=[rstd])
        S.op('dve', lambda e: e.scalar_tensor_tensor(nmr[:], mv[:, 0:1], -1.0, rstd[:], ALU.mult, ALU.mult),
             reads=[mv, rstd], writes=[nmr])
        S.op('act', lambda e: e.activation(yg[:], yg[:], AF.Identity, bias=nmr[:], scale=rstd[:]),
             reads=[yg, nmr, rstd], writes=[yg])
        S.op('dve', lambda e: e.tensor_tensor(yg[:], yg[:], lnG[:], ALU.mult), reads=[yg, lnG], writes=[yg])
        S.op('pool', lambda e: e.tensor_tensor(yg[:], yg[:], lnB[:], ALU.add), reads=[yg, lnB], writes=[yg])
        S.dma('sp', out_ap, yg[:], reads=[yg], writes=[out_key])

    def epi_small(self):
        S = self.S
        return (S.sb("st", [128, 4, 6], F32), S.sb("mv", [128, 2], F32), S.sb("sd", [128, 1], F32),
                S.sb("rstd", [128, 1], F32), S.sb("nmr", [128, 1], F32))

    def phase_out(self, oT_d, wo_b, wo_key, xloc, xmid):
        S = self.S
        with S.phase():
            Wo = S.sb("Wo", [128, 16, D], BF16)
            self.load_w(Wo, wo_b, wo_key)
            lnG = S.sb("lnG", [128, D], F32)
            lnB = S.sb("lnB", [128, D], F32)
            S.dma('sp', lnG[:], self.io['ln_g'][0:1, :].partition_broadcast(128), writes=[lnG])
            S.dma('sp', lnB[:], self.io['ln_b'][0:1, :].partition_broadcast(128), writes=[lnB])
            gB = [S.sb("gB0", [128, D], F32), S.sb("gB1", [128, D], F32)]
            psB_ring = Ring(S, "psB", 1, [128, 512], F32, psum=True)
            diag_ring = Ring(S, "diag", 2, [128, 128], F32)
            self.bcast_rows(gB[0], 2, 0, psB_ring, diag_ring)
            if not self.last:
                self.bcast_rows(gB[1], 2, 1, psB_ring, diag_ring)
            oring = Ring(S, "oT", 2, [128, 16, 128], BF16)
            xring = Ring(S, "xr", 2, [128, D], F32)
            yring = Ring(S, "yg", 2, [128, D], F32)
            psY = Ring(S, "psY", 1, [128, D], F32, psum=True)
            small = self.epi_small()
            for lt in self.ltiles:
                s = 1 if lt == 0 else 0
                oT = oring.next()
                S.dma('sp', oT[:], oT_d[:, :, lt * 128:(lt + 1) * 128].rearrange("h p t -> p h t"), reads=['oT_d'], writes=[oT])
                xt = xring.next()
                S.dma('sp', xt[:], xloc(lt), reads=[self.xl_key(lt)], writes=[xt])
                ps = psY.next()
                for ng in range(4):
                    for hd in range(16):
                        S.op('pe', lambda e, ng=ng, hd=hd, oT=oT, ps=ps: e.matmul(ps[:, ng * 512:(ng + 1) * 512], oT[:, hd, :],
                                                                                 Wo[:, hd, ng * 512:(ng + 1) * 512],
                                                                                 start=(hd == 0), stop=(hd == 15)),
                             reads=[oT, self.wk(Wo, hd)], writes=[ps])
                yg = yring.next()
                S.op('dve', lambda e, yg=yg, ps=ps, s=s: e.tensor_tensor(yg[:], ps[:], gB[s][:], ALU.mult),
                     reads=[ps, gB[s]], writes=[yg])
                self.epilogue(yg, xt, lnG, lnB, xmid[lt * 128:(lt + 1) * 128, :], ('xmid', lt), small)

    def phase_mlp(self, xmid, xout):
        S, io = self.S, self.io
        w1v = self.w1b.rearrange("(kc p) f -> p kc f", p=128)
        w2v = self.w2b.rearrange("(fc p) d -> p fc d", p=128)
        yd = self.dram("yd", [NTL * 128, D], F32)
        with S.phase():
            lnG = S.sb("lnG", [128, D], F32)
            lnB = S.sb("lnB", [128, D], F32)
            S.dma('sp', lnG[:], io['ln_g'][1:2, :].partition_broadcast(128), writes=[lnG])
            S.dma('sp', lnB[:], io['ln_b'][1:2, :].partition_broadcast(128), writes=[lnB])
            gB = S.sb("gB", [128, D], F32)
            psA_ring = Ring(S, "psA", 2, [128, 512], F32, psum=True)
            psO = [S.ps("psO%d" % i, [128, 512], F32) for i in range(4)]
            diag_ring = Ring(S, "diag", 2, [128, 128], F32)
            tools = self.make_hT_tools(nx=1)
            hT2 = S.sb("hT2", [128, 16, 512], BF16)
            aT = S.sb("aT", [128, 64, 512], BF16)
            w1r = Ring(S, "w1p", 2, [128, 16, 256], BF16)
            w2r = Ring(S, "w2p", 2, [128, 8, 512], BF16)
            sqr = Ring(S, "sq", 2, [128, 512], F32)
            ysr = Ring(S, "ys", 2, [128, 512], F32)
            xt_e = S.sb("xte", [128, D], F32)
            yg_e = S.sb("yge", [128, D], F32)
            small = self.epi_small()
            blocks = []
            if not self.last:
                blocks.append(([0], 1))
            for b in range(4):
                blocks.append((list(range(1 + 4 * b, 5 + 4 * b)), 0))
            cur_s = None
            for tiles, s in blocks:
                T = 128 * len(tiles)
                if cur_s != s:
                    self.bcast_rows(gB, 5, s, psA_ring, diag_ring)
                    cur_s = s
                for ti, lt in enumerate(tiles):
                    self.load_hT(tools, xmid[lt * 128:(lt + 1) * 128, :], s, 3, 4, hT2, col0=ti * 128, src_key=('xmid', lt))
                for fp in range(32):
                    w1p = w1r.next()
                    S.dma('sp', w1p[:], w1v[:, :, fp * 256:(fp + 1) * 256], reads=self.wkeys['w1b'], writes=[w1p])
                    for fcl in range(2):
                        fc = fp * 2 + fcl
                        ps = psA_ring.next()
                        for kc in range(16):
                            S.op('pe', lambda e, kc=kc, fcl=fcl, ps=ps, w1p=w1p: e.matmul(
                                ps[:, :T], w1p[:, kc, fcl * 128:(fcl + 1) * 128], hT2[:, kc, :T],
                                start=(kc == 0), stop=(kc == 15)), reads=[w1p, hT2], writes=[ps])
                        sq = sqr.next()
                        S.op('act', lambda e, sq=sq, ps=ps: e.activation(sq[:, :T], ps[:, :T], AF.Square), reads=[ps], writes=[sq])
                        S.op('dve', lambda e, sq=sq, ps=ps, fc=fc: e.scalar_tensor_tensor(aT[:, fc, :T], ps[:, :T], 0.0, sq[:, :T],
                                                                                        ALU.is_gt, ALU.mult),
                             reads=[ps, sq], writes=[aT])
                for dg in range(4):
                    for pc in range(8):
                        w2p = w2r.next()
                        S.dma('sp', w2p[:], w2v[:, pc * 8:(pc + 1) * 8, dg * 512:(dg + 1) * 512], reads=self.wkeys['w2b'], writes=[w2p])
                        for ti in range(len(tiles)):
                            for fcl in range(8):
                                fc = pc * 8 + fcl
                                S.op('pe', lambda e, ti=ti, fc=fc, fcl=fcl, w2p=w2p: e.matmul(
                                    psO[ti][:], aT[:, fc, ti * 128:(ti + 1) * 128], w2p[:, fcl, :],
                                    start=(fc == 0), stop=(fc == 63)), reads=[aT, w2p], writes=[psO[ti]])
                    for ti, lt in enumerate(tiles):
                        ys = ysr.next()
                        S.op('dve', lambda e, ys=ys, ti=ti, dg=dg: e.tensor_tensor(ys[:], psO[ti][:], gB[:, dg * 512:(dg + 1) * 512], ALU.mult),
                             reads=[psO[ti], gB], writes=[ys])
                        S.dma('sp', yd[lt * 128:(lt + 1) * 128, dg * 512:(dg + 1) * 512], ys[:], reads=[ys], writes=[('yd', lt)])
                for lt in tiles:
                    S.dma('sp', xt_e[:], xmid[lt * 128:(lt + 1) * 128, :], reads=[('xmid', lt)], writes=[xt_e])
                    S.dma('sp', yg_e[:], yd[lt * 128:(lt + 1) * 128, :], reads=[('yd', lt)], writes=[yg_e])
                    self.epilogue(yg_e, xt_e, lnG, lnB, xout(lt), self.xo_key(lt), small)
                    if self.on_tile_done is not None:
                        self.on_tile_done(lt)

    def attention(self, blocks, nheads, score_fn, v_fn, load_q_fn, oT_d, scale, pre_head=None, head_outer=False):
        S = self.S
        pT_ring = Ring(S, "pT", 4, [128, 512], BF16)
        psS = Ring(S, "psS", 2, [128, 512], F32, psum=True)
        psOr = Ring(S, "psOa", 2, [128, 512], F32, psum=True)
        psSum = Ring(S, "psSum", 2, [128, 512], F32, psum=True)
        rec_ring = Ring(S, "rec", 2, [128, 512], F32)
        o_ring = Ring(S, "oblk", 2, [128, 512], BF16)
        accd_ring = Ring(S, "accd", 2, [128, 512], F32)
        accp_ring = Ring(S, "accp", 2, [128, 512], F32)
        ones_f = self.c['ones_f']
        order = []
        if head_outer:
            for hd in range(nheads):
                for bi in range(len(blocks)):
                    order.append((hd, bi))
        else:
            for bi in range(len(blocks)):
                for hd in range(nheads):
                    order.append((hd, bi))
        prev_hd, prev_bi = None, None
        for hd, bi in order:
            tiles, ktiles = blocks[bi]
            T = 128 * len(tiles)
            if head_outer and hd != prev_hd and pre_head is not None:
                pre_head(hd)
            qb = load_q_fn(hd, bi, tiles, new_block=(bi != prev_bi), new_head=(hd != prev_hd))
            prev_hd, prev_bi = hd, bi
            pso = psOr.next()
            pss = psSum.next()
            nk = len(ktiles)
            accs = (('dve', accd_ring.next()), ('pool', accp_ring.next()))
            for ki, kt in enumerate(ktiles):
                ps = psS.next()
                sreads = score_fn(ps, hd, kt, qb, T)
                pT = pT_ring.next()
                S.op('act', lambda e, pT=pT, ps=ps: e.activation(pT[:, :T], ps[:, :T], AF.Exp, scale=float(scale)),
                     reads=[ps], writes=[pT])
                vap, vkeys = v_fn(hd, kt)
                S.op('pe', lambda e, pT=pT, vap=vap, pso=pso, ki=ki: e.matmul(pso[:, :T], vap, pT[:, :T], start=(ki == 0), stop=(ki == nk - 1)),
                     reads=[pT] + vkeys, writes=[pso])
                aeng, acc = accs[ki % 2]
                if ki < 2:
                    S.op(aeng, lambda e, acc=acc, pT=pT: e.tensor_copy(acc[:, :T], pT[:, :T]), reads=[pT], writes=[acc])
                else:
                    S.op(aeng, lambda e, acc=acc, pT=pT: e.tensor_tensor(acc[:, :T], acc[:, :T], pT[:, :T], ALU.add), reads=[pT, acc], writes=[acc])
            na = min(nk, 2)
            for ai in range(na):
                acc = accs[ai][1]
                S.op('pe', lambda e, acc=acc, pss=pss, ai=ai: e.matmul(pss[:, :T], ones_f[:], acc[:, :T], start=(ai == 0), stop=(ai == na - 1)),
                     reads=[acc, ones_f], writes=[pss])
            rec = rec_ring.next()
            S.op('dve', lambda e, rec=rec, pss=pss: e.reciprocal(rec[:, :T], pss[:, :T]), reads=[pss], writes=[rec])
            ob = o_ring.next()
            S.op('dve', lambda e, ob=ob, pso=pso, rec=rec: e.tensor_tensor(ob[:, :T], pso[:, :T], rec[:, :T], ALU.mult),
                 reads=[pso, rec], writes=[ob])
            t0 = tiles[0] * 128
            S.dma('sp', oT_d[hd, :, t0:t0 + T], ob[:, :T], reads=[ob], writes=['oT_d'])

    def mixer_gqa(self, xfull, xloc, oT_d):
        S, io = self.S, self.io
        wq_b = self.wb['wq']
        qT_d = self.dram("qT", [16, 128, NTL * 128], BF16)
        with S.phase():
            kT_all = S.sb("kT_all", [128, 4, NTF * 128], BF16)
            V_all = S.sb("V_all", [128, NTF, 512], BF16)
            with S.phase():
                Wkv = S.sb("Wkv", [128, 16, 1024], BF16)
                for nm, c0 in (('wk', 0), ('wv', 512)):
                    srcw = self.wb[nm].rearrange("(kc p) n -> p kc n", p=128)
                    for k0 in range(0, 16, 4):
                        S.dma('sp', Wkv[:, k0:k0 + 4, c0:c0 + 512], srcw[:, k0:k0 + 4, :], reads=self.wkeys[nm], writes=[(Wkv.name, k0 // 4, c0)])
                kgB = S.sb("kgB", [128, 128], F32)
                S.dma('sp', kgB[:], io['k_g'][0:1, :].partition_broadcast(128), writes=[kgB])
                tools = self.make_hT_tools()
                hring = Ring(S, "hT", 2, [128, 16, 128], BF16)
                psKV = Ring(S, "psKV", 1, [128, 1024], F32, psum=True)
                psKT = Ring(S, "psKT", 1, [128, 4, 128], BF16, psum=True)
                tbr = Ring(S, "tb", 2, [128, 256], F32)
                sqk = S.sb("sqk", [128, 512], F32)
                ss = S.sb("ss", [128, 4], F32); tmp4 = S.sb("tmp4", [128, 4], F32); rstd = S.sb("rstd4", [128, 4], F32)
                kn = S.sb("kn", [128, 4, 128], F32); t1 = S.sb("t1", [128, 4, 128], F32); t2 = S.sb("t2", [128, 4, 128], F32)
                kr_ring = Ring(S, "kr", 2, [128, 4, 128], BF16)
                for t in range(NTF):
                    s = 1 if t in CTXF else 0
                    hT = hring.next()
                    self.load_hT(tools, xfull(t), s, 0, 1, hT, src_key=self.xf_key(t))
                    ps = psKV.next()
                    for half in range(2):
                        for kc in range(16):
                            S.op('pe', lambda e, half=half, kc=kc, hT=hT, ps=ps: e.matmul(
                                ps[:, half * 512:(half + 1) * 512], hT[:, kc, :], Wkv[:, kc, half * 512:(half + 1) * 512],
                                start=(kc == 0), stop=(kc == 15)), reads=[hT, (Wkv.name, kc // 4, half * 512)], writes=[ps])
                    S.op('act', lambda e, t=t, ps=ps: e.activation(V_all[:, t, :], ps[:, 512:1024], AF.Copy), reads=[ps], writes=[V_all])
                    S.op('act', lambda e, ps=ps: e.activation(sqk[:], ps[:, 0:512], AF.Square), reads=[ps], writes=[sqk])
                    S.op('dve', lambda e: e.reduce_sum(ss[:], sqk[:].rearrange("p (h d) -> p h d", h=4), AX.X), reads=[sqk], writes=[ss])
                    self.rstd_from_ss(ss[:], rstd[:], 128, tmp4[:])
                    S.op('dve', lambda e, ps=ps: e.tensor_tensor(kn[:], ps[:, 0:512].rearrange("p (h d) -> p h d", h=4),
                                                                rstd[:, :, None].to_broadcast([128, 4, 128]), ALU.mult),
                         reads=[ps, rstd], writes=[kn])
                    S.op('dve', lambda e: e.tensor_tensor(kn[:], kn[:], kgB[:, None, :].to_broadcast([128, 4, 128]), ALU.mult),
                         reads=[kn, kgB], writes=[kn])
                    tb = tbr.next()
                    S.dma('sp', tb[:], io['ropeF'][t * 128:(t + 1) * 128, :], writes=[tb])
                    kr = kr_ring.next()
                    self.rope(kn[:], kr[:], tb, 4, 128, t1[:], t2[:])
                    pk = psKT.next()
                    for g in range(4):
                        S.op('pe', lambda e, g=g, kr=kr, pk=pk: e.transpose(pk[:, g, :], kr[:, g, :], self.c['ident_b'][:]),
                             reads=[kr, self.c['ident_b']], writes=[pk])
                    S.op('act', lambda e, t=t, pk=pk: e.activation(kT_all[:, :, t * 128:(t + 1) * 128], pk[:], AF.Copy), reads=[pk], writes=[kT_all])
            with S.phase():
                Wq = S.sb("Wq", [128, 16, D], BF16)
                self.load_w(Wq, wq_b, 'wq')
                qgB = S.sb("qgB", [128, 128], F32)
                S.dma('sp', qgB[:], io['q_g'][0:1, :].partition_broadcast(128), writes=[qgB])
                tools = self.make_hT_tools(nx=1)
                hring = Ring(S, "hT", 2, [128, 16, 128], BF16)
                psQ = Ring(S, "psQ", 1, [128, D], F32, psum=True)
                psQT = Ring(S, "psQT", 1, [128, 16, 128], BF16, psum=True)
                tbr = Ring(S, "tb", 2, [128, 256], F32)
                sq = S.sb("sq", [128, D], F32)
                ss = S.sb("ss", [128, 16], F32); tmp16 = S.sb("tmp16", [128, 16], F32); rstd = S.sb("rstd16", [128, 16], F32)
                qn = S.sb("qn", [128, 16, 128], F32); t1 = S.sb("t1", [128, 16, 128], F32)
                qr_ring = Ring(S, "qr", 2, [128, 16, 128], BF16)
                qo_ring = Ring(S, "qo", 2, [128, 16, 128], BF16)
                for lt in self.ltiles:
                    s = 1 if lt == 0 else 0
                    hT = hring.next()
                    self.load_hT(tools, xloc(lt), s, 0, 1, hT, src_key=self.xl_key(lt))
                    ps = psQ.next()
                    for ng in range(4):
                        for kc in range(16):
                            S.op('pe', lambda e, ng=ng, kc=kc, hT=hT, ps=ps: e.matmul(
                                ps[:, ng * 512:(ng + 1) * 512], hT[:, kc, :], Wq[:, kc, ng * 512:(ng + 1) * 512],
                                start=(kc == 0), stop=(kc == 15)), reads=[hT, self.wk(Wq, kc)], writes=[ps])
                    S.op('act', lambda e, ps=ps: e.activation(sq[:], ps[:], AF.Square), reads=[ps], writes=[sq])
                    S.op('dve', lambda e: e.reduce_sum(ss[:], sq[:].rearrange("p (h d) -> p h d", h=16), AX.X), reads=[sq], writes=[ss])
                    self.rstd_from_ss(ss[:], rstd[:], 128, tmp16[:])
                    S.op('dve', lambda e, ps=ps: e.tensor_tensor(qn[:], ps[:].rearrange("p (h d) -> p h d", h=16),
                                                                rstd[:, :, None].to_broadcast([128, 16, 128]), ALU.mult),
                         reads=[ps, rstd], writes=[qn])
                    S.op('dve', lambda e: e.tensor_tensor(qn[:], qn[:], qgB[:, None, :].to_broadcast([128, 16, 128]), ALU.mult),
                         reads=[qn, qgB], writes=[qn])
                    tb = tbr.next()
                    S.dma('sp', tb[:], io['ropeL'][lt * 128:(lt + 1) * 128, :], writes=[tb])
                    qr = qr_ring.next()
                    t2 = sq[:].rearrange("p (h d) -> p h d", h=16)
                    self.rope(qn[:], qr[:], tb, 16, 128, t1[:], t2)
                    pq = psQT.next()
                    for hd in range(16):
                        S.op('pe', lambda e, hd=hd, qr=qr, pq=pq: e.transpose(pq[:, hd, :], qr[:, hd, :], self.c['ident_b'][:]),
                             reads=[qr, self.c['ident_b']], writes=[pq])
                    qo = qo_ring.next()
                    S.op('act', lambda e, qo=qo, pq=pq: e.activation(qo[:], pq[:], AF.Copy), reads=[pq], writes=[qo])
                    S.dma('sp', qT_d[:, :, lt * 128:(lt + 1) * 128].rearrange("h p t -> p h t"), qo[:], reads=[qo], writes=['qT_d'])
            with S.phase():
                blocks = []
                if not self.last:
                    blocks.append(([0], list(CTXF)))
                for b in range(4):
                    blocks.append((list(range(1 + 4 * b, 5 + 4 * b)), list(range(NTF))))
                qring = Ring(S, "qblk", 2, [128, 16, 512], BF16)
                state = {}

                def load_q(hd, bi, tiles, new_block, new_head):
                    if new_block:
                        qb = qring.next()
                        T = 128 * len(tiles)
                        t0 = tiles[0] * 128
                        for h0 in range(0, 16, 4):
                            S.dma('sp', qb[:, h0:h0 + 4, :T], qT_d[h0:h0 + 4, :, t0:t0 + T].rearrange("h p t -> p h t"),
                                  reads=['qT_d'], writes=[qb])
                        state['qb'] = qb
                    return state['qb']

                def score(ps, hd, kt, qb, T):
                    g = hd // 4
                    S.op('pe', lambda e: e.matmul(ps[:, :T], kT_all[:, g, kt * 128:(kt + 1) * 128], qb[:, hd, :T], start=True, stop=True),
                         reads=[kT_all, qb], writes=[ps])

                def vfn(hd, kt):
                    g = hd // 4
                    return V_all[:, kt, g * 128:(g + 1) * 128], [V_all]
                self.attention(blocks, 16, score, vfn, load_q, oT_d, 128 ** -0.5)
        return self.wb['wo'], 'wo'

    def prep_gqa(self):
        io = self.io
        self.wb = {'wk': self.cast2d("wk", io['wk']), 'wv': self.cast2d("wv", io['wv']),
                   'wq': self.cast2d("wq", io['wq']), 'wo': self.cast2d("wo", io['wo'])}

    def prep_mlp(self):
        io = self.io
        self.w1b = self.cast2d("w1b", io['mlp_w1'])
        self.w2b = self.cast2d("w2b", io['mlp_w2'])

    def prep_lru(self):
        io = self.io
        self.wb = {'wx': self.cast2d("wx", io['wx']), 'wy': self.cast2d("wy", io['wy']), 'wo': self.cast2d("wo", io['wo']),
                   'ra_w': self.cast2d("ra_w", io['ra_w']), 'ix_w': self.cast2d("ix_w", io['ix_w'])}

    def mixer_lru(self, xfull, xloc, oT_d):
        S, io = self.S, self.io
        uT_d = self.dram("uT", [16, 128, NTF * 128], F32)
        yT_d = self.dram("yT", [16, 128, NTL * 128], F32)
        NT = NTF * 128
        with S.phase():
            Wx = S.sb("Wx", [128, 16, D], BF16)
            self.load_w(Wx, self.wb['wx'], 'wx')
            tools = self.make_hT_tools()
            hring = Ring(S, "hTb", 2, [128, 16, 512], BF16)
            psU = Ring(S, "psU", 2, [128, 512], F32, psum=True)
            ust = Ring(S, "ust", 3, [128, 512], F32)
            blocks = [([0, 17], 1, 0)]
            for b in range(8):
                f0 = 1 + 4 * b if b < 4 else 18 + 4 * (b - 4)
                blocks.append((list(range(f0, f0 + 4)), 0, 256 + 512 * b))
            for tiles, s, t0 in blocks:
                T = 128 * len(tiles)
                hb = hring.next()
                for ti, t in enumerate(tiles):
                    self.load_hT(tools, xfull(t), s, 0, 1, hb, col0=ti * 128, src_key=self.xf_key(t))
                for cc in range(16):
                    ps = psU.next()
                    for kc in range(16):
                        S.op('pe', lambda e, kc=kc, cc=cc, ps=ps, hb=hb: e.matmul(ps[:, :T], Wx[:, kc, cc * 128:(cc + 1) * 128], hb[:, kc, :T],
                                                                                 start=(kc == 0), stop=(kc == 15)),
                             reads=[self.wk(Wx, kc), hb], writes=[ps])
                    u = ust.next()
                    S.op('act', lambda e, u=u, ps=ps: e.activation(u[:, :T], ps[:, :T], AF.Copy), reads=[ps], writes=[u])
                    S.dma('sp', uT_d[cc, :, t0:t0 + T], u[:, :T], reads=[u], writes=[('uT', cc)])
        with S.phase():
            convw = S.sb("convw", [128, 16, 4], F32); convb = S.sb("convb", [128, 16], F32)
            rab = S.sb("rab", [128, 16, 2], F32); ixb = S.sb("ixb", [128, 16, 2], F32)
            lam = S.sb("lam", [128, 16, 2], F32); cd = S.sb("cd", [128, 16, 2], F32)
            ee = S.sb("ee", [128, 16, 2], F32); tt = S.sb("tt", [128, 16, 2], F32)
            hmask = S.sb("hmask", [128, 2], F32)
            for tl, nm in ((convw, 'convwT'), (convb, 'convbT'), (rab, 'rabT'), (ixb, 'ixbT'), (lam, 'lamT'), (hmask, 'hmask')):
                S.dma('sp', tl[:], io[nm], writes=[tl])
            S.op('act', lambda e: e.activation(ee[:], lam[:], AF.Exp, scale=-1.0), reads=[lam], writes=[ee])
            S.op('dve', lambda e: e.tensor_scalar(tt[:], ee[:], -0.25, 1.0 / 3.0, ALU.mult, ALU.add), reads=[ee], writes=[tt])
            S.op('dve', lambda e: e.tensor_tensor(tt[:], tt[:], ee[:], ALU.mult), reads=[tt, ee], writes=[tt])
            S.op('dve', lambda e: e.tensor_scalar(tt[:], tt[:], -1.0, 0.5, ALU.mult, ALU.add), reads=[tt], writes=[tt])
            S.op('dve', lambda e: e.tensor_tensor(tt[:], tt[:], ee[:], ALU.mult), reads=[tt, ee], writes=[tt])
            S.op('dve', lambda e: e.tensor_scalar(tt[:], tt[:], -1.0, 1.0, ALU.mult, ALU.add), reads=[tt], writes=[tt])
            S.op('dve', lambda e: e.tensor_tensor(tt[:], tt[:], ee[:], ALU.mult), reads=[tt, ee], writes=[tt])
            S.op('dve', lambda e: e.tensor_scalar_mul(cd[:], tt[:], -8.0), reads=[tt], writes=[cd])
            up = S.sb("up", [128, 4360], F32)
            S.op('pool', lambda e: e.memset(up[:], 0.0), writes=[up])
            uc = S.sb("uc", [128, 2, NT], F32)
            ucb = S.sb("ucb", [128, 2, NT], BF16)
            R = S.sb("R", [128, NT], F32); G = S.sb("G", [128, NT], F32); M = S.sb("M", [128, NT], F32)
            Y = S.sb("Y", [128, NT], F32); H = S.sb("H", [128, NT], F32)
            ysel = S.sb("ysel", [128, NTL * 128], F32)
            gwr = Ring(S, "gw", 2, [128, 2, 2, 2, 256], BF16)
            psR = Ring(S, "psR", 2, [128, 512], F32, psum=True)
            psI = Ring(S, "psI", 2, [128, 512], F32, psum=True)
            tblocks = [(t0, min(512, NT - t0)) for t0 in range(0, NT, 512)]
            for n in range(8):
                gw = gwr.next()
                for d in range(2):
                    for m, nm in enumerate(('ra_w', 'ix_w')):
                        S.dma('sp', gw[:, :, d, m, :], self.wb[nm][d, n].rearrange("(kk p) j -> p kk j", p=128),
                              reads=self.wkeys[nm], writes=[(gw.name, d, m)])
                for oc in range(2):
                    cc = 2 * n + oc
                    S.dma('sp', up[:, 2:258], uT_d[cc, :, 0:256], reads=[('uT', cc)], writes=[up])
                    S.dma('sp', up[:, 262:4358], uT_d[cc, :, 256:NT], reads=[('uT', cc)], writes=[up])
                    for (o0, n_, u0) in ((0, 256, 0), (256, 4096, 260)):
                        S.op('dve', lambda e, o0=o0, n_=n_, u0=u0, oc=oc, cc=cc: e.tensor_scalar(
                            uc[:, oc, o0:o0 + n_], up[:, u0:u0 + n_], convw[:, cc, 0:1], convb[:, cc:cc + 1], ALU.mult, ALU.add),
                            reads=[up, convw, convb], writes=[uc])
                        for j in range(1, 4):
                            S.op('dve', lambda e, o0=o0, n_=n_, u0=u0, oc=oc, cc=cc, j=j: e.scalar_tensor_tensor(
                                uc[:, oc, o0:o0 + n_], up[:, u0 + j:u0 + j + n_], convw[:, cc, j:j + 1], uc[:, oc, o0:o0 + n_],
                                ALU.mult, ALU.add), reads=[up, convw, uc], writes=[uc])
                    S.op('pool', lambda e, oc=oc: e.tensor_copy(ucb[:, oc, :], uc[:, oc, :]), reads=[uc], writes=[ucb])
                for oc in range(2):
                    cc = 2 * n + oc
                    for d in range(2):
                        for (t0, T) in tblocks:
                            pr = psR.next(); pi = psI.next()
                            for kk in range(2):
                                S.op('pe', lambda e, kk=kk, d=d, oc=oc, pr=pr, t0=t0, T=T, gw=gw: e.matmul(
                                    pr[:, :T], gw[:, kk, d, 0, oc * 128:(oc + 1) * 128], ucb[:, kk, t0:t0 + T],
                                    start=(kk == 0), stop=(kk == 1)), reads=[(gw.name, d, 0), ucb], writes=[pr])
                            for kk in range(2):
                                S.op('pe', lambda e, kk=kk, d=d, oc=oc, pi=pi, t0=t0, T=T, gw=gw: e.matmul(
                                    pi[:, :T], gw[:, kk, d, 1, oc * 128:(oc + 1) * 128], ucb[:, kk, t0:t0 + T],
                                    start=(kk == 0), stop=(kk == 1)), reads=[(gw.name, d, 1), ucb], writes=[pi])
                            S.op('act', lambda e, pr=pr, t0=t0, T=T, cc=cc, d=d: e.activation(R[:, t0:t0 + T], pr[:, :T], AF.Sigmoid,
                                                                                           bias=rab[:, cc, d:d + 1]),
                                 reads=[pr, rab], writes=[R])
                            S.op('act', lambda e, pi=pi, t0=t0, T=T, cc=cc, d=d: e.activation(G[:, t0:t0 + T], pi[:, :T], AF.Sigmoid,
                                                                                           bias=ixb[:, cc, d:d + 1]),
                                 reads=[pi, ixb], writes=[G])
                        S.op('act', lambda e, cc=cc, d=d: e.activation(R[:], R[:], AF.Exp, scale=cd[:, cc, d:d + 1]), reads=[R, cd], writes=[R])
                        S.op('pool', lambda e: e.tensor_tensor(M[:], R[:], R[:], ALU.mult), reads=[R], writes=[M])
                        S.op('act', lambda e: e.activation(M[:], M[:], AF.Sqrt, bias=1.0, scale=-1.0), reads=[M], writes=[M])
                        S.op('dve', lambda e, oc=oc: e.tensor_tensor(G[:], G[:], uc[:, oc, :], ALU.mult), reads=[G, uc], writes=[G])
                        S.op('pool', lambda e: e.tensor_tensor(G[:], G[:], M[:], ALU.mult), reads=[G, M], writes=[G])
                        if d == 0:
                            S.op('dve', lambda e: e.tensor_tensor_scan(Y[:, 0:256], R[:, 0:256], G[:, 0:256], 0.0, ALU.mult, ALU.add),
                                 reads=[R, G], writes=[Y])
                            S.op('dve', lambda e: e.tensor_tensor_scan(Y[:, 256:NT], R[:, 256:NT], G[:, 256:NT], Y[:, 255:256], ALU.mult, ALU.add),
                                 reads=[R, G, Y], writes=[Y])
                        else:
                            S.op('dve', lambda e: e.tensor_tensor_scan(H[:, 0:256][:, ::-1], R[:, 0:256][:, ::-1], G[:, 0:256][:, ::-1], 0.0,
                                                                       ALU.mult, ALU.add), reads=[R, G], writes=[H])
                            S.op('dve', lambda e: e.tensor_tensor_scan(H[:, 256:NT][:, ::-1], R[:, 256:NT][:, ::-1], G[:, 256:NT][:, ::-1],
                                                                       H[:, 0:1], ALU.mult, ALU.add), reads=[R, G, H], writes=[H])
                            S.op('pool', lambda e: e.tensor_tensor(Y[:], Y[:], H[:], ALU.add), reads=[Y, H], writes=[Y])
                    for (o0, n_, a0, a1) in ((0, 128, 0, 128), (128, 2048, 256, 2304)):
                        S.op('dve', lambda e, o0=o0, n_=n_, a0=a0: e.tensor_scalar_mul(ysel[:, o0:o0 + n_], Y[:, a0:a0 + n_], hmask[:, 0:1]),
                             reads=[Y, hmask], writes=[ysel])
                        S.op('dve', lambda e, o0=o0, n_=n_, a1=a1: e.scalar_tensor_tensor(ysel[:, o0:o0 + n_], Y[:, a1:a1 + n_], hmask[:, 1:2],
                                                                                         ysel[:, o0:o0 + n_], ALU.mult, ALU.add),
                             reads=[Y, hmask, ysel], writes=[ysel])
                    S.dma('sp', yT_d[cc], ysel[:], reads=[ysel], writes=[('yT', cc)])
        with S.phase():
            Wy = S.sb("Wy", [128, 16, D], BF16)
            self.load_w(Wy, self.wb['wy'], 'wy')
            tools = self.make_hT_tools()
            hring = Ring(S, "hTb", 2, [128, 16, 512], BF16)
            psG = Ring(S, "psG", 2, [128, 512], F32, psum=True)
            gst = Ring(S, "gst", 2, [128, 512], F32)
            ysl = Ring(S, "ysl", 2, [128, 512], F32)
            zb = Ring(S, "zb", 2, [128, 512], BF16)
            blocks = []
            if not self.last:
                blocks.append(([0], 1))
            for b in range(4):
                blocks.append((list(range(1 + 4 * b, 5 + 4 * b)), 0))
            for tiles, s in blocks:
                T = 128 * len(tiles)
                t0 = tiles[0] * 128
                hb = hring.next()
                for ti, lt in enumerate(tiles):
                    self.load_hT(tools, xloc(lt), s, 0, 1, hb, col0=ti * 128, src_key=self.xl_key(lt))
                for cc in range(16):
                    ps = psG.next()
                    for kc in range(16):
                        S.op('pe', lambda e, kc=kc, cc=cc, ps=ps, hb=hb: e.matmul(ps[:, :T], Wy[:, kc, cc * 128:(cc + 1) * 128], hb[:, kc, :T],
                                                                                 start=(kc == 0), stop=(kc == 15)),
                             reads=[self.wk(Wy, kc), hb], writes=[ps])
                    g = gst.next()
                    S.op('act', lambda e, g=g, ps=ps: e.activation(g[:, :T], ps[:, :T], AF.Gelu_apprx_tanh), reads=[ps], writes=[g])
                    yl = ysl.next()
                    S.dma('sp', yl[:, :T], yT_d[cc, :, t0:t0 + T], reads=[('yT', cc)], writes=[yl])
                    z = zb.next()
                    S.op('dve', lambda e, z=z, g=g, yl=yl: e.tensor_tensor(z[:, :T], g[:, :T], yl[:, :T], ALU.mult), reads=[g, yl], writes=[z])
                    S.dma('sp', oT_d[cc, :, t0:t0 + T], z[:, :T], reads=[z], writes=['oT_d'])
        return self.wb['wo'], 'wo'

    def prep_mla(self):
        io = self.io
        self.wb = {'wq_a': self.cast2d("wq_a", io['wq_a']), 'wq_b': self.cast2d("wq_b", io['wq_b'], piece_cols=1536),
                   'wkv_a': self.cast2d("wkv_a", io['wkv_a']), 'wkv_b': self.cast2d("wkv_b", io['wkv_b']),
                   'wo': self.cast2d("wo", io['wo'])}

    def mixer_mla(self, xfull, xloc, oT_d):
        S, io = self.S, self.io
        NT = NTF * 128
        qnT_d = self.dram("qnT", [16, 128, NTL * 128], BF16)
        qpT_d = self.dram("qpT", [16, 64, NTL * 128], BF16)
        with S.phase():
            ckvT = S.sb("ckvT", [128, 4, NT], BF16)
            kpeT = S.sb("kpeT", [64, NT], BF16)
            with S.phase():
                Wkva = S.sb("Wkva", [128, 16, 576], BF16)
                self.load_w(Wkva, self.wb['wkv_a'], 'wkv_a')
                kvgB = S.sb("kvgB", [128, 512], F32)
                S.dma('sp', kvgB[:], io['kv_a_g'][0:1, :].partition_broadcast(128), writes=[kvgB])
                tools = self.make_hT_tools()
                hring = Ring(S, "hT", 2, [128, 16, 128], BF16)
                psC = Ring(S, "psC", 1, [128, 1024], F32, psum=True)
                psCT = Ring(S, "psCT", 1, [128, 4, 128], BF16, psum=True)
                psKP = Ring(S, "psKP", 1, [128, 1024], BF16, psum=True)
                tbr = Ring(S, "tb", 2, [128, 128], F32)
                sqc = S.sb("sqc", [128, 512], F32)
                ss = S.sb("ss", [128, 1], F32); tmp1 = S.sb("tmp1", [128, 1], F32); rstd = S.sb("rstd1", [128, 1], F32)
                cnr = Ring(S, "cn", 2, [128, 512], BF16)
                kp = S.sb("kp", [128, 1, 64], F32); t1 = S.sb("t1", [128, 1, 64], F32); t2 = S.sb("t2", [128, 1, 64], F32)
                krr = Ring(S, "kr", 2, [128, 1, 64], BF16)
                for t in range(NTF):
                    s = 1 if t in CTXF else 0
                    hT = hring.next()
                    self.load_hT(tools, xfull(t), s, 0, 1, hT, src_key=self.xf_key(t))
                    ps = psC.next()
                    for (c0, c1) in ((0, 512), (512, 576)):
                        for kc in range(16):
                            S.op('pe', lambda e, kc=kc, c0=c0, c1=c1, hT=hT, ps=ps: e.matmul(ps[:, c0:c1], hT[:, kc, :], Wkva[:, kc, c0:c1],
                                                                                            start=(kc == 0), stop=(kc == 15)),
                                 reads=[hT, self.wk(Wkva, kc)], writes=[ps])
                    S.op('act', lambda e, ps=ps: e.activation(sqc[:], ps[:, 0:512], AF.Square, accum_out=ss[:]), reads=[ps], writes=[sqc, ss])
                    self.rstd_from_ss(ss[:], rstd[:], 512, tmp1[:])
                    cn = cnr.next()
                    S.op('dve', lambda e, cn=cn, ps=ps: e.scalar_tensor_tensor(cn[:], ps[:, 0:512], rstd[:], kvgB[:], ALU.mult, ALU.mult),
                         reads=[ps, rstd, kvgB], writes=[cn])
                    S.op('act', lambda e, ps=ps: e.activation(kp[:, 0, :], ps[:, 512:576], AF.Copy), reads=[ps], writes=[kp])
                    tb = tbr.next()
                    S.dma('sp', tb[:], io['ropeF'][t * 128:(t + 1) * 128, :], writes=[tb])
                    kr = krr.next()
                    self.rope(kp[:], kr[:], tb, 1, 64, t1[:], t2[:])
                    pc = psCT.next()
                    for c in range(4):
                        S.op('pe', lambda e, c=c, cn=cn, pc=pc: e.transpose(pc[:, c, :], cn[:, c * 128:(c + 1) * 128], self.c['ident_b'][:]),
                             reads=[cn, self.c['ident_b']], writes=[pc])
                    S.op('act', lambda e, t=t, pc=pc: e.activation(ckvT[:, :, t * 128:(t + 1) * 128], pc[:], AF.Copy), reads=[pc], writes=[ckvT])
                    pk = psKP.next()
                    S.op('pe', lambda e, kr=kr, pk=pk: e.transpose(pk[0:64, 0:128], kr[:, 0, :], self.c['ident_b'][:]),
                         reads=[kr, self.c['ident_b']], writes=[pk])
                    S.op('dve', lambda e, t=t, pk=pk: e.tensor_copy(kpeT[:, t * 128:(t + 1) * 128], pk[0:64, 0:128]), reads=[pk], writes=[kpeT])
            with S.phase():
                Wqa = S.sb("Wqa", [128, 16, 512], BF16)
                self.load_w(Wqa, self.wb['wq_a'], 'wq_a')
                Wqb = S.sb("Wqb", [128, 4, 3072], BF16)
                self.load_w(Wqb, self.wb['wq_b'], 'wq_b')
                qagB = S.sb("qagB", [128, 512], F32)
                S.dma('sp', qagB[:], io['q_a_g'][0:1, :].partition_broadcast(128), writes=[qagB])
                tools = self.make_hT_tools()
                hring = Ring(S, "hT", 2, [128, 16, 128], BF16)
                psA = Ring(S, "psA", 1, [128, 512], F32, psum=True)
                psAT = Ring(S, "psAT", 1, [128, 4, 128], BF16, psum=True)
                psQh = Ring(S, "psQh", 1, [128, 1536], F32, psum=True)
                psT8 = Ring(S, "psT8", 1, [128, 8, 128], BF16, psum=True)
                tbr = Ring(S, "tb", 2, [128, 128], F32)
                sqa = S.sb("sqa", [128, 512], F32)
                ss = S.sb("ss", [128, 1], F32); tmp1 = S.sb("tmp1", [128, 1], F32); rstd = S.sb("rstd1", [128, 1], F32)
                qar = Ring(S, "qa", 2, [128, 512], BF16)
                qaTr = Ring(S, "qaT", 2, [128, 4, 128], BF16)
                qn8r = Ring(S, "qn8", 2, [128, 8, 128], BF16)
                pe8 = S.sb("pe8", [128, 8, 64], F32); t1 = S.sb("t1", [128, 8, 64], F32); t2 = S.sb("t2", [128, 8, 64], F32)
                qp8r = Ring(S, "qp8", 2, [128, 8, 64], BF16)
                qnor = Ring(S, "qno", 2, [128, 8, 128], BF16)
                qpor = Ring(S, "qpo", 2, [64, 8, 128], BF16)
                for lt in self.ltiles:
                    s = 1 if lt == 0 else 0
                    hT = hring.next()
                    self.load_hT(tools, xloc(lt), s, 0, 1, hT, src_key=self.xl_key(lt))
                    ps = psA.next()
                    for kc in range(16):
                        S.op('pe', lambda e, kc=kc, hT=hT, ps=ps: e.matmul(ps[:], hT[:, kc, :], Wqa[:, kc, :], start=(kc == 0), stop=(kc == 15)),
                             reads=[hT, self.wk(Wqa, kc)], writes=[ps])
                    S.op('act', lambda e, ps=ps: e.activation(sqa[:], ps[:], AF.Square, accum_out=ss[:]), reads=[ps], writes=[sqa, ss])
                    self.rstd_from_ss(ss[:], rstd[:], 512, tmp1[:])
                    qa = qar.next()
                    S.op('dve', lambda e, qa=qa, ps=ps: e.scalar_tensor_tensor(qa[:], ps[:], rstd[:], qagB[:], ALU.mult, ALU.mult),
                         reads=[ps, rstd, qagB], writes=[qa])
                    pa = psAT.next()
                    for c in range(4):
                        S.op('pe', lambda e, c=c, qa=qa, pa=pa: e.transpose(pa[:, c, :], qa[:, c * 128:(c + 1) * 128], self.c['ident_b'][:]),
                             reads=[qa, self.c['ident_b']], writes=[pa])
                    qaT = qaTr.next()
                    S.op('act', lambda e, qaT=qaT, pa=pa: e.activation(qaT[:], pa[:], AF.Copy), reads=[pa], writes=[qaT])
                    tb = tbr.next()
                    S.dma('sp', tb[:], io['ropeL'][lt * 128:(lt + 1) * 128, :], writes=[tb])
                    for half in range(2):
                        pq = psQh.next()
                        for ng in range(3):
                            c0 = half * 1536 + ng * 512
                            for kc in range(4):
                                S.op('pe', lambda e, ng=ng, kc=kc, c0=c0, pq=pq, qaT=qaT: e.matmul(
                                    pq[:, ng * 512:(ng + 1) * 512], qaT[:, kc, :], Wqb[:, kc, c0:c0 + 512], start=(kc == 0), stop=(kc == 3)),
                                    reads=[qaT, self.wk(Wqb, kc)], writes=[pq])
                        pv = pq[:].rearrange("p (h d) -> p h d", h=8)
                        qn8 = qn8r.next()
                        S.op('act', lambda e, qn8=qn8, pv=pv: e.activation(qn8[:], pv[:, :, 0:128], AF.Copy), reads=[pq], writes=[qn8])
                        S.op('act', lambda e, pv=pv: e.activation(pe8[:], pv[:, :, 128:192], AF.Copy), reads=[pq], writes=[pe8])
                        qp8 = qp8r.next()
                        self.rope(pe8[:], qp8[:], tb, 8, 64, t1[:], t2[:])
                        pt = psT8.next()
                        for h in range(8):
                            S.op('pe', lambda e, h=h, qn8=qn8, pt=pt: e.transpose(pt[:, h, :], qn8[:, h, :], self.c['ident_b'][:]),
                                 reads=[qn8, self.c['ident_b']], writes=[pt])
                        qno = qnor.next()
                        S.op('dve', lambda e, qno=qno, pt=pt: e.tensor_copy(qno[:], pt[:]), reads=[pt], writes=[qno])
                        S.dma('sp', qnT_d[half * 8:(half + 1) * 8, :, lt * 128:(lt + 1) * 128].rearrange("h p t -> p h t"), qno[:],
                              reads=[qno], writes=['qnT_d'])
                        for h in range(8):
                            S.op('pe', lambda e, h=h, qp8=qp8, pt=pt: e.transpose(pt[0:64, h, :], qp8[:, h, :], self.c['ident_b'][:]),
                                 reads=[qp8, self.c['ident_b']], writes=[pt])
                        qpo = qpor.next()
                        S.op('dve', lambda e, qpo=qpo, pt=pt: e.tensor_copy(qpo[:], pt[0:64, :, :]), reads=[pt], writes=[qpo])
                        S.dma('sp', qpT_d[half * 8:(half + 1) * 8, :, lt * 128:(lt + 1) * 128].rearrange("h p t -> p h t"), qpo[:],
                              reads=[qpo], writes=['qpT_d'])
            with S.phase():
                Wkvb = S.sb("Wkvb", [128, 4, 4096], BF16)
                self.load_w(Wkvb, self.wb['wkv_b'], 'wkv_b')
                knTr = Ring(S, "knT", 2, [128, NT], BF16)
                Vhr = Ring(S, "Vh", 2, [128, NTF, 128], BF16)
                psKV = Ring(S, "psKVm", 2, [128, 512], F32, psum=True)
                qnr = Ring(S, "qnb", 2, [128, 512], BF16)
                qpr = Ring(S, "qpb", 2, [64, 512], BF16)
                blocks = []
                if not self.last:
                    blocks.append(([0], list(CTXF)))
                for b in range(4):
                    blocks.append((list(range(1 + 4 * b, 5 + 4 * b)), list(range(NTF))))
                st = {}

                def pre_head(hd):
                    knT = knTr.next(); Vh = Vhr.next()
                    st['knT'], st['Vh'] = knT, Vh
                    for i, t0 in enumerate(range(0, NT, 512)):
                        T = min(512, NT - t0)
                        ps = psKV.next()
                        for kc in range(4):
                            S.op('pe', lambda e, kc=kc, ps=ps, t0=t0, T=T: e.matmul(ps[:, :T], Wkvb[:, kc, hd * 256:hd * 256 + 128], ckvT[:, kc, t0:t0 + T],
                                                                                 start=(kc == 0), stop=(kc == 3)),
                                 reads=[self.wk(Wkvb, kc), ckvT], writes=[ps])
                        S.op('dve', lambda e, ps=ps, t0=t0, T=T, knT=knT: e.tensor_copy(knT[:, t0:t0 + T], ps[:, :T]), reads=[ps], writes=[knT])
                    for g4 in range(0, NTF, 4):
                        nj = min(4, NTF - g4)
                        ps = psKV.next()
                        for j in range(nj):
                            kt = g4 + j
                            for kc in range(4):
                                S.op('pe', lambda e, kc=kc, ps=ps, j=j, kt=kt: e.matmul(ps[:, j * 128:(j + 1) * 128], ckvT[:, kc, kt * 128:(kt + 1) * 128],
                                                                                      Wkvb[:, kc, hd * 256 + 128:hd * 256 + 256],
                                                                                      start=(kc == 0), stop=(kc == 3)),
                                     reads=[self.wk(Wkvb, kc), ckvT], writes=[ps])
                        S.op('dve', lambda e, ps=ps, g4=g4, nj=nj, Vh=Vh: e.tensor_copy(Vh[:, g4:g4 + nj, :],
                                                                                       ps[:, 0:nj * 128].rearrange("p (j d) -> p j d", j=nj)),
                             reads=[ps], writes=[Vh])

                def load_q(hd, bi, tiles, new_block, new_head):
                    T = 128 * len(tiles)
                    t0 = tiles[0] * 128
                    qn = qnr.next(); qp = qpr.next()
                    S.dma('sp', qn[:, :T], qnT_d[hd, :, t0:t0 + T], reads=['qnT_d'], writes=[qn])
                    S.dma('sp', qp[:, :T], qpT_d[hd, :, t0:t0 + T], reads=['qpT_d'], writes=[qp])
                    return (qn, qp)

                def score(ps, hd, kt, qb, T):
                    qn, qp = qb
                    knT = st['knT']
                    S.op('pe', lambda e: e.matmul(ps[:, :T], knT[:, kt * 128:(kt + 1) * 128], qn[:, :T], start=True, stop=False),
                         reads=[knT, qn], writes=[ps])
                    S.op('pe', lambda e: e.matmul(ps[:, :T], kpeT[:, kt * 128:(kt + 1) * 128], qp[:, :T], start=False, stop=True),
                         reads=[kpeT, qp], writes=[ps])

                def vfn(hd, kt):
                    Vh = st['Vh']
                    return Vh[:, kt, :], [Vh]
                self.attention(blocks, 16, score, vfn, load_q, oT_d, 192 ** -0.5, pre_head=pre_head, head_outer=True)
        return self.wb['wo'], 'wo'

    def prep(self):
        ns = self.S.ns
        self.S.ns = self.li
        if self.kind == 0:
            self.prep_gqa()
        elif self.kind == 1:
            self.prep_lru()
        else:
            self.prep_mla()
        self.prep_mlp()
        self.S.ns = ns

    def build(self, xfull, xloc, xout, keys, on_tile_done=None, before_mlp=None):
        S = self.S
        S.ns = self.li
        self.xf_key, self.xl_key, self.xo_key = keys
        self.on_tile_done = on_tile_done
        self.phase_mod()
        if before_mlp is not None:
            before_mlp()
        oT_d = self.dram("oT", [16, 128, NTL * 128], BF16)
        xmid = self.dram("xmid", [NTL * 128, D], F32)
        if self.kind == 0:
            wo_b, wo_key = self.mixer_gqa(xfull, xloc, oT_d)
        elif self.kind == 1:
            wo_b, wo_key = self.mixer_lru(xfull, xloc, oT_d)
        else:
            wo_b, wo_key = self.mixer_mla(xfull, xloc, oT_d)
        self.phase_out(oT_d, wo_b, wo_key, xloc, xmid)
        self.phase_mlp(xmid, xout)


def make_consts(S):
    c = {}
    c['ident_f'] = S.sb("ident_f", [128, 128], F32)
    c['ident_b'] = S.sb("ident_b", [128, 128], BF16)
    c['ones_f'] = S.sb("ones_f", [128, 128], F32)
    c['ones_b'] = S.sb("ones_b", [128, 128], BF16)
    S.op('pool', lambda e: e.memset(c['ident_f'][:], 1.0), writes=[c['ident_f']])
    S.op('pool', lambda e: e.affine_select(out=c['ident_f'][:], in_=c['ident_f'][:], pattern=[[-1, 128]],
                                           compare_op=ALU.is_equal, fill=0.0, base=0, channel_multiplier=1),
         reads=[c['ident_f']], writes=[c['ident_f']])
    S.op('dve', lambda e: e.tensor_copy(c['ident_b'][:], c['ident_f'][:]), reads=[c['ident_f']], writes=[c['ident_b']])
    S.op('pool', lambda e: e.memset(c['ones_f'][:], 1.0), writes=[c['ones_f']])
    S.op('pool', lambda e: e.memset(c['ones_b'][:], 1.0), writes=[c['ones_b']])
    return c


def layer_input_specs(kind):
    sp = {
        'cT': ([128, 32], F32), 'adabT': ([128, 96], F32), 'ada_w': ([D, 6 * D], F32),
        'ln_g': ([2, D], F32), 'ln_b': ([2, D], F32), 'mlp_w1': ([D, DFF], F32), 'mlp_w2': ([DFF, D], F32),
    }
    if kind == 0:
        sp.update({'wq': ([D, D], F32), 'wk': ([D, 512], F32), 'wv': ([D, 512], F32), 'wo': ([D, D], F32),
                   'q_g': ([1, 128], F32), 'k_g': ([1, 128], F32),
                   'ropeF': ([NTF * 128, 256], F32), 'ropeL': ([NTL * 128, 256], F32)})
    elif kind == 1:
        sp.update({'wx': ([D, D], F32), 'wy': ([D, D], F32), 'wo': ([D, D], F32),
                   'ra_w': ([2, 8, 256, 256], F32), 'ix_w': ([2, 8, 256, 256], F32),
                   'convwT': ([128, 16, 4], F32), 'convbT': ([128, 16], F32), 'rabT': ([128, 16, 2], F32),
                   'ixbT': ([128, 16, 2], F32), 'lamT': ([128, 16, 2], F32), 'hmask': ([128, 2], F32)})
    else:
        sp.update({'wq_a': ([D, 512], F32), 'wq_b': ([512, 3072], F32), 'wkv_a': ([D, 576], F32), 'wkv_b': ([512, 4096], F32),
                   'wo': ([D, D], F32), 'q_a_g': ([1, 512], F32), 'kv_a_g': ([1, 512], F32),
                   'ropeF': ([NTF * 128, 128], F32), 'ropeL': ([NTL * 128, 128], F32)})
    return sp


_DBG = {}
PAIRS = [[0, 1], [2, 3], [4, 5], [6, 7]]
CHUNKS = [(c, [2 * c, 2 * c + 1]) for c in range(8)] + [(8, [16])]


def build_program(layers=(0, 1, 2, 3)):
    nc = bass.Bass("TRN2", target_bir_lowering=False)
    S = Sched(nc)
    S.limit = _DBG.get('limit', 1 << 60)
    consts = make_consts(S)
    x_in = nc.dram_tensor("xfull", [NTF * 128, D], F32, kind="ExternalInput").ap()
    xl_in = nc.dram_tensor("xloc", [NTL * 128, D], F32, kind="ExternalInput").ap()
    final = layers[-1] == DEPTH - 1
    if final:
        out = nc.dram_tensor("out", [2048, D], F32, kind="ExternalOutput").ap()
    else:
        out = nc.dram_tensor("out", [NTL * 128, D], F32, kind="ExternalOutput").ap()
    progs = []
    for li in layers:
        io = {}
        for name, (shape, dt) in layer_input_specs(li % 3).items():
            io[name] = nc.dram_tensor("L%d_%s" % (li, name), shape, dt, kind="ExternalInput").ap()
        progs.append(LayerProg(S, nc, li, li % 3, li == DEPTH - 1, io, consts))
    xfull = lambda t: x_in[t * 128:(t + 1) * 128, :]
    xf_key = lambda t: ('abs', 'xin', t)
    xloc = lambda lt: xl_in[lt * 128:(lt + 1) * 128, :]
    xl_key = lambda lt: ('abs', 'xlin', lt)
    progs[0].prep()
    for i, lp in enumerate(progs):
        li = lp.li
        is_last_in_prog = i == len(progs) - 1
        if is_last_in_prog:
            if final:
                xout = lambda lt: out[(lt - 1) * 128:lt * 128, :]
            else:
                xout = lambda lt: out[lt * 128:(lt + 1) * 128, :]
            on_done = None
            gat = None
        else:
            xres = [nc.dram_tensor("xres%d_%d" % (li, c), [128 * len(tl), D], F32) for c, tl in CHUNKS]
            gat = [nc.dram_tensor("xgat%d_%d" % (li, c), [2 * 128 * len(tl), D], F32) for c, tl in CHUNKS]

            def xout(lt, xres=xres):
                c = min(lt // 2, 8)
                r = (lt - 2 * c) * 128
                return xres[c].ap()[r:r + 128, :]

            def on_done(lt, li=li, xres=xres, gat=gat):
                c, tl = CHUNKS[min(lt // 2, 8)]
                if lt != tl[-1]:
                    return
                if _DBG.get('nocc'):
                    n = 128 * len(tl)
                    S.dma('sp', gat[c].ap()[0:n, :], xres[c].ap()[:, :], reads=[('abs', 'xout', li, t) for t in tl], writes=[('abs', 'xgat', li, c)])
                    S.dma('sp', gat[c].ap()[n:2 * n, :], xres[c].ap()[:, :], reads=[('abs', 'xout', li, t) for t in tl], writes=[('abs', 'xgat', li, c)])
                    return
                S.collective("AllGather", [xres[c].ap().opt()], [gat[c].ap().opt()], PAIRS,
                             reads=[('abs', 'xout', li, t) for t in tl], writes=[('abs', 'xgat', li, c)])
        xo_key = lambda lt, li=li: ('abs', 'xout', li, lt)
        nxt = progs[i + 1] if not is_last_in_prog else None
        early = nxt is not None and not _DBG.get('late_prep')
        lp.build(xfull, xloc, xout, (xf_key, xl_key, xo_key), on_tile_done=on_done,
                 before_mlp=(nxt.prep if early else None))
        if nxt is not None and not early:
            nxt.prep()
        if not is_last_in_prog:
            def xfull(t, gat=gat):
                r, lt = divmod(t, NTL)
                c, tl = CHUNKS[min(lt // 2, 8)]
                row = r * 128 * len(tl) + (lt - tl[0]) * 128
                return gat[c].ap()[row:row + 128, :]
            xf_key = lambda t, li=li: ('abs', 'xgat', li, min((t % NTL) // 2, 8))
            xloc = xout
            xl_key = xo_key
    S.finish()
    return nc, S


def rope_tables(rot_dim):
    t = np.arange(4096)
    row = (t // 64).astype(np.float32)
    col = (t % 64).astype(np.float32)
    m = rot_dim // 2
    inv = (np.float32(10000.0) ** (-(np.arange(m // 2, dtype=np.float32) * np.float32(2.0)) / np.float32(m))).astype(np.float32)
    ar = (row[:, None] * inv[None, :]).astype(np.float32)
    ac = (col[:, None] * inv[None, :]).astype(np.float32)
    cr, sr, cc, sc = np.cos(ar), np.sin(ar), np.cos(ac), np.sin(ac)
    cosF = np.concatenate([cr, cr, cc, cc], 1)
    sinA = np.concatenate([-sr, sr, -sc, sc], 1)
    return np.concatenate([cosF, sinA], 1).astype(np.float32)


def ident_table(n, rot_dim):
    return np.concatenate([np.ones((n, rot_dim), np.float32), np.zeros((n, rot_dim), np.float32)], 1)


def full_order(ctx_b, lat_b):
    return np.concatenate([ctx_b[0:128], lat_b[0:2048], ctx_b[128:256], lat_b[2048:4096]], 0)


def local_order(ctx_b, lat_b, h):
    return np.concatenate([ctx_b[128 * h:128 * h + 128], lat_b[2048 * h:2048 * h + 2048]], 0)


def per_part(v, nchunk):
    return np.ascontiguousarray(v.reshape(nchunk, 128).T)


_PROG_CACHE = {}


def layer_host_inputs(inp, li):
    f = lambda a: np.ascontiguousarray(np.asarray(a, np.float32))
    kind, slot = li % 3, li // 3
    m = {
        'adabT': per_part(f(inp['ada_b'][li]), 96), 'ada_w': f(inp['ada_w'][li]),
        'ln_g': f(inp['ln_g'][li]), 'ln_b': f(inp['ln_b'][li]),
        'mlp_w1': f(inp['mlp_w1'][li]), 'mlp_w2': f(inp['mlp_w2'][li]),
    }
    percore = {}
    if kind == 0:
        tab, idt = rope_tables(128), ident_table(256, 128)
        m.update({'wq': f(inp['gqa_wq'][slot]), 'wk': f(inp['gqa_wk'][slot]), 'wv': f(inp['gqa_wv'][slot]), 'wo': f(inp['gqa_wo'][slot]),
                  'q_g': f(inp['gqa_q_g'][slot]).reshape(1, 128), 'k_g': f(inp['gqa_k_g'][slot]).reshape(1, 128),
                  'ropeF': full_order(idt, tab)})
        percore['ropeL'] = [local_order(idt, tab, h) for h in range(2)]
    elif kind == 1:
        pp2 = lambda v: np.ascontiguousarray(np.stack([per_part(f(v[0]), 16), per_part(f(v[1]), 16)], 2))
        m.update({'wx': f(inp['lru_wx'][slot]), 'wy': f(inp['lru_wy'][slot]), 'wo': f(inp['lru_wo'][slot]),
                  'ra_w': f(inp['lru_ra_w'][slot]), 'ix_w': f(inp['lru_ix_w'][slot]),
                  'convwT': np.ascontiguousarray(np.stack([per_part(f(inp['lru_conv_w'][slot][j]), 16) for j in range(4)], 2)),
                  'convbT': per_part(f(inp['lru_conv_b'][slot]), 16),
                  'rabT': pp2(inp['lru_ra_b'][slot]), 'ixbT': pp2(inp['lru_ix_b'][slot]), 'lamT': pp2(inp['lru_lam'][slot])})
        hm = []
        for h in range(2):
            a = np.zeros((128, 2), np.float32)
            a[:, h] = 1.0
            hm.append(a)
        percore['hmask'] = hm
    else:
        tab, idt = rope_tables(64), ident_table(256, 64)
        m.update({'wq_a': f(inp['mla_wq_a'][slot]), 'wq_b': f(inp['mla_wq_b'][slot]), 'wkv_a': f(inp['mla_wkv_a'][slot]),
                  'wkv_b': f(inp['mla_wkv_b'][slot]), 'wo': f(inp['mla_wo'][slot]),
                  'q_a_g': f(inp['mla_q_a_g'][slot]).reshape(1, 512), 'kv_a_g': f(inp['mla_kv_a_g'][slot]).reshape(1, 512),
                  'ropeF': full_order(idt, tab)})
        percore['ropeL'] = [local_order(idt, tab, h) for h in range(2)]
    return m, percore


def kernel(**inp):
    x = np.asarray(inp['x'], np.float32)
    ctx = np.asarray(inp['ctx'], np.float32)
    layers = tuple(_DBG.get('layers', range(DEPTH)))
    lat = [x[b] for b in range(4)]
    cx = [ctx[b] for b in range(4)]
    if 'init' in _DBG:
        lat, cx = [np.asarray(a) for a in _DBG['init'][0]], [np.asarray(a) for a in _DBG['init'][1]]
    if layers not in _PROG_CACHE:
        _PROG_CACHE[layers] = build_program(layers)[0]
    nc = _PROG_CACHE[layers]
    common = {}
    percore = {}
    for li in layers:
        m, pc = layer_host_inputs(inp, li)
        for k, v in m.items():
            common["L%d_%s" % (li, k)] = v
        for k, v in pc.items():
            percore["L%d_%s" % (li, k)] = v
    in_maps = []
    for core in range(NCORES):
        b, h = core // 2, core % 2
        m = dict(common)
        for k, v in percore.items():
            m[k] = v[h]
        cpair = np.stack([np.asarray(inp['c'][b], np.float32), np.asarray(inp['c_ctx'], np.float32)], 1)
        cT = np.ascontiguousarray(cpair.reshape(16, 128, 2).transpose(1, 0, 2).reshape(128, 32))
        for li in layers:
            m["L%d_cT" % li] = cT
        m['xfull'] = full_order(cx[b], lat[b])
        m['xloc'] = local_order(cx[b], lat[b], h)
        in_maps.append(m)
    ncr = _DBG.get('ncores', NCORES)
    res = run_bass_kernel_spmd(nc, in_maps[:ncr], core_ids=list(range(ncr)))
    final = layers[-1] == DEPTH - 1
    for b in range(ncr // 2):
        o0 = res.results[2 * b]['out']
        o1 = res.results[2 * b + 1]['out']
        if final:
            lat[b] = np.concatenate([o0, o1], 0)
        else:
            lat[b] = np.concatenate([o0[128:], o1[128:]], 0)
            cx[b] = np.concatenate([o0[:128], o1[:128]], 0)
    _DBG['lat'], _DBG['cx'] = lat, cx
    return np.stack(lat, 0).astype(np.float32)
```

```python
import contextlib
import math
import numpy as np
import ml_dtypes
import concourse.bass as bass
import concourse.mybir as mybir
from concourse.bass_utils import run_bass_kernel_spmd

F32 = mybir.dt.float32
BF16 = mybir.dt.bfloat16
AF = mybir.ActivationFunctionType
ALU = mybir.AluOpType
AX = mybir.AxisListType

D = 2048
DFF = 8192
NTF = 34
NTL = 17
CTXF = (0, 17)
TIME_ORDER = [0, 17] + list(range(1, 17)) + list(range(18, 34))
EPS = 1e-6
DEPTH = 4
ALPHA = (2 * DEPTH) ** 0.25
NCORES = 8


class Sched:
    ENGS = ('pe', 'act', 'dve', 'pool', 'sp')
    NDMA = 48

    def __init__(self, nc):
        self.nc = nc
        self.es = contextlib.ExitStack()
        self.eng = {'pe': nc.tensor, 'act': nc.scalar, 'dve': nc.vector, 'pool': nc.gpsimd, 'sp': nc.sync}
        self.sem = {e: self.es.enter_context(nc.semaphore("s_" + e)) for e in self.ENGS if e != 'sp'}
        self.dsem = [self.es.enter_context(nc.semaphore("d%d" % i)) for i in range(self.NDMA)]
        self.seq = {e: 0 for e in self.ENGS}
        self.waited = {e: {} for e in self.ENGS}
        self.buf = {}
        self.dma_i = 0
        self.nwait = 0
        self.nops = 0
        self.last_dma = {}
        self.stack = [self.es]
        self.uid = 0
        self.ns = 0
        self.NCC = 4
        for i in range(self.NCC):
            self.dsem.append(self.es.enter_context(nc.semaphore("cc%d" % i)))
        self.cc_i = 0
        self.NBG = 24
        for i in range(self.NBG):
            self.dsem.append(self.es.enter_context(nc.semaphore("bg%d" % i)))
        self.bg_i = 0
        self.pool_dma_out = {}
        self.cc_out = {}
        self.limit = 1 << 60
        self.marks = []
        self.any_dma = {}

    def sb(self, name, shape, dt):
        self.uid += 1
        return self.stack[-1].enter_context(self.nc.sbuf_tensor("%s_%d" % (name, self.uid), shape, dt))

    def ps(self, name, shape, dt=F32):
        self.uid += 1
        return self.stack[-1].enter_context(self.nc.psum_tensor("%s_%d" % (name, self.uid), shape, dt))

    @contextlib.contextmanager
    def phase(self):
        es = contextlib.ExitStack()
        self.marks.append((self.ns, self.nops))
        self.stack.append(es)
        try:
            yield
        finally:
            self.barrier()
            self.stack.pop()
            es.close()

    def _key(self, b):
        if isinstance(b, tuple):
            return b if b[0] == 'abs' else (self.ns, b)
        if isinstance(b, str):
            return (self.ns, b)
        return b.name

    def _deps(self, eng, reads, writes, is_dma):
        need = {}

        def add(ref, kind):
            if ref[0] == 'e':
                _, E, s = ref
                if E == eng and not is_dma and (eng == 'pe' or kind != 'raw'):
                    return
                k = ('e', E)
                v = s
            else:
                k = ('d', ref[1])
                v = ref[2]
            if need.get(k, 0) < v:
                need[k] = v

        for r in reads:
            st = self.buf.get(self._key(r))
            if st and st[0] is not None:
                add(st[0], 'raw')
        for w in writes:
            st = self.buf.get(self._key(w))
            if st:
                if st[0] is not None:
                    add(st[0], 'waw')
                for rr in st[1]:
                    add(rr, 'war')
        return need

    def _emit_waits(self, eng, need):
        e = self.eng[eng]
        wd = self.waited[eng]
        for k, v in need.items():
            if wd.get(k, 0) >= v:
                continue
            sem = self.sem[k[1]] if k[0] == 'e' else self.dsem[k[1]]
            e.wait_ge(sem, v)
            wd[k] = v
            self.nwait += 1

    def _update(self, ref, reads, writes):
        for r in reads:
            st = self.buf.setdefault(self._key(r), [None, []])
            st[1].append(ref)
            if len(st[1]) > 64:
                best = {}
                for x in st[1]:
                    k = (x[0], x[1])
                    if k not in best or best[k][2] < x[2]:
                        best[k] = x
                st[1] = list(best.values())
        for w in writes:
            st = self.buf.setdefault(self._key(w), [None, []])
            st[0] = ref
            st[1] = []

    def op(self, eng, fn, reads=(), writes=()):
        if self.nops >= self.limit:
            return None
        need = self._deps(eng, reads, writes, False)
        self._emit_waits(eng, need)
        ins = fn(self.eng[eng])
        self.seq[eng] += 1
        ins.then_inc(self.sem[eng], 1)
        self._update(('e', eng, self.seq[eng]), reads, writes)
        self.nops += 1
        return ins

    def dma(self, q, out, in_, reads=(), writes=(), bg=False, **kw):
        if self.nops >= self.limit:
            return None
        need = self._deps(q, reads, writes, True)
        if bg:
            i = self.bg_i
            self.bg_i += 1
            idx = self.NDMA + self.NCC + i % self.NBG
            val = 16 * (i // self.NBG + 1)
        else:
            i = self.dma_i
            self.dma_i += 1
            idx = i % self.NDMA
            val = 16 * (i // self.NDMA + 1)
        if val > 16:
            k = ('d', idx)
            need[k] = max(need.get(k, 0), val - 16)
        if q == 'pool':
            for k, v in self.cc_out.items():
                need[k] = max(need.get(k, 0), v)
            self.pool_dma_out[('d', idx)] = val
        self._emit_waits(q, need)
        ins = self.eng[q].dma_start(out=out, in_=in_, **kw)
        ins.then_inc(self.dsem[idx], 16)
        ref = ('d', idx, val)
        self._update(ref, reads, writes)
        self.any_dma[idx] = val
        if bg:
            self.last_dma.pop(idx, None)
        else:
            self.last_dma[idx] = val
        self.nops += 1
        return ins

    def collective(self, kind, ins, outs, groups, reads=(), writes=()):
        need = self._deps('pool', reads, writes, True)
        i = self.cc_i
        self.cc_i += 1
        idx = self.NDMA + i % self.NCC
        val = i // self.NCC + 1
        if val > 1:
            k = ('d', idx)
            need[k] = max(need.get(k, 0), val - 1)
        for k, v in self.pool_dma_out.items():
            need[k] = max(need.get(k, 0), v)
        self.cc_out[('d', idx)] = val
        self._emit_waits('pool', need)
        ins_ = self.nc.gpsimd.collective_compute(kind, ALU.bypass, replica_groups=groups, ins=ins, outs=outs)
        ins_.then_inc(self.dsem[idx])
        ref = ('d', idx, val)
        self._update(ref, reads, writes)
        self.last_dma[idx] = val
        self.nops += 1
        return ins_

    def barrier(self):
        need = {}
        for e in self.ENGS:
            if e != 'sp' and self.seq[e] > 0:
                need[('e', e)] = self.seq[e]
        for idx, val in self.last_dma.items():
            need[('d', idx)] = val
        for e in self.ENGS:
            self._emit_waits(e, dict(need))

    def finish(self):
        self.last_dma.update(self.any_dma)
        self.barrier()
        self.es.close()


class Ring:
    def __init__(self, S, name, n, shape, dt, psum=False):
        self.bufs = [(S.ps if psum else S.sb)("%s%d" % (name, i), shape, dt) for i in range(n)]
        self.i = 0

    def next(self):
        b = self.bufs[self.i % len(self.bufs)]
        self.i += 1
        return b


class LayerProg:
    def __init__(self, S, nc, li, kind, last, io, consts):
        self.S, self.nc, self.li, self.kind, self.last = S, nc, li, kind, last
        self.io = io
        self.c = consts
        self.ltiles = list(range(1, NTL)) if last else list(range(NTL))
        self.nscr = 0
        self.wkeys = {}

    def dram(self, name, shape, dt):
        self.nscr += 1
        return self.nc.dram_tensor("L%d_%s_%d" % (self.li, name, self.nscr), shape, dt).ap()

    def cast2d(self, name, src, piece_cols=2048):
        shape = list(src.shape)
        dst = self.dram(name, shape, BF16)
        nd = len(shape)
        names = " ".join("d%d" % i for i in range(nd))
        pat = "%s -> (%s)" % (names, names)
        fs = src.rearrange(pat).rearrange("(r c) -> r c", c=16384)
        fd = dst.rearrange(pat).rearrange("(r c) -> r c", c=16384)
        R = fs.shape[0]
        keys = []
        for r0 in range(0, R, 128):
            r1 = min(R, r0 + 128)
            k = (name, 'w', r0)
            self.S.dma('pool', fd[r0:r1, :], fs[r0:r1, :], writes=[k], bg=True)
            keys.append(k)
        self.wkeys[name] = keys
        return dst

    def load_w(self, wt, wb, key):
        kc = wt.shape[1]
        src = wb.rearrange("(kc p) n -> p kc n", p=128)
        step = max(1, kc // 4)
        for k0 in range(0, kc, step):
            self.S.dma('sp', wt[:, k0:k0 + step, :], src[:, k0:k0 + step, :], reads=self.wkeys[key], writes=[(wt.name, k0 // step)])

    @staticmethod
    def wk(wt, kc):
        step = max(1, wt.shape[1] // 4)
        return (wt.name, kc // step)

    def phase_mod(self):
        S, io = self.S, self.io
        mod = S.sb("mod", [128, 96, 2], F32)
        self.mod = mod
        with S.phase():
            cT = S.sb("cT", [128, 16, 2], F32)
            scT = S.sb("scT", [128, 16, 2], F32)
            adab = S.sb("adab", [128, 96], F32)
            S.dma('sp', cT[:], io['cT'].rearrange("p (k s) -> p k s", s=2), writes=[cT])
            S.dma('sp', adab[:], io['adabT'], writes=[adab])
            S.op('act', lambda e: e.activation(scT[:], cT[:], AF.Silu), reads=[cT], writes=[scT])
            psM = S.ps("psM", [128, 96, 2], F32)
            ring = Ring(S, "wa", 2, [128, 16, 512], F32)
            src = io['ada_w'].rearrange("(kc p) n -> p kc n", p=128)
            for ng in range(24):
                wa = ring.next()
                for k0 in range(0, 16, 4):
                    S.dma('sp', wa[:, k0:k0 + 4, :], src[:, k0:k0 + 4, ng * 512:(ng + 1) * 512], writes=[(wa.name, k0 // 4)])
                for nb in range(4):
                    q = ng * 4 + nb
                    for kc in range(16):
                        S.op('pe', lambda e, q=q, kc=kc, nb=nb, wa=wa: e.matmul(
                            psM[:, q, :], wa[:, kc, nb * 128:(nb + 1) * 128], scT[:, kc, :],
                            start=(kc == 0), stop=(kc == 15)), reads=[(wa.name, kc // 4), scT], writes=[psM])
            S.op('dve', lambda e: e.tensor_tensor(mod[:], psM[:], adab[:, :, None].to_broadcast([128, 96, 2]), ALU.add),
                 reads=[psM, adab], writes=[mod])
            for j in (1, 2, 4, 5):
                S.op('dve', lambda e, j=j: e.tensor_scalar_add(mod[:, j * 16:(j + 1) * 16, :], mod[:, j * 16:(j + 1) * 16, :], 1.0),
                     reads=[mod], writes=[mod])

    def mcol(self, j, c, s):
        return self.mod[:, j * 16 + c, s:s + 1]

    def bcast_rows(self, dst, j, s, psB_ring, diag_ring):
        S = self.S
        for g in range(4):
            psB = psB_ring.next()
            for cc in range(4):
                c = g * 4 + cc
                dg = diag_ring.next()
                S.op('dve', lambda e, dg=dg, c=c: e.tensor_scalar_mul(dg[:], self.c['ident_f'][:], self.mcol(j, c, s)),
                     reads=[self.c['ident_f'], self.mod], writes=[dg])
                S.op('pe', lambda e, dg=dg, cc=cc, psB=psB: e.matmul(psB[:, cc * 128:(cc + 1) * 128], self.c['ones_f'][:], dg[:],
                                                                      start=True, stop=True),
                     reads=[dg, self.c['ones_f']], writes=[psB])
            S.op('act', lambda e, g=g, psB=psB: e.activation(dst[:, g * 512:(g + 1) * 512], psB[:], AF.Copy),
                 reads=[psB], writes=[dst])

    def make_hT_tools(self, nx=2):
        S = self.S
        return dict(xt=Ring(S, "hx", nx, [128, D], F32), ps=Ring(S, "hps", 2, [128, 512], F32, psum=True))

    def load_hT(self, tools, src_tile, s, jshift, jscale, dst, col0=None, src_key=None):
        S = self.S
        xt = tools['xt'].next()
        S.dma('sp', xt[:], src_tile, reads=[src_key] if src_key else [], writes=[xt])
        for g in range(4):
            ps = tools['ps'].next()
            for cc in range(4):
                c = g * 4 + cc
                S.op('pe', lambda e, c=c, cc=cc, ps=ps, xt=xt: e.transpose(ps[:, cc * 128:(cc + 1) * 128], xt[:, c * 128:(c + 1) * 128],
                                                                        self.c['ident_f'][:]),
                     reads=[xt, self.c['ident_f']], writes=[ps])
            for cc in range(4):
                c = g * 4 + cc
                o = dst[:, c, :] if col0 is None else dst[:, c, col0:col0 + 128]
                S.op('act', lambda e, o=o, cc=cc, ps=ps, c=c: e.activation(o, ps[:, cc * 128:(cc + 1) * 128], AF.Identity,
                                                                         bias=self.mcol(jshift, c, s), scale=self.mcol(jscale, c, s)),
                     reads=[ps, self.mod], writes=[dst])

    def rope(self, xin, out_bf, tb, H, rd, t1, t2):
        S = self.S
        hw = rd // 4
        cosb = tb[:, 0:rd][:, None, :].to_broadcast([128, H, rd])
        S.op('dve', lambda e: e.tensor_tensor(t1, xin, cosb, ALU.mult), reads=[xin, tb], writes=[t1])
        xv = xin.rearrange("p h (a f w) -> p h a f w", a=2, f=2)
        tv = t2.rearrange("p h (a f w) -> p h a f w", a=2, f=2)
        sv = tb[:, rd:2 * rd].rearrange("p (a f w) -> p a f w", a=2, f=2)
        for a in range(2):
            for f in range(2):
                S.op('pool', lambda e, a=a, f=f: e.tensor_tensor(tv[:, :, a, f, :], xv[:, :, a, 1 - f, :],
                                                                 sv[:, a, f, :][:, None, :].to_broadcast([128, H, hw]), ALU.mult),
                     reads=[xin, tb], writes=[t2])
        S.op('dve', lambda e: e.tensor_tensor(out_bf, t1, t2, ALU.add), reads=[t1, t2], writes=[out_bf])

    def rstd_from_ss(self, ss, rstd, n, tmp):
        S = self.S
        S.op('act', lambda e: e.activation(tmp, ss, AF.Sqrt, bias=EPS, scale=1.0 / n), reads=[ss], writes=[tmp])
        S.op('dve', lambda e: e.reciprocal(rstd, tmp), reads=[tmp], writes=[rstd])

    def epilogue(self, yg, xt, lnG, lnB, out_ap, out_key, small):
        S = self.S
        st, mv, sd, rstd, nmr = small
        S.op('dve', lambda e: e.scalar_tensor_tensor(yg[:], xt[:], float(ALPHA), yg[:], ALU.mult, ALU.add),
             reads=[xt, yg], writes=[yg])
        for c in range(4):
            S.op('dve', lambda e, c=c: e.bn_stats(st[:, c, :], yg[:, c * 512:(c + 1) * 512]), reads=[yg], writes=[st])
        S.op('dve', lambda e: e.bn_aggr(mv[:], st[:]), reads=[st], writes=[mv])
        S.op('act', lambda e: e.activation(sd[:], mv[:, 1:2], AF.Sqrt, bias=EPS, scale=1.0), reads=[mv], writes=[sd])
        S.op('dve', lambda e: e.reciprocal(rstd[:], sd[:]), reads=[sd], writes=[rstd])
        S.op('dve', lambda e: e.scalar_tensor_tensor(nmr[:], mv[:, 0:1], -1.0, rstd[:], ALU.mult, ALU.mult),
             reads=[mv, rstd], writes=[nmr])
        S.op('act', lambda e: e.activation(yg[:], yg[:], AF.Identity, bias=nmr[:], scale=rstd[:]),
             reads=[yg, nmr, rstd], writes=[yg])
        S.op('dve', lambda e: e.tensor_tensor(yg[:], yg[:], lnG[:], ALU.mult), reads=[yg, lnG], writes=[yg])
        S.op('pool', lambda e: e.tensor_tensor(yg[:], yg[:], lnB[:], ALU.add), reads=[yg, lnB], writes=[yg])
        S.dma('sp', out_ap, yg[:], reads=[yg], writes=[out_key])

    def epi_small(self):
        S = self.S
        return (S.sb("st", [128, 4, 6], F32), S.sb("mv", [128, 2], F32), S.sb("sd", [128, 1], F32),
                S.sb("rstd", [128, 1], F32), S.sb("nmr", [128, 1], F32))

    def phase_out(self, oT_d, wo_b, wo_key, xloc, xmid):
        S = self.S
        with S.phase():
            Wo = S.sb("Wo", [128, 16, D], BF16)
            self.load_w(Wo, wo_b, wo_key)
            lnG = S.sb("lnG", [128, D], F32)
            lnB = S.sb("lnB", [128, D], F32)
            S.dma('sp', lnG[:], self.io['ln_g'][0:1, :].partition_broadcast(128), writes=[lnG])
            S.dma('sp', lnB[:], self.io['ln_b'][0:1, :].partition_broadcast(128), writes=[lnB])
            gB = [S.sb("gB0", [128, D], F32), S.sb("gB1", [128, D], F32)]
            psB_ring = Ring(S, "psB", 1, [128, 512], F32, psum=True)
            diag_ring = Ring(S, "diag", 2, [128, 128], F32)
            self.bcast_rows(gB[0], 2, 0, psB_ring, diag_ring)
            if not self.last:
                self.bcast_rows(gB[1], 2, 1, psB_ring, diag_ring)
            oring = Ring(S, "oT", 2, [128, 16, 128], BF16)
            xring = Ring(S, "xr", 2, [128, D], F32)
            yring = Ring(S, "yg", 2, [128, D], F32)
            psY = Ring(S, "psY", 1, [128, D], F32, psum=True)
            small = self.epi_small()
            for lt in self.ltiles:
                s = 1 if lt == 0 else 0
                oT = oring.next()
                S.dma('sp', oT[:], oT_d[:, :, lt * 128:(lt + 1) * 128].rearrange("h p t -> p h t"), reads=['oT_d'], writes=[oT])
                xt = xring.next()
                S.dma('sp', xt[:], xloc(lt), reads=[self.xl_key(lt)], writes=[xt])
                ps = psY.next()
                for ng in range(4):
                    for hd in range(16):
                        S.op('pe', lambda e, ng=ng, hd=hd, oT=oT, ps=ps: e.matmul(ps[:, ng * 512:(ng + 1) * 512], oT[:, hd, :],
                                                                                 Wo[:, hd, ng * 512:(ng + 1) * 512],
                                                                                 start=(hd == 0), stop=(hd == 15)),
                             reads=[oT, self.wk(Wo, hd)], writes=[ps])
                yg = yring.next()
                S.op('dve', lambda e, yg=yg, ps=ps, s=s: e.tensor_tensor(yg[:], ps[:], gB[s][:], ALU.mult),
                     reads=[ps, gB[s]], writes=[yg])
                self.epilogue(yg, xt, lnG, lnB, xmid[lt * 128:(lt + 1) * 128, :], ('xmid', lt), small)

    def phase_mlp(self, xmid, xout):
        S, io = self.S, self.io
        w1v = self.w1b.rearrange("(kc p) f -> p kc f", p=128)
        w2v = self.w2b.rearrange("(fc p) d -> p fc d", p=128)
        yd = self.dram("yd", [NTL * 128, D], F32)
        with S.phase():
            lnG = S.sb("lnG", [128, D], F32)
            lnB = S.sb("lnB", [128, D], F32)
            S.dma('sp', lnG[:], io['ln_g'][1:2, :].partition_broadcast(128), writes=[lnG])
            S.dma('sp', lnB[:], io['ln_b'][1:2, :].partition_broadcast(128), writes=[lnB])
            gB = S.sb("gB", [128, D], F32)
            psA_ring = Ring(S, "psA", 2, [128, 512], F32, psum=True)
            psO = [S.ps("psO%d" % i, [128, 512], F32) for i in range(4)]
            diag_ring = Ring(S, "diag", 2, [128, 128], F32)
            tools = self.make_hT_tools(nx=1)
            hT2 = S.sb("hT2", [128, 16, 512], BF16)
            aT = S.sb("aT", [128, 64, 512], BF16)
            w1r = Ring(S, "w1p", 2, [128, 16, 256], BF16)
            w2r = Ring(S, "w2p", 2, [128, 8, 512], BF16)
            sqr = Ring(S, "sq", 2, [128, 512], F32)
            ysr = Ring(S, "ys", 2, [128, 512], F32)
            xt_e = S.sb("xte", [128, D], F32)
            yg_e = S.sb("yge", [128, D], F32)
            small = self.epi_small()
            blocks = []
            if not self.last:
                blocks.append(([0], 1))
            for b in range(4):
                blocks.append((list(range(1 + 4 * b, 5 + 4 * b)), 0))
            cur_s = None
            for tiles, s in blocks:
                T = 128 * len(tiles)
                if cur_s != s:
                    self.bcast_rows(gB, 5, s, psA_ring, diag_ring)
                    cur_s = s
                for ti, lt in enumerate(tiles):
                    self.load_hT(tools, xmid[lt * 128:(lt + 1) * 128, :], s, 3, 4, hT2, col0=ti * 128, src_key=('xmid', lt))
                for fp in range(32):
                    w1p = w1r.next()
                    S.dma('sp', w1p[:], w1v[:, :, fp * 256:(fp + 1) * 256], reads=self.wkeys['w1b'], writes=[w1p])
                    for fcl in range(2):
                        fc = fp * 2 + fcl
                        ps = psA_ring.next()
                        for kc in range(16):
                            S.op('pe', lambda e, kc=kc, fcl=fcl, ps=ps, w1p=w1p: e.matmul(
                                ps[:, :T], w1p[:, kc, fcl * 128:(fcl + 1) * 128], hT2[:, kc, :T],
                                start=(kc == 0), stop=(kc == 15)), reads=[w1p, hT2], writes=[ps])
                        sq = sqr.next()
                        S.op('act', lambda e, sq=sq, ps=ps: e.activation(sq[:, :T], ps[:, :T], AF.Square), reads=[ps], writes=[sq])
                        S.op('dve', lambda e, sq=sq, ps=ps, fc=fc: e.scalar_tensor_tensor(aT[:, fc, :T], ps[:, :T], 0.0, sq[:, :T],
                                                                                        ALU.is_gt, ALU.mult),
                             reads=[ps, sq], writes=[aT])
                for dg in range(4):
                    for pc in range(8):
                        w2p = w2r.next()
                        S.dma('sp', w2p[:], w2v[:, pc * 8:(pc + 1) * 8, dg * 512:(dg + 1) * 512], reads=self.wkeys['w2b'], writes=[w2p])
                        for ti in range(len(tiles)):
                            for fcl in range(8):
                                fc = pc * 8 + fcl
                                S.op('pe', lambda e, ti=ti, fc=fc, fcl=fcl, w2p=w2p: e.matmul(
                                    psO[ti][:], aT[:, fc, ti * 128:(ti + 1) * 128], w2p[:, fcl, :],
                                    start=(fc == 0), stop=(fc == 63)), reads=[aT, w2p], writes=[psO[ti]])
                    for ti, lt in enumerate(tiles):
                        ys = ysr.next()
                        S.op('dve', lambda e, ys=ys, ti=ti, dg=dg: e.tensor_tensor(ys[:], psO[ti][:], gB[:, dg * 512:(dg + 1) * 512], ALU.mult),
                             reads=[psO[ti], gB], writes=[ys])
                        S.dma('sp', yd[lt * 128:(lt + 1) * 128, dg * 512:(dg + 1) * 512], ys[:], reads=[ys], writes=[('yd', lt)])
                for lt in tiles:
                    S.dma('sp', xt_e[:], xmid[lt * 128:(lt + 1) * 128, :], reads=[('xmid', lt)], writes=[xt_e])
                    S.dma('sp', yg_e[:], yd[lt * 128:(lt + 1) * 128, :], reads=[('yd', lt)], writes=[yg_e])
                    self.epilogue(yg_e, xt_e, lnG, lnB, xout(lt), self.xo_key(lt), small)
                    if self.on_tile_done is not None:
                        self.on_tile_done(lt)

    def attention(self, blocks, nheads, score_fn, v_fn, load_q_fn, oT_d, scale, pre_head=None, head_outer=False):
        S = self.S
        pT_ring = Ring(S, "pT", 4, [128, 512], BF16)
        psS = Ring(S, "psS", 2 if head_outer else 3, [128, 512], F32, psum=True)
        psOr = Ring(S, "psOa", 2, [128, 512], F32, psum=True)
        psSum = Ring(S, "psSum", 2, [128, 512], F32, psum=True)
        rec_ring = Ring(S, "rec", 2, [128, 512], F32)
        o_ring = Ring(S, "oblk", 2, [128, 512], BF16)
        accd_ring = Ring(S, "accd", 2, [128, 512], F32)
        accp_ring = Ring(S, "accp", 2, [128, 512], F32)
        ones_f = self.c['ones_f']
        order = []
        if head_outer:
            for hd in range(nheads):
                for bi in range(len(blocks)):
                    order.append((hd, bi))
        else:
            for bi in range(len(blocks)):
                for hd in range(nheads):
                    order.append((hd, bi))
        prev_hd, prev_bi = None, None
        for hd, bi in order:
            tiles, ktiles = blocks[bi]
            T = 128 * len(tiles)
            if head_outer and hd != prev_hd and pre_head is not None:
                pre_head(hd)
            qb = load_q_fn(hd, bi, tiles, new_block=(bi != prev_bi), new_head=(hd != prev_hd))
            prev_hd, prev_bi = hd, bi
            pso = psOr.next()
            pss = psSum.next()
            nk = len(ktiles)
            accs = (('dve', accd_ring.next()), ('pool', accp_ring.next()))
            pend = None

            def emit_pv(ki, kt, pT):
                vap, vkeys = v_fn(hd, kt)
                S.op('pe', lambda e: e.matmul(pso[:, :T], vap, pT[:, :T], start=(ki == 0), stop=(ki == nk - 1)),
                     reads=[pT] + vkeys, writes=[pso])
                aeng, acc = accs[ki % 2]
                if ki < 2:
                    S.op(aeng, lambda e: e.tensor_copy(acc[:, :T], pT[:, :T]), reads=[pT], writes=[acc])
                else:
                    S.op(aeng, lambda e: e.tensor_tensor(acc[:, :T], acc[:, :T], pT[:, :T], ALU.add), reads=[pT, acc], writes=[acc])

            for ki, kt in enumerate(ktiles):
                ps = psS.next()
                score_fn(ps, hd, kt, qb, T)
                if pend is not None:
                    emit_pv(*pend)
                pT = pT_ring.next()
                S.op('act', lambda e, pT=pT, ps=ps: e.activation(pT[:, :T], ps[:, :T], AF.Exp, scale=float(scale)),
                     reads=[ps], writes=[pT])
                pend = (ki, kt, pT)
            emit_pv(*pend)
            na = min(nk, 2)
            for ai in range(na):
                acc = accs[ai][1]
                S.op('pe', lambda e, acc=acc, pss=pss, ai=ai: e.matmul(pss[:, :T], ones_f[:], acc[:, :T], start=(ai == 0), stop=(ai == na - 1)),
                     reads=[acc, ones_f], writes=[pss])
            rec = rec_ring.next()
            S.op('dve', lambda e, rec=rec, pss=pss: e.reciprocal(rec[:, :T], pss[:, :T]), reads=[pss], writes=[rec])
            ob = o_ring.next()
            S.op('dve', lambda e, ob=ob, pso=pso, rec=rec: e.tensor_tensor(ob[:, :T], pso[:, :T], rec[:, :T], ALU.mult),
                 reads=[pso, rec], writes=[ob])
            t0 = tiles[0] * 128
            S.dma('sp', oT_d[hd, :, t0:t0 + T], ob[:, :T], reads=[ob], writes=['oT_d'])

    def mixer_gqa(self, xfull, xloc, oT_d):
        S, io = self.S, self.io
        wq_b = self.wb['wq']
        qT_d = self.dram("qT", [16, 128, NTL * 128], BF16)
        with S.phase():
            kT_all = S.sb("kT_all", [128, 4, NTF * 128], BF16)
            V_all = S.sb("V_all", [128, NTF, 512], BF16)
            with S.phase():
                Wkv = S.sb("Wkv", [128, 16, 1024], BF16)
                for nm, c0 in (('wk', 0), ('wv', 512)):
                    srcw = self.wb[nm].rearrange("(kc p) n -> p kc n", p=128)
                    for k0 in range(0, 16, 4):
                        S.dma('sp', Wkv[:, k0:k0 + 4, c0:c0 + 512], srcw[:, k0:k0 + 4, :], reads=self.wkeys[nm], writes=[(Wkv.name, k0 // 4, c0)])
                kgB = S.sb("kgB", [128, 128], F32)
                S.dma('sp', kgB[:], io['k_g'][0:1, :].partition_broadcast(128), writes=[kgB])
                tools = self.make_hT_tools()
                hring = Ring(S, "hT", 2, [128, 16, 128], BF16)
                psKV = Ring(S, "psKV", 1, [128, 1024], F32, psum=True)
                psKT = Ring(S, "psKT", 1, [128, 4, 128], BF16, psum=True)
                tbr = Ring(S, "tb", 2, [128, 256], F32)
                sqk = S.sb("sqk", [128, 512], F32)
                ss = S.sb("ss", [128, 4], F32); tmp4 = S.sb("tmp4", [128, 4], F32); rstd = S.sb("rstd4", [128, 4], F32)
                kn = S.sb("kn", [128, 4, 128], F32); t1 = S.sb("t1", [128, 4, 128], F32); t2 = S.sb("t2", [128, 4, 128], F32)
                kr_ring = Ring(S, "kr", 2, [128, 4, 128], BF16)
                for t in range(NTF):
                    s = 1 if t in CTXF else 0
                    hT = hring.next()
                    self.load_hT(tools, xfull(t), s, 0, 1, hT, src_key=self.xf_key(t))
                    ps = psKV.next()
                    for half in range(2):
                        for kc in range(16):
                            S.op('pe', lambda e, half=half, kc=kc, hT=hT, ps=ps: e.matmul(
                                ps[:, half * 512:(half + 1) * 512], hT[:, kc, :], Wkv[:, kc, half * 512:(half + 1) * 512],
                                start=(kc == 0), stop=(kc == 15)), reads=[hT, (Wkv.name, kc // 4, half * 512)], writes=[ps])
                    S.op('act', lambda e, t=t, ps=ps: e.activation(V_all[:, t, :], ps[:, 512:1024], AF.Copy), reads=[ps], writes=[V_all])
                    S.op('act', lambda e, ps=ps: e.activation(sqk[:], ps[:, 0:512], AF.Square), reads=[ps], writes=[sqk])
                    S.op('dve', lambda e: e.reduce_sum(ss[:], sqk[:].rearrange("p (h d) -> p h d", h=4), AX.X), reads=[sqk], writes=[ss])
                    self.rstd_from_ss(ss[:], rstd[:], 128, tmp4[:])
                    S.op('dve', lambda e, ps=ps: e.tensor_tensor(kn[:], ps[:, 0:512].rearrange("p (h d) -> p h d", h=4),
                                                                rstd[:, :, None].to_broadcast([128, 4, 128]), ALU.mult),
                         reads=[ps, rstd], writes=[kn])
                    S.op('dve', lambda e: e.tensor_tensor(kn[:], kn[:], kgB[:, None, :].to_broadcast([128, 4, 128]), ALU.mult),
                         reads=[kn, kgB], writes=[kn])
                    tb = tbr.next()
                    S.dma('sp', tb[:], io['ropeF'][t * 128:(t + 1) * 128, :], writes=[tb])
                    kr = kr_ring.next()
                    self.rope(kn[:], kr[:], tb, 4, 128, t1[:], t2[:])
                    pk = psKT.next()
                    for g in range(4):
                        S.op('pe', lambda e, g=g, kr=kr, pk=pk: e.transpose(pk[:, g, :], kr[:, g, :], self.c['ident_b'][:]),
                             reads=[kr, self.c['ident_b']], writes=[pk])
                    S.op('act', lambda e, t=t, pk=pk: e.activation(kT_all[:, :, t * 128:(t + 1) * 128], pk[:], AF.Copy), reads=[pk], writes=[kT_all])
            with S.phase():
                Wq = S.sb("Wq", [128, 16, D], BF16)
                self.load_w(Wq, wq_b, 'wq')
                qgB = S.sb("qgB", [128, 128], F32)
                S.dma('sp', qgB[:], io['q_g'][0:1, :].partition_broadcast(128), writes=[qgB])
                tools = self.make_hT_tools(nx=1)
                hring = Ring(S, "hT", 2, [128, 16, 128], BF16)
                psQ = Ring(S, "psQ", 1, [128, D], F32, psum=True)
                psQT = Ring(S, "psQT", 1, [128, 16, 128], BF16, psum=True)
                tbr = Ring(S, "tb", 2, [128, 256], F32)
                sq = S.sb("sq", [128, D], F32)
                ss = S.sb("ss", [128, 16], F32); tmp16 = S.sb("tmp16", [128, 16], F32); rstd = S.sb("rstd16", [128, 16], F32)
                qn = S.sb("qn", [128, 16, 128], F32); t1 = S.sb("t1", [128, 16, 128], F32)
                qr_ring = Ring(S, "qr", 2, [128, 16, 128], BF16)
                qo_ring = Ring(S, "qo", 2, [128, 16, 128], BF16)
                for lt in self.ltiles:
                    s = 1 if lt == 0 else 0
                    hT = hring.next()
                    self.load_hT(tools, xloc(lt), s, 0, 1, hT, src_key=self.xl_key(lt))
                    ps = psQ.next()
                    for ng in range(4):
                        for kc in range(16):
                            S.op('pe', lambda e, ng=ng, kc=kc, hT=hT, ps=ps: e.matmul(
                                ps[:, ng * 512:(ng + 1) * 512], hT[:, kc, :], Wq[:, kc, ng * 512:(ng + 1) * 512],
                                start=(kc == 0), stop=(kc == 15)), reads=[hT, self.wk(Wq, kc)], writes=[ps])
                    S.op('act', lambda e, ps=ps: e.activation(sq[:], ps[:], AF.Square), reads=[ps], writes=[sq])
                    S.op('dve', lambda e: e.reduce_sum(ss[:], sq[:].rearrange("p (h d) -> p h d", h=16), AX.X), reads=[sq], writes=[ss])
                    self.rstd_from_ss(ss[:], rstd[:], 128, tmp16[:])
                    S.op('dve', lambda e, ps=ps: e.tensor_tensor(qn[:], ps[:].rearrange("p (h d) -> p h d", h=16),
                                                                rstd[:, :, None].to_broadcast([128, 16, 128]), ALU.mult),
                         reads=[ps, rstd], writes=[qn])
                    S.op('dve', lambda e: e.tensor_tensor(qn[:], qn[:], qgB[:, None, :].to_broadcast([128, 16, 128]), ALU.mult),
                         reads=[qn, qgB], writes=[qn])
                    tb = tbr.next()
                    S.dma('sp', tb[:], io['ropeL'][lt * 128:(lt + 1) * 128, :], writes=[tb])
                    qr = qr_ring.next()
                    t2 = sq[:].rearrange("p (h d) -> p h d", h=16)
                    self.rope(qn[:], qr[:], tb, 16, 128, t1[:], t2)
                    pq = psQT.next()
                    for hd in range(16):
                        S.op('pe', lambda e, hd=hd, qr=qr, pq=pq: e.transpose(pq[:, hd, :], qr[:, hd, :], self.c['ident_b'][:]),
                             reads=[qr, self.c['ident_b']], writes=[pq])
                    qo = qo_ring.next()
                    S.op('act', lambda e, qo=qo, pq=pq: e.activation(qo[:], pq[:], AF.Copy), reads=[pq], writes=[qo])
                    S.dma('sp', qT_d[:, :, lt * 128:(lt + 1) * 128].rearrange("h p t -> p h t"), qo[:], reads=[qo], writes=['qT_d'])
            with S.phase():
                blocks = []
                if not self.last:
                    blocks.append(([0], list(CTXF)))
                for b in range(4):
                    blocks.append((list(range(1 + 4 * b, 5 + 4 * b)), list(range(NTF))))
                qring = Ring(S, "qblk", 2, [128, 16, 512], BF16)
                state = {}

                def load_q(hd, bi, tiles, new_block, new_head):
                    if new_block:
                        qb = qring.next()
                        T = 128 * len(tiles)
                        t0 = tiles[0] * 128
                        for h0 in range(0, 16, 4):
                            S.dma('sp', qb[:, h0:h0 + 4, :T], qT_d[h0:h0 + 4, :, t0:t0 + T].rearrange("h p t -> p h t"),
                                  reads=['qT_d'], writes=[qb])
                        state['qb'] = qb
                    return state['qb']

                def score(ps, hd, kt, qb, T):
                    g = hd // 4
                    S.op('pe', lambda e: e.matmul(ps[:, :T], kT_all[:, g, kt * 128:(kt + 1) * 128], qb[:, hd, :T], start=True, stop=True),
                         reads=[kT_all, qb], writes=[ps])

                def vfn(hd, kt):
                    g = hd // 4
                    return V_all[:, kt, g * 128:(g + 1) * 128], [V_all]
                self.attention(blocks, 16, score, vfn, load_q, oT_d, 128 ** -0.5)
        return self.wb['wo'], 'wo'

    def prep_gqa(self):
        io = self.io
        self.wb = {'wk': self.cast2d("wk", io['wk']), 'wv': self.cast2d("wv", io['wv']),
                   'wq': self.cast2d("wq", io['wq']), 'wo': self.cast2d("wo", io['wo'])}

    def prep_mlp(self):
        io = self.io
        self.w1b = self.cast2d("w1b", io['mlp_w1'])
        self.w2b = self.cast2d("w2b", io['mlp_w2'])

    def prep_lru(self):
        io = self.io
        self.wb = {'wx': self.cast2d("wx", io['wx']), 'wy': self.cast2d("wy", io['wy']), 'wo': self.cast2d("wo", io['wo']),
                   'ra_w': self.cast2d("ra_w", io['ra_w']), 'ix_w': self.cast2d("ix_w", io['ix_w'])}

    def mixer_lru(self, xfull, xloc, oT_d):
        S, io = self.S, self.io
        uT_d = self.dram("uT", [16, 128, NTF * 128], F32)
        yT_d = self.dram("yT", [16, 128, NTL * 128], F32)
        NT = NTF * 128
        with S.phase():
            Wx = S.sb("Wx", [128, 16, D], BF16)
            self.load_w(Wx, self.wb['wx'], 'wx')
            tools = self.make_hT_tools()
            hring = Ring(S, "hTb", 2, [128, 16, 512], BF16)
            psU = Ring(S, "psU", 2, [128, 512], F32, psum=True)
            ust = Ring(S, "ust", 3, [128, 512], F32)
            blocks = [([0, 17], 1, 0)]
            for b in range(8):
                f0 = 1 + 4 * b if b < 4 else 18 + 4 * (b - 4)
                blocks.append((list(range(f0, f0 + 4)), 0, 256 + 512 * b))
            for tiles, s, t0 in blocks:
                T = 128 * len(tiles)
                hb = hring.next()
                for ti, t in enumerate(tiles):
                    self.load_hT(tools, xfull(t), s, 0, 1, hb, col0=ti * 128, src_key=self.xf_key(t))
                for cc in range(16):
                    ps = psU.next()
                    for kc in range(16):
                        S.op('pe', lambda e, kc=kc, cc=cc, ps=ps, hb=hb: e.matmul(ps[:, :T], Wx[:, kc, cc * 128:(cc + 1) * 128], hb[:, kc, :T],
                                                                                 start=(kc == 0), stop=(kc == 15)),
                             reads=[self.wk(Wx, kc), hb], writes=[ps])
                    u = ust.next()
                    S.op('act', lambda e, u=u, ps=ps: e.activation(u[:, :T], ps[:, :T], AF.Copy), reads=[ps], writes=[u])
                    S.dma('sp', uT_d[cc, :, t0:t0 + T], u[:, :T], reads=[u], writes=[('uT', cc)])
        with S.phase():
            convw = S.sb("convw", [128, 16, 4], F32); convb = S.sb("convb", [128, 16], F32)
            rab = S.sb("rab", [128, 16, 2], F32); ixb = S.sb("ixb", [128, 16, 2], F32)
            lam = S.sb("lam", [128, 16, 2], F32); cd = S.sb("cd", [128, 16, 2], F32)
            ee = S.sb("ee", [128, 16, 2], F32); tt = S.sb("tt", [128, 16, 2], F32)
            hmask = S.sb("hmask", [128, 2], F32)
            for tl, nm in ((convw, 'convwT'), (convb, 'convbT'), (rab, 'rabT'), (ixb, 'ixbT'), (lam, 'lamT'), (hmask, 'hmask')):
                S.dma('sp', tl[:], io[nm], writes=[tl])
            S.op('act', lambda e: e.activation(ee[:], lam[:], AF.Exp, scale=-1.0), reads=[lam], writes=[ee])
            S.op('dve', lambda e: e.tensor_scalar(tt[:], ee[:], -0.25, 1.0 / 3.0, ALU.mult, ALU.add), reads=[ee], writes=[tt])
            S.op('dve', lambda e: e.tensor_tensor(tt[:], tt[:], ee[:], ALU.mult), reads=[tt, ee], writes=[tt])
            S.op('dve', lambda e: e.tensor_scalar(tt[:], tt[:], -1.0, 0.5, ALU.mult, ALU.add), reads=[tt], writes=[tt])
            S.op('dve', lambda e: e.tensor_tensor(tt[:], tt[:], ee[:], ALU.mult), reads=[tt, ee], writes=[tt])
            S.op('dve', lambda e: e.tensor_scalar(tt[:], tt[:], -1.0, 1.0, ALU.mult, ALU.add), reads=[tt], writes=[tt])
            S.op('dve', lambda e: e.tensor_tensor(tt[:], tt[:], ee[:], ALU.mult), reads=[tt, ee], writes=[tt])
            S.op('dve', lambda e: e.tensor_scalar_mul(cd[:], tt[:], -8.0), reads=[tt], writes=[cd])
            up = S.sb("up", [128, 4360], F32)
            S.op('pool', lambda e: e.memset(up[:], 0.0), writes=[up])
            uc = S.sb("uc", [128, 2, NT], F32)
            ucb = S.sb("ucb", [128, 2, NT], BF16)
            R = S.sb("R", [128, NT], F32); G = S.sb("G", [128, NT], F32); M = S.sb("M", [128, NT], F32)
            Y = S.sb("Y", [128, NT], F32); H = S.sb("H", [128, NT], F32)
            ysel = S.sb("ysel", [128, NTL * 128], F32)
            gwr = Ring(S, "gw", 2, [128, 2, 2, 2, 256], BF16)
            psR = Ring(S, "psR", 2, [128, 512], F32, psum=True)
            psI = Ring(S, "psI", 2, [128, 512], F32, psum=True)
            tblocks = [(t0, min(512, NT - t0)) for t0 in range(0, NT, 512)]
            for n in range(8):
                gw = gwr.next()
                for d in range(2):
                    for m, nm in enumerate(('ra_w', 'ix_w')):
                        S.dma('sp', gw[:, :, d, m, :], self.wb[nm][d, n].rearrange("(kk p) j -> p kk j", p=128),
                              reads=self.wkeys[nm], writes=[(gw.name, d, m)])
                for oc in range(2):
                    cc = 2 * n + oc
                    S.dma('sp', up[:, 2:258], uT_d[cc, :, 0:256], reads=[('uT', cc)], writes=[up])
                    S.dma('sp', up[:, 262:4358], uT_d[cc, :, 256:NT], reads=[('uT', cc)], writes=[up])
                    for (o0, n_, u0) in ((0, 256, 0), (256, 4096, 260)):
                        S.op('dve', lambda e, o0=o0, n_=n_, u0=u0, oc=oc, cc=cc: e.tensor_scalar(
                            uc[:, oc, o0:o0 + n_], up[:, u0:u0 + n_], convw[:, cc, 0:1], convb[:, cc:cc + 1], ALU.mult, ALU.add),
                            reads=[up, convw, convb], writes=[uc])
                        for j in range(1, 4):
                            S.op('dve', lambda e, o0=o0, n_=n_, u0=u0, oc=oc, cc=cc, j=j: e.scalar_tensor_tensor(
                                uc[:, oc, o0:o0 + n_], up[:, u0 + j:u0 + j + n_], convw[:, cc, j:j + 1], uc[:, oc, o0:o0 + n_],
                                ALU.mult, ALU.add), reads=[up, convw, uc], writes=[uc])
                    S.op('pool', lambda e, oc=oc: e.tensor_copy(ucb[:, oc, :], uc[:, oc, :]), reads=[uc], writes=[ucb])
                for oc in range(2):
                    cc = 2 * n + oc
                    for d in range(2):
                        for (t0, T) in tblocks:
                            pr = psR.next(); pi = psI.next()
                            for kk in range(2):
                                S.op('pe', lambda e, kk=kk, d=d, oc=oc, pr=pr, t0=t0, T=T, gw=gw: e.matmul(
                                    pr[:, :T], gw[:, kk, d, 0, oc * 128:(oc + 1) * 128], ucb[:, kk, t0:t0 + T],
                                    start=(kk == 0), stop=(kk == 1)), reads=[(gw.name, d, 0), ucb], writes=[pr])
                            for kk in range(2):
                                S.op('pe', lambda e, kk=kk, d=d, oc=oc, pi=pi, t0=t0, T=T, gw=gw: e.matmul(
                                    pi[:, :T], gw[:, kk, d, 1, oc * 128:(oc + 1) * 128], ucb[:, kk, t0:t0 + T],
                                    start=(kk == 0), stop=(kk == 1)), reads=[(gw.name, d, 1), ucb], writes=[pi])
                            S.op('act', lambda e, pr=pr, t0=t0, T=T, cc=cc, d=d: e.activation(R[:, t0:t0 + T], pr[:, :T], AF.Sigmoid,
                                                                                           bias=rab[:, cc, d:d + 1]),
                                 reads=[pr, rab], writes=[R])
                            S.op('act', lambda e, pi=pi, t0=t0, T=T, cc=cc, d=d: e.activation(G[:, t0:t0 + T], pi[:, :T], AF.Sigmoid,
                                                                                           bias=ixb[:, cc, d:d + 1]),
                                 reads=[pi, ixb], writes=[G])
                        S.op('act', lambda e, cc=cc, d=d: e.activation(R[:], R[:], AF.Exp, scale=cd[:, cc, d:d + 1]), reads=[R, cd], writes=[R])
                        S.op('pool', lambda e: e.tensor_tensor(M[:], R[:], R[:], ALU.mult), reads=[R], writes=[M])
                        S.op('act', lambda e: e.activation(M[:], M[:], AF.Sqrt, bias=1.0, scale=-1.0), reads=[M], writes=[M])
                        S.op('dve', lambda e, oc=oc: e.tensor_tensor(G[:], G[:], uc[:, oc, :], ALU.mult), reads=[G, uc], writes=[G])
                        S.op('pool', lambda e: e.tensor_tensor(G[:], G[:], M[:], ALU.mult), reads=[G, M], writes=[G])
                        if d == 0:
                            S.op('dve', lambda e: e.tensor_tensor_scan(Y[:, 0:256], R[:, 0:256], G[:, 0:256], 0.0, ALU.mult, ALU.add),
                                 reads=[R, G], writes=[Y])
                            S.op('dve', lambda e: e.tensor_tensor_scan(Y[:, 256:NT], R[:, 256:NT], G[:, 256:NT], Y[:, 255:256], ALU.mult, ALU.add),
                                 reads=[R, G, Y], writes=[Y])
                        else:
                            S.op('dve', lambda e: e.tensor_tensor_scan(H[:, 0:256][:, ::-1], R[:, 0:256][:, ::-1], G[:, 0:256][:, ::-1], 0.0,
                                                                       ALU.mult, ALU.add), reads=[R, G], writes=[H])
                            S.op('dve', lambda e: e.tensor_tensor_scan(H[:, 256:NT][:, ::-1], R[:, 256:NT][:, ::-1], G[:, 256:NT][:, ::-1],
                                                                       H[:, 0:1], ALU.mult, ALU.add), reads=[R, G, H], writes=[H])
                            S.op('pool', lambda e: e.tensor_tensor(Y[:], Y[:], H[:], ALU.add), reads=[Y, H], writes=[Y])
                    for (o0, n_, a0, a1) in ((0, 128, 0, 128), (128, 2048, 256, 2304)):
                        S.op('dve', lambda e, o0=o0, n_=n_, a0=a0: e.tensor_scalar_mul(ysel[:, o0:o0 + n_], Y[:, a0:a0 + n_], hmask[:, 0:1]),
                             reads=[Y, hmask], writes=[ysel])
                        S.op('dve', lambda e, o0=o0, n_=n_, a1=a1: e.scalar_tensor_tensor(ysel[:, o0:o0 + n_], Y[:, a1:a1 + n_], hmask[:, 1:2],
                                                                                         ysel[:, o0:o0 + n_], ALU.mult, ALU.add),
                             reads=[Y, hmask, ysel], writes=[ysel])
                    S.dma('sp', yT_d[cc], ysel[:], reads=[ysel], writes=[('yT', cc)])
        with S.phase():
            Wy = S.sb("Wy", [128, 16, D], BF16)
            self.load_w(Wy, self.wb['wy'], 'wy')
            tools = self.make_hT_tools()
            hring = Ring(S, "hTb", 2, [128, 16, 512], BF16)
            psG = Ring(S, "psG", 2, [128, 512], F32, psum=True)
            gst = Ring(S, "gst", 2, [128, 512], F32)
            ysl = Ring(S, "ysl", 2, [128, 512], F32)
            zb = Ring(S, "zb", 2, [128, 512], BF16)
            blocks = []
            if not self.last:
                blocks.append(([0], 1))
            for b in range(4):
                blocks.append((list(range(1 + 4 * b, 5 + 4 * b)), 0))
            for tiles, s in blocks:
                T = 128 * len(tiles)
                t0 = tiles[0] * 128
                hb = hring.next()
                for ti, lt in enumerate(tiles):
                    self.load_hT(tools, xloc(lt), s, 0, 1, hb, col0=ti * 128, src_key=self.xl_key(lt))
                for cc in range(16):
                    ps = psG.next()
                    for kc in range(16):
                        S.op('pe', lambda e, kc=kc, cc=cc, ps=ps, hb=hb: e.matmul(ps[:, :T], Wy[:, kc, cc * 128:(cc + 1) * 128], hb[:, kc, :T],
                                                                                 start=(kc == 0), stop=(kc == 15)),
                             reads=[self.wk(Wy, kc), hb], writes=[ps])
                    g = gst.next()
                    S.op('act', lambda e, g=g, ps=ps: e.activation(g[:, :T], ps[:, :T], AF.Gelu_apprx_tanh), reads=[ps], writes=[g])
                    yl = ysl.next()
                    S.dma('sp', yl[:, :T], yT_d[cc, :, t0:t0 + T], reads=[('yT', cc)], writes=[yl])
                    z = zb.next()
                    S.op('dve', lambda e, z=z, g=g, yl=yl: e.tensor_tensor(z[:, :T], g[:, :T], yl[:, :T], ALU.mult), reads=[g, yl], writes=[z])
                    S.dma('sp', oT_d[cc, :, t0:t0 + T], z[:, :T], reads=[z], writes=['oT_d'])
        return self.wb['wo'], 'wo'

    def prep_mla(self):
        io = self.io
        self.wb = {'wq_a': self.cast2d("wq_a", io['wq_a']), 'wq_b': self.cast2d("wq_b", io['wq_b'], piece_cols=1536),
                   'wkv_a': self.cast2d("wkv_a", io['wkv_a']), 'wkv_b': self.cast2d("wkv_b", io['wkv_b']),
                   'wo': self.cast2d("wo", io['wo'])}

    def mixer_mla(self, xfull, xloc, oT_d):
        S, io = self.S, self.io
        NT = NTF * 128
        qnT_d = self.dram("qnT", [16, 128, NTL * 128], BF16)
        qpT_d = self.dram("qpT", [16, 64, NTL * 128], BF16)
        with S.phase():
            ckvT = S.sb("ckvT", [128, 4, NT], BF16)
            kpeT = S.sb("kpeT", [64, NT], BF16)
            with S.phase():
                Wkva = S.sb("Wkva", [128, 16, 576], BF16)
                self.load_w(Wkva, self.wb['wkv_a'], 'wkv_a')
                kvgB = S.sb("kvgB", [128, 512], F32)
                S.dma('sp', kvgB[:], io['kv_a_g'][0:1, :].partition_broadcast(128), writes=[kvgB])
                tools = self.make_hT_tools()
                hring = Ring(S, "hT", 2, [128, 16, 128], BF16)
                psC = Ring(S, "psC", 1, [128, 1024], F32, psum=True)
                psCT = Ring(S, "psCT", 1, [128, 4, 128], BF16, psum=True)
                psKP = Ring(S, "psKP", 1, [128, 1024], BF16, psum=True)
                tbr = Ring(S, "tb", 2, [128, 128], F32)
                sqc = S.sb("sqc", [128, 512], F32)
                ss = S.sb("ss", [128, 1], F32); tmp1 = S.sb("tmp1", [128, 1], F32); rstd = S.sb("rstd1", [128, 1], F32)
                cnr = Ring(S, "cn", 2, [128, 512], BF16)
                kp = S.sb("kp", [128, 1, 64], F32); t1 = S.sb("t1", [128, 1, 64], F32); t2 = S.sb("t2", [128, 1, 64], F32)
                krr = Ring(S, "kr", 2, [128, 1, 64], BF16)
                for t in range(NTF):
                    s = 1 if t in CTXF else 0
                    hT = hring.next()
                    self.load_hT(tools, xfull(t), s, 0, 1, hT, src_key=self.xf_key(t))
                    ps = psC.next()
                    for (c0, c1) in ((0, 512), (512, 576)):
                        for kc in range(16):
                            S.op('pe', lambda e, kc=kc, c0=c0, c1=c1, hT=hT, ps=ps: e.matmul(ps[:, c0:c1], hT[:, kc, :], Wkva[:, kc, c0:c1],
                                                                                            start=(kc == 0), stop=(kc == 15)),
                                 reads=[hT, self.wk(Wkva, kc)], writes=[ps])
                    S.op('act', lambda e, ps=ps: e.activation(sqc[:], ps[:, 0:512], AF.Square, accum_out=ss[:]), reads=[ps], writes=[sqc, ss])
                    self.rstd_from_ss(ss[:], rstd[:], 512, tmp1[:])
                    cn = cnr.next()
                    S.op('dve', lambda e, cn=cn, ps=ps: e.scalar_tensor_tensor(cn[:], ps[:, 0:512], rstd[:], kvgB[:], ALU.mult, ALU.mult),
                         reads=[ps, rstd, kvgB], writes=[cn])
                    S.op('act', lambda e, ps=ps: e.activation(kp[:, 0, :], ps[:, 512:576], AF.Copy), reads=[ps], writes=[kp])
                    tb = tbr.next()
                    S.dma('sp', tb[:], io['ropeF'][t * 128:(t + 1) * 128, :], writes=[tb])
                    kr = krr.next()
                    self.rope(kp[:], kr[:], tb, 1, 64, t1[:], t2[:])
                    pc = psCT.next()
                    for c in range(4):
                        S.op('pe', lambda e, c=c, cn=cn, pc=pc: e.transpose(pc[:, c, :], cn[:, c * 128:(c + 1) * 128], self.c['ident_b'][:]),
                             reads=[cn, self.c['ident_b']], writes=[pc])
                    S.op('act', lambda e, t=t, pc=pc: e.activation(ckvT[:, :, t * 128:(t + 1) * 128], pc[:], AF.Copy), reads=[pc], writes=[ckvT])
                    pk = psKP.next()
                    S.op('pe', lambda e, kr=kr, pk=pk: e.transpose(pk[0:64, 0:128], kr[:, 0, :], self.c['ident_b'][:]),
                         reads=[kr, self.c['ident_b']], writes=[pk])
                    S.op('dve', lambda e, t=t, pk=pk: e.tensor_copy(kpeT[:, t * 128:(t + 1) * 128], pk[0:64, 0:128]), reads=[pk], writes=[kpeT])
            with S.phase():
                Wqa = S.sb("Wqa", [128, 16, 512], BF16)
                self.load_w(Wqa, self.wb['wq_a'], 'wq_a')
                Wqb = S.sb("Wqb", [128, 4, 3072], BF16)
                self.load_w(Wqb, self.wb['wq_b'], 'wq_b')
                qagB = S.sb("qagB", [128, 512], F32)
                S.dma('sp', qagB[:], io['q_a_g'][0:1, :].partition_broadcast(128), writes=[qagB])
                tools = self.make_hT_tools()
                hring = Ring(S, "hT", 2, [128, 16, 128], BF16)
                psA = Ring(S, "psA", 1, [128, 512], F32, psum=True)
                psAT = Ring(S, "psAT", 1, [128, 4, 128], BF16, psum=True)
                psQh = Ring(S, "psQh", 1, [128, 1536], F32, psum=True)
                psT8 = Ring(S, "psT8", 1, [128, 8, 128], BF16, psum=True)
                tbr = Ring(S, "tb", 2, [128, 128], F32)
                sqa = S.sb("sqa", [128, 512], F32)
                ss = S.sb("ss", [128, 1], F32); tmp1 = S.sb("tmp1", [128, 1], F32); rstd = S.sb("rstd1", [128, 1], F32)
                qar = Ring(S, "qa", 2, [128, 512], BF16)
                qaTr = Ring(S, "qaT", 2, [128, 4, 128], BF16)
                qn8r = Ring(S, "qn8", 2, [128, 8, 128], BF16)
                pe8 = S.sb("pe8", [128, 8, 64], F32); t1 = S.sb("t1", [128, 8, 64], F32); t2 = S.sb("t2", [128, 8, 64], F32)
                qp8r = Ring(S, "qp8", 2, [128, 8, 64], BF16)
                qnor = Ring(S, "qno", 2, [128, 8, 128], BF16)
                qpor = Ring(S, "qpo", 2, [64, 8, 128], BF16)
                for lt in self.ltiles:
                    s = 1 if lt == 0 else 0
                    hT = hring.next()
                    self.load_hT(tools, xloc(lt), s, 0, 1, hT, src_key=self.xl_key(lt))
                    ps = psA.next()
                    for kc in range(16):
                        S.op('pe', lambda e, kc=kc, hT=hT, ps=ps: e.matmul(ps[:], hT[:, kc, :], Wqa[:, kc, :], start=(kc == 0), stop=(kc == 15)),
                             reads=[hT, self.wk(Wqa, kc)], writes=[ps])
                    S.op('act', lambda e, ps=ps: e.activation(sqa[:], ps[:], AF.Square, accum_out=ss[:]), reads=[ps], writes=[sqa, ss])
                    self.rstd_from_ss(ss[:], rstd[:], 512, tmp1[:])
                    qa = qar.next()
                    S.op('dve', lambda e, qa=qa, ps=ps: e.scalar_tensor_tensor(qa[:], ps[:], rstd[:], qagB[:], ALU.mult, ALU.mult),
                         reads=[ps, rstd, qagB], writes=[qa])
                    pa = psAT.next()
                    for c in range(4):
                        S.op('pe', lambda e, c=c, qa=qa, pa=pa: e.transpose(pa[:, c, :], qa[:, c * 128:(c + 1) * 128], self.c['ident_b'][:]),
                             reads=[qa, self.c['ident_b']], writes=[pa])
                    qaT = qaTr.next()
                    S.op('act', lambda e, qaT=qaT, pa=pa: e.activation(qaT[:], pa[:], AF.Copy), reads=[pa], writes=[qaT])
                    tb = tbr.next()
                    S.dma('sp', tb[:], io['ropeL'][lt * 128:(lt + 1) * 128, :], writes=[tb])
                    for half in range(2):
                        pq = psQh.next()
                        for ng in range(3):
                            c0 = half * 1536 + ng * 512
                            for kc in range(4):
                                S.op('pe', lambda e, ng=ng, kc=kc, c0=c0, pq=pq, qaT=qaT: e.matmul(
                                    pq[:, ng * 512:(ng + 1) * 512], qaT[:, kc, :], Wqb[:, kc, c0:c0 + 512], start=(kc == 0), stop=(kc == 3)),
                                    reads=[qaT, self.wk(Wqb, kc)], writes=[pq])
                        pv = pq[:].rearrange("p (h d) -> p h d", h=8)
                        qn8 = qn8r.next()
                        S.op('act', lambda e, qn8=qn8, pv=pv: e.activation(qn8[:], pv[:, :, 0:128], AF.Copy), reads=[pq], writes=[qn8])
                        S.op('act', lambda e, pv=pv: e.activation(pe8[:], pv[:, :, 128:192], AF.Copy), reads=[pq], writes=[pe8])
                        qp8 = qp8r.next()
                        self.rope(pe8[:], qp8[:], tb, 8, 64, t1[:], t2[:])
                        pt = psT8.next()
                        for h in range(8):
                            S.op('pe', lambda e, h=h, qn8=qn8, pt=pt: e.transpose(pt[:, h, :], qn8[:, h, :], self.c['ident_b'][:]),
                                 reads=[qn8, self.c['ident_b']], writes=[pt])
                        qno = qnor.next()
                        S.op('dve', lambda e, qno=qno, pt=pt: e.tensor_copy(qno[:], pt[:]), reads=[pt], writes=[qno])
                        S.dma('sp', qnT_d[half * 8:(half + 1) * 8, :, lt * 128:(lt + 1) * 128].rearrange("h p t -> p h t"), qno[:],
                              reads=[qno], writes=['qnT_d'])
                        for h in range(8):
                            S.op('pe', lambda e, h=h, qp8=qp8, pt=pt: e.transpose(pt[0:64, h, :], qp8[:, h, :], self.c['ident_b'][:]),
                                 reads=[qp8, self.c['ident_b']], writes=[pt])
                        qpo = qpor.next()
                        S.op('dve', lambda e, qpo=qpo, pt=pt: e.tensor_copy(qpo[:], pt[0:64, :, :]), reads=[pt], writes=[qpo])
                        S.dma('sp', qpT_d[half * 8:(half + 1) * 8, :, lt * 128:(lt + 1) * 128].rearrange("h p t -> p h t"), qpo[:],
                              reads=[qpo], writes=['qpT_d'])
            with S.phase():
                Wkvb = S.sb("Wkvb", [128, 4, 4096], BF16)
                self.load_w(Wkvb, self.wb['wkv_b'], 'wkv_b')
                knTr = Ring(S, "knT", 2, [128, NT], BF16)
                Vhr = Ring(S, "Vh", 2, [128, NTF, 128], BF16)
                psKV = Ring(S, "psKVm", 2, [128, 512], F32, psum=True)
                qnr = Ring(S, "qnb", 2, [128, 512], BF16)
                qpr = Ring(S, "qpb", 2, [64, 512], BF16)
                blocks = []
                if not self.last:
                    blocks.append(([0], list(CTXF)))
                for b in range(4):
                    blocks.append((list(range(1 + 4 * b, 5 + 4 * b)), list(range(NTF))))
                st = {}

                def pre_head(hd):
                    knT = knTr.next(); Vh = Vhr.next()
                    st['knT'], st['Vh'] = knT, Vh
                    for i, t0 in enumerate(range(0, NT, 512)):
                        T = min(512, NT - t0)
                        ps = psKV.next()
                        for kc in range(4):
                            S.op('pe', lambda e, kc=kc, ps=ps, t0=t0, T=T: e.matmul(ps[:, :T], Wkvb[:, kc, hd * 256:hd * 256 + 128], ckvT[:, kc, t0:t0 + T],
                                                                                 start=(kc == 0), stop=(kc == 3)),
                                 reads=[self.wk(Wkvb, kc), ckvT], writes=[ps])
                        S.op('dve', lambda e, ps=ps, t0=t0, T=T, knT=knT: e.tensor_copy(knT[:, t0:t0 + T], ps[:, :T]), reads=[ps], writes=[knT])
                    for g4 in range(0, NTF, 4):
                        nj = min(4, NTF - g4)
                        ps = psKV.next()
                        for j in range(nj):
                            kt = g4 + j
                            for kc in range(4):
                                S.op('pe', lambda e, kc=kc, ps=ps, j=j, kt=kt: e.matmul(ps[:, j * 128:(j + 1) * 128], ckvT[:, kc, kt * 128:(kt + 1) * 128],
                                                                                      Wkvb[:, kc, hd * 256 + 128:hd * 256 + 256],
                                                                                      start=(kc == 0), stop=(kc == 3)),
                                     reads=[self.wk(Wkvb, kc), ckvT], writes=[ps])
                        S.op('dve', lambda e, ps=ps, g4=g4, nj=nj, Vh=Vh: e.tensor_copy(Vh[:, g4:g4 + nj, :],
                                                                                       ps[:, 0:nj * 128].rearrange("p (j d) -> p j d", j=nj)),
                             reads=[ps], writes=[Vh])

                def load_q(hd, bi, tiles, new_block, new_head):
                    T = 128 * len(tiles)
                    t0 = tiles[0] * 128
                    qn = qnr.next(); qp = qpr.next()
                    S.dma('sp', qn[:, :T], qnT_d[hd, :, t0:t0 + T], reads=['qnT_d'], writes=[qn])
                    S.dma('sp', qp[:, :T], qpT_d[hd, :, t0:t0 + T], reads=['qpT_d'], writes=[qp])
                    return (qn, qp)

                def score(ps, hd, kt, qb, T):
                    qn, qp = qb
                    knT = st['knT']
                    S.op('pe', lambda e: e.matmul(ps[:, :T], knT[:, kt * 128:(kt + 1) * 128], qn[:, :T], start=True, stop=False),
                         reads=[knT, qn], writes=[ps])
                    S.op('pe', lambda e: e.matmul(ps[:, :T], kpeT[:, kt * 128:(kt + 1) * 128], qp[:, :T], start=False, stop=True),
                         reads=[kpeT, qp], writes=[ps])

                def vfn(hd, kt):
                    Vh = st['Vh']
                    return Vh[:, kt, :], [Vh]
                self.attention(blocks, 16, score, vfn, load_q, oT_d, 192 ** -0.5, pre_head=pre_head, head_outer=True)
        return self.wb['wo'], 'wo'

    def prep(self):
        ns = self.S.ns
        self.S.ns = self.li
        if self.kind == 0:
            self.prep_gqa()
        elif self.kind == 1:
            self.prep_lru()
        else:
            self.prep_mla()
        self.prep_mlp()
        self.S.ns = ns

    def build(self, xfull, xloc, xout, keys, on_tile_done=None, before_mlp=None):
        S = self.S
        S.ns = self.li
        self.xf_key, self.xl_key, self.xo_key = keys
        self.on_tile_done = on_tile_done
        self.phase_mod()
        if before_mlp is not None:
            before_mlp()
        oT_d = self.dram("oT", [16, 128, NTL * 128], BF16)
        xmid = self.dram("xmid", [NTL * 128, D], F32)
        if self.kind == 0:
            wo_b, wo_key = self.mixer_gqa(xfull, xloc, oT_d)
        elif self.kind == 1:
            wo_b, wo_key = self.mixer_lru(xfull, xloc, oT_d)
        else:
            wo_b, wo_key = self.mixer_mla(xfull, xloc, oT_d)
        self.phase_out(oT_d, wo_b, wo_key, xloc, xmid)
        self.phase_mlp(xmid, xout)


def make_consts(S):
    c = {}
    c['ident_f'] = S.sb("ident_f", [128, 128], F32)
    c['ident_b'] = S.sb("ident_b", [128, 128], BF16)
    c['ones_f'] = S.sb("ones_f", [128, 128], F32)
    c['ones_b'] = S.sb("ones_b", [128, 128], BF16)
    S.op('pool', lambda e: e.memset(c['ident_f'][:], 1.0), writes=[c['ident_f']])
    S.op('pool', lambda e: e.affine_select(out=c['ident_f'][:], in_=c['ident_f'][:], pattern=[[-1, 128]],
                                           compare_op=ALU.is_equal, fill=0.0, base=0, channel_multiplier=1),
         reads=[c['ident_f']], writes=[c['ident_f']])
    S.op('dve', lambda e: e.tensor_copy(c['ident_b'][:], c['ident_f'][:]), reads=[c['ident_f']], writes=[c['ident_b']])
    S.op('pool', lambda e: e.memset(c['ones_f'][:], 1.0), writes=[c['ones_f']])
    S.op('pool', lambda e: e.memset(c['ones_b'][:], 1.0), writes=[c['ones_b']])
    return c


def layer_input_specs(kind):
    sp = {
        'cT': ([128, 32], F32), 'adabT': ([128, 96], F32), 'ada_w': ([D, 6 * D], F32),
        'ln_g': ([2, D], F32), 'ln_b': ([2, D], F32), 'mlp_w1': ([D, DFF], F32), 'mlp_w2': ([DFF, D], F32),
    }
    if kind == 0:
        sp.update({'wq': ([D, D], F32), 'wk': ([D, 512], F32), 'wv': ([D, 512], F32), 'wo': ([D, D], F32),
                   'q_g': ([1, 128], F32), 'k_g': ([1, 128], F32),
                   'ropeF': ([NTF * 128, 256], F32), 'ropeL': ([NTL * 128, 256], F32)})
    elif kind == 1:
        sp.update({'wx': ([D, D], F32), 'wy': ([D, D], F32), 'wo': ([D, D], F32),
                   'ra_w': ([2, 8, 256, 256], F32), 'ix_w': ([2, 8, 256, 256], F32),
                   'convwT': ([128, 16, 4], F32), 'convbT': ([128, 16], F32), 'rabT': ([128, 16, 2], F32),
                   'ixbT': ([128, 16, 2], F32), 'lamT': ([128, 16, 2], F32), 'hmask': ([128, 2], F32)})
    else:
        sp.update({'wq_a': ([D, 512], F32), 'wq_b': ([512, 3072], F32), 'wkv_a': ([D, 576], F32), 'wkv_b': ([512, 4096], F32),
                   'wo': ([D, D], F32), 'q_a_g': ([1, 512], F32), 'kv_a_g': ([1, 512], F32),
                   'ropeF': ([NTF * 128, 128], F32), 'ropeL': ([NTL * 128, 128], F32)})
    return sp


_DBG = {}
PAIRS = [[0, 1], [2, 3], [4, 5], [6, 7]]
CHUNKS = [(c, [2 * c, 2 * c + 1]) for c in range(8)] + [(8, [16])]


def build_program(layers=(0, 1, 2, 3)):
    nc = bass.Bass("TRN2", target_bir_lowering=False)
    S = Sched(nc)
    S.limit = _DBG.get('limit', 1 << 60)
    consts = make_consts(S)
    x_in = nc.dram_tensor("xfull", [NTF * 128, D], F32, kind="ExternalInput").ap()
    xl_in = nc.dram_tensor("xloc", [NTL * 128, D], F32, kind="ExternalInput").ap()
    final = layers[-1] == DEPTH - 1
    if final:
        out = nc.dram_tensor("out", [2048, D], F32, kind="ExternalOutput").ap()
    else:
        out = nc.dram_tensor("out", [NTL * 128, D], F32, kind="ExternalOutput").ap()
    progs = []
    for li in layers:
        io = {}
        for name, (shape, dt) in layer_input_specs(li % 3).items():
            io[name] = nc.dram_tensor("L%d_%s" % (li, name), shape, dt, kind="ExternalInput").ap()
        progs.append(LayerProg(S, nc, li, li % 3, li == DEPTH - 1, io, consts))
    xfull = lambda t: x_in[t * 128:(t + 1) * 128, :]
    xf_key = lambda t: ('abs', 'xin', t)
    xloc = lambda lt: xl_in[lt * 128:(lt + 1) * 128, :]
    xl_key = lambda lt: ('abs', 'xlin', lt)
    progs[0].prep()
    for i, lp in enumerate(progs):
        li = lp.li
        is_last_in_prog = i == len(progs) - 1
        if is_last_in_prog:
            if final:
                xout = lambda lt: out[(lt - 1) * 128:lt * 128, :]
            else:
                xout = lambda lt: out[lt * 128:(lt + 1) * 128, :]
            on_done = None
            gat = None
        else:
            xres = [nc.dram_tensor("xres%d_%d" % (li, c), [128 * len(tl), D], F32) for c, tl in CHUNKS]
            gat = [nc.dram_tensor("xgat%d_%d" % (li, c), [2 * 128 * len(tl), D], F32) for c, tl in CHUNKS]

            def xout(lt, xres=xres):
                c = min(lt // 2, 8)
                r = (lt - 2 * c) * 128
                return xres[c].ap()[r:r + 128, :]

            def on_done(lt, li=li, xres=xres, gat=gat):
                c, tl = CHUNKS[min(lt // 2, 8)]
                if lt != tl[-1]:
                    return
                if _DBG.get('nocc'):
                    n = 128 * len(tl)
                    S.dma('sp', gat[c].ap()[0:n, :], xres[c].ap()[:, :], reads=[('abs', 'xout', li, t) for t in tl], writes=[('abs', 'xgat', li, c)])
                    S.dma('sp', gat[c].ap()[n:2 * n, :], xres[c].ap()[:, :], reads=[('abs', 'xout', li, t) for t in tl], writes=[('abs', 'xgat', li, c)])
                    return
                S.collective("AllGather", [xres[c].ap().opt()], [gat[c].ap().opt()], PAIRS,
                             reads=[('abs', 'xout', li, t) for t in tl], writes=[('abs', 'xgat', li, c)])
        xo_key = lambda lt, li=li: ('abs', 'xout', li, lt)
        nxt = progs[i + 1] if not is_last_in_prog else None
        early = nxt is not None and not _DBG.get('late_prep')
        lp.build(xfull, xloc, xout, (xf_key, xl_key, xo_key), on_tile_done=on_done,
                 before_mlp=(nxt.prep if early else None))
        if nxt is not None and not early:
            nxt.prep()
        if not is_last_in_prog:
            def xfull(t, gat=gat):
                r, lt = divmod(t, NTL)
                c, tl = CHUNKS[min(lt // 2, 8)]
                row = r * 128 * len(tl) + (lt - tl[0]) * 128
                return gat[c].ap()[row:row + 128, :]
            xf_key = lambda t, li=li: ('abs', 'xgat', li, min((t % NTL) // 2, 8))
            xloc = xout
            xl_key = xo_key
    S.finish()
    return nc, S


def rope_tables(rot_dim):
    t = np.arange(4096)
    row = (t // 64).astype(np.float32)
    col = (t % 64).astype(np.float32)
    m = rot_dim // 2
    inv = (np.float32(10000.0) ** (-(np.arange(m // 2, dtype=np.float32) * np.float32(2.0)) / np.float32(m))).astype(np.float32)
    ar = (row[:, None] * inv[None, :]).astype(np.float32)
    ac = (col[:, None] * inv[None, :]).astype(np.float32)
    cr, sr, cc, sc = np.cos(ar), np.sin(ar), np.cos(ac), np.sin(ac)
    cosF = np.concatenate([cr, cr, cc, cc], 1)
    sinA = np.concatenate([-sr, sr, -sc, sc], 1)
    return np.concatenate([cosF, sinA], 1).astype(np.float32)


def ident_table(n, rot_dim):
    return np.concatenate([np.ones((n, rot_dim), np.float32), np.zeros((n, rot_dim), np.float32)], 1)


def full_order(ctx_b, lat_b):
    return np.concatenate([ctx_b[0:128], lat_b[0:2048], ctx_b[128:256], lat_b[2048:4096]], 0)


def local_order(ctx_b, lat_b, h):
    return np.concatenate([ctx_b[128 * h:128 * h + 128], lat_b[2048 * h:2048 * h + 2048]], 0)


def per_part(v, nchunk):
    return np.ascontiguousarray(v.reshape(nchunk, 128).T)


_PROG_CACHE = {}


def layer_host_inputs(inp, li):
    f = lambda a: np.ascontiguousarray(np.asarray(a, np.float32))
    kind, slot = li % 3, li // 3
    m = {
        'adabT': per_part(f(inp['ada_b'][li]), 96), 'ada_w': f(inp['ada_w'][li]),
        'ln_g': f(inp['ln_g'][li]), 'ln_b': f(inp['ln_b'][li]),
        'mlp_w1': f(inp['mlp_w1'][li]), 'mlp_w2': f(inp['mlp_w2'][li]),
    }
    percore = {}
    if kind == 0:
        tab, idt = rope_tables(128), ident_table(256, 128)
        m.update({'wq': f(inp['gqa_wq'][slot]), 'wk': f(inp['gqa_wk'][slot]), 'wv': f(inp['gqa_wv'][slot]), 'wo': f(inp['gqa_wo'][slot]),
                  'q_g': f(inp['gqa_q_g'][slot]).reshape(1, 128), 'k_g': f(inp['gqa_k_g'][slot]).reshape(1, 128),
                  'ropeF': full_order(idt, tab)})
        percore['ropeL'] = [local_order(idt, tab, h) for h in range(2)]
    elif kind == 1:
        pp2 = lambda v: np.ascontiguousarray(np.stack([per_part(f(v[0]), 16), per_part(f(v[1]), 16)], 2))
        m.update({'wx': f(inp['lru_wx'][slot]), 'wy': f(inp['lru_wy'][slot]), 'wo': f(inp['lru_wo'][slot]),
                  'ra_w': f(inp['lru_ra_w'][slot]), 'ix_w': f(inp['lru_ix_w'][slot]),
                  'convwT': np.ascontiguousarray(np.stack([per_part(f(inp['lru_conv_w'][slot][j]), 16) for j in range(4)], 2)),
                  'convbT': per_part(f(inp['lru_conv_b'][slot]), 16),
                  'rabT': pp2(inp['lru_ra_b'][slot]), 'ixbT': pp2(inp['lru_ix_b'][slot]), 'lamT': pp2(inp['lru_lam'][slot])})
        hm = []
        for h in range(2):
            a = np.zeros((128, 2), np.float32)
            a[:, h] = 1.0
            hm.append(a)
        percore['hmask'] = hm
    else:
        tab, idt = rope_tables(64), ident_table(256, 64)
        m.update({'wq_a': f(inp['mla_wq_a'][slot]), 'wq_b': f(inp['mla_wq_b'][slot]), 'wkv_a': f(inp['mla_wkv_a'][slot]),
                  'wkv_b': f(inp['mla_wkv_b'][slot]), 'wo': f(inp['mla_wo'][slot]),
                  'q_a_g': f(inp['mla_q_a_g'][slot]).reshape(1, 512), 'kv_a_g': f(inp['mla_kv_a_g'][slot]).reshape(1, 512),
                  'ropeF': full_order(idt, tab)})
        percore['ropeL'] = [local_order(idt, tab, h) for h in range(2)]
    return m, percore


def kernel(**inp):
    x = np.asarray(inp['x'], np.float32)
    ctx = np.asarray(inp['ctx'], np.float32)
    layers = tuple(_DBG.get('layers', range(DEPTH)))
    lat = [x[b] for b in range(4)]
    cx = [ctx[b] for b in range(4)]
    if 'init' in _DBG:
        lat, cx = [np.asarray(a) for a in _DBG['init'][0]], [np.asarray(a) for a in _DBG['init'][1]]
    if layers not in _PROG_CACHE:
        _PROG_CACHE[layers] = build_program(layers)[0]
    nc = _PROG_CACHE[layers]
    common = {}
    percore = {}
    for li in layers:
        m, pc = layer_host_inputs(inp, li)
        for k, v in m.items():
            common["L%d_%s" % (li, k)] = v
        for k, v in pc.items():
            percore["L%d_%s" % (li, k)] = v
    in_maps = []
    for core in range(NCORES):
        b, h = core // 2, core % 2
        m = dict(common)
        for k, v in percore.items():
            m[k] = v[h]
        cpair = np.stack([np.asarray(inp['c'][b], np.float32), np.asarray(inp['c_ctx'], np.float32)], 1)
        cT = np.ascontiguousarray(cpair.reshape(16, 128, 2).transpose(1, 0, 2).reshape(128, 32))
        for li in layers:
            m["L%d_cT" % li] = cT
        m['xfull'] = full_order(cx[b], lat[b])
        m['xloc'] = local_order(cx[b], lat[b], h)
        in_maps.append(m)
    ncr = _DBG.get('ncores', NCORES)
    res = run_bass_kernel_spmd(nc, in_maps[:ncr], core_ids=list(range(ncr)))
    final = layers[-1] == DEPTH - 1
    for b in range(ncr // 2):
        o0 = res.results[2 * b]['out']
        o1 = res.results[2 * b + 1]['out']
        if final:
            lat[b] = np.concatenate([o0, o1], 0)
        else:
            lat[b] = np.concatenate([o0[128:], o1[128:]], 0)
            cx[b] = np.concatenate([o0[:128], o1[:128]], 0)
    _DBG['lat'], _DBG['cx'] = lat, cx
    return np.stack(lat, 0).astype(np.float32)
```
